# Optimizing a Trainium2 kernel written in Bass

```python
import math
import jax, jax.numpy as jnp
from jax import lax
import numpy as np

D_MODEL = 1024
BATCH = 4
SEQ = 4096
DEPTH = 2
DEC_BATCH = 128
DEC_SEQ = 8
PAST_LEN = 2048
PAGE_SIZE = 128

FOX_HEADS = 8
FOX_HEAD_DIM = 64
FOX_WIDTH = FOX_HEADS * FOX_HEAD_DIM
FOX_BLOCK = 128
GDN_HEADS = 4
GDN_HEAD_DIM = 128
GDN_WIDTH = GDN_HEADS * GDN_HEAD_DIM
GDN_CONV_DIM = 3 * GDN_WIDTH
CONV_WIDTH = 4
GDN_CHUNK = 64
MIX_WIDTH = FOX_WIDTH + GDN_WIDTH
N_MEM = 256
MEM_HEADS = 4
MEM_HEAD_DIM = 128
MEM_WIDTH = MEM_HEADS * MEM_HEAD_DIM
D_FF = -(-8 * D_MODEL // (3 * 256)) * 256
EPS = 1e-6
IN_SPLITS = (FOX_WIDTH, FOX_WIDTH, FOX_WIDTH, FOX_HEADS, GDN_CONV_DIM, GDN_HEADS, GDN_HEADS, GDN_WIDTH)
IN_COLS = sum(IN_SPLITS)

kernel_name = 'fox_gdn_parallel_heads_step'


def rmsnorm(x, g):
    xf = x.astype(jnp.float32)
    xf = xf * lax.rsqrt(jnp.mean(xf * xf, axis=-1, keepdims=True) + EPS)
    return xf.astype(x.dtype) * g


def l2norm(x):
    return x * lax.rsqrt(jnp.sum(x * x, axis=-1, keepdims=True) + EPS)


def project_mix(x, g_norm, w_in, b_f):
    b, l, _ = x.shape
    n = rmsnorm(x, g_norm)
    offs = np.cumsum(IN_SPLITS)[:-1].tolist()
    fq, fk, fv, ff, qkv, ga, gb, gg = jnp.split(n @ w_in, offs, axis=-1)
    heads = lambda t: t.reshape(b, l, FOX_HEADS, FOX_HEAD_DIM)
    logf = jax.nn.log_sigmoid((ff + b_f).astype(jnp.float32))
    return heads(fq), heads(fk), heads(fv), logf, qkv, ga, gb, gg


def fox_attend(q, k, v, c_q, c_k, pos_q, pos_k):
    s = jnp.einsum('bqhd,bkhd->bhqk', q, k).astype(jnp.float32) * (FOX_HEAD_DIM ** -0.5)
    s = s + jnp.swapaxes(c_q, 1, 2)[..., :, None] - jnp.swapaxes(c_k, 1, 2)[..., None, :]
    s = jnp.where(pos_k[None, :] <= pos_q[:, None], s, -jnp.inf)
    p = jax.nn.softmax(s, axis=-1).astype(v.dtype)
    return jnp.einsum('bhqk,bkhd->bqhd', p, v)


def fox_prompt(q, k, v, c, pos):
    b, l, h, d = q.shape
    nb = l // FOX_BLOCK
    qb = q.reshape(b, nb, FOX_BLOCK, h, d).transpose(1, 0, 2, 3, 4)
    cb = c.reshape(b, nb, FOX_BLOCK, h).transpose(1, 0, 2, 3)
    pb = pos.reshape(nb, FOX_BLOCK)
    out = lax.map(lambda a: fox_attend(a[0], k, v, a[1], c, a[2], pos), (qb, cb, pb))
    return out.transpose(1, 0, 2, 3, 4).reshape(b, l, h * d)


def gated_delta_chunked(q, k, v, g, beta, s0):
    b, l, h, dk = q.shape
    dv = v.shape[-1]
    cs = math.gcd(l, GDN_CHUNK)
    n = l // cs
    chunks = lambda t: jnp.moveaxis(t.reshape(b, n, cs, h, *t.shape[3:]), 3, 1)
    q, k, v, g, beta = chunks(q), chunks(k), chunks(v), chunks(g), chunks(beta)
    gc = jnp.cumsum(g, axis=-1)
    incl = jnp.tril(jnp.ones((cs, cs), bool))
    strict = jnp.tril(jnp.ones((cs, cs), bool), -1)
    decay = jnp.exp(jnp.where(incl, gc[..., :, None] - gc[..., None, :], -jnp.inf))
    kk = jnp.einsum('bhncd,bhnsd->bhncs', k, k)
    a_mat = jnp.where(strict, kk * decay * beta[..., :, None], 0.0)
    rhs = jnp.concatenate([v * beta[..., None], k * (beta * jnp.exp(gc))[..., None]], axis=-1)
    sol = lax.linalg.triangular_solve(a_mat, rhs, left_side=True, lower=True, unit_diagonal=True)
    u_base, k_cum = sol[..., :dv], sol[..., dv:]
    qk = jnp.einsum('bhncd,bhnsd->bhncs', q, k) * decay
    q_dec = q * jnp.exp(gc)[..., None]
    k_dec = k * jnp.exp(gc[..., -1:] - gc)[..., None]
    g_tot = jnp.exp(gc[..., -1])
    xs = tuple(jnp.moveaxis(t, 2, 0) for t in (u_base, k_cum, qk, q_dec, k_dec, g_tot))

    def step(s, inp):
        u_n, kc_n, qk_n, qd_n, kd_n, gt_n = inp
        u = u_n - jnp.einsum('bhcd,bhde->bhce', kc_n, s)
        o = jnp.einsum('bhcd,bhde->bhce', qd_n, s) + jnp.einsum('bhcs,bhse->bhce', qk_n, u)
        s = s * gt_n[..., None, None] + jnp.einsum('bhcd,bhce->bhde', kd_n, u)
        return s, o

    s_fin, o = lax.scan(step, s0, xs)
    o = jnp.transpose(o, (1, 0, 3, 2, 4)).reshape(b, l, h, dv)
    return o, s_fin


def gdn_mixer(qkv, ga, gb, gg, conv_buf, s0, conv_w, a_log, dt_bias, norm_w):
    b, l, _ = qkv.shape
    xp = jnp.concatenate([conv_buf.astype(qkv.dtype), qkv], axis=1)
    conv = sum(xp[:, w:w + l] * conv_w[w] for w in range(CONV_WIDTH))
    new_buf = xp[:, l:]
    act = jax.nn.silu(conv).astype(jnp.float32)
    q, k, v = [t.reshape(b, l, GDN_HEADS, GDN_HEAD_DIM) for t in jnp.split(act, 3, axis=-1)]
    q = l2norm(q) * (GDN_HEAD_DIM ** -0.5)
    k = l2norm(k)
    beta = jax.nn.sigmoid(gb.astype(jnp.float32))
    g = -jnp.exp(a_log.astype(jnp.float32)) * jax.nn.softplus(ga.astype(jnp.float32) + dt_bias.astype(jnp.float32))
    o, s_new = gated_delta_chunked(q, k, v, g, beta, s0.astype(jnp.float32))
    gate = jax.nn.silu(gg.reshape(b, l, GDN_HEADS, GDN_HEAD_DIM).astype(jnp.float32))
    o = rmsnorm(o, norm_w) * gate
    return o.reshape(b, l, GDN_WIDTH).astype(qkv.dtype), s_new.astype(qkv.dtype), new_buf


def mem_kv(mem, g_norm, w_kv):
    b, m, _ = mem.shape
    mk, mv = jnp.split(rmsnorm(mem, g_norm) @ w_kv, 2, axis=-1)
    return mk.reshape(b, m, MEM_HEADS, MEM_HEAD_DIM), mv.reshape(b, m, MEM_HEADS, MEM_HEAD_DIM)


def mem_attend(x, g_norm, w_q, w_o, mk, mv):
    b, l, _ = x.shape
    q = (rmsnorm(x, g_norm) @ w_q).reshape(b, l, MEM_HEADS, MEM_HEAD_DIM)
    s = jnp.einsum('bqhd,bkhd->bhqk', q, mk).astype(jnp.float32) * (MEM_HEAD_DIM ** -0.5)
    p = jax.nn.softmax(s, axis=-1).astype(mv.dtype)
    o = jnp.einsum('bhqk,bkhd->bqhd', p, mv).reshape(b, l, MEM_WIDTH)
    return o @ w_o


def swiglu(x, g_norm, w_in, w_out):
    a, u = jnp.split(rmsnorm(x, g_norm) @ w_in, 2, axis=-1)
    return (jax.nn.silu(a) * u) @ w_out


def setup_inputs(seed: int = 0) -> dict:
    key = jax.random.key(seed)
    ks = iter(jax.random.split(key, 40))
    f32 = jnp.float32
    nrm = lambda shape, scale: jax.random.normal(next(ks), shape, f32) * scale
    gain = lambda shape: 1.0 + nrm(shape, 0.01)
    n_pages = PAST_LEN // PAGE_SIZE
    n_used = DEC_BATCH * n_pages
    n_phys = n_used + n_used // 4
    page_table = jax.random.permutation(next(ks), n_phys)[:n_used].reshape(DEC_BATCH, n_pages).astype(jnp.int32)
    a_log = jnp.log(jax.random.uniform(next(ks), (DEPTH, GDN_HEADS), f32, 1.0, 16.0))
    dt = jnp.exp(jax.random.uniform(next(ks), (DEPTH, GDN_HEADS), f32, math.log(1e-3), math.log(1e-1)))
    dt_bias = dt + jnp.log(-jnp.expm1(-dt))
    return {
        'x_prompt': nrm((BATCH, SEQ, D_MODEL), 1.0),
        'x_sample': nrm((DEC_BATCH, DEC_SEQ, D_MODEL), 1.0),
        'cache_fox_k': nrm((DEPTH, n_phys, PAGE_SIZE, FOX_HEADS, FOX_HEAD_DIM), 1.0),
        'cache_fox_v': nrm((DEPTH, n_phys, PAGE_SIZE, FOX_HEADS, FOX_HEAD_DIM), 1.0),
        'cache_fox_logf': jax.nn.log_sigmoid(3.0 + nrm((DEPTH, n_phys, PAGE_SIZE, FOX_HEADS), 1.0)),
        'state_gdn': nrm((DEPTH, DEC_BATCH, GDN_HEADS, GDN_HEAD_DIM, GDN_HEAD_DIM), 0.05),
        'state_gdn_conv': nrm((DEPTH, DEC_BATCH, CONV_WIDTH - 1, GDN_CONV_DIM), 1.0),
        'cache_mem_k': nrm((DEPTH, DEC_BATCH, N_MEM, MEM_HEADS, MEM_HEAD_DIM), 1.0),
        'cache_mem_v': nrm((DEPTH, DEC_BATCH, N_MEM, MEM_HEADS, MEM_HEAD_DIM), 1.0),
        'page_table': page_table,
        'mem_prompt': nrm((BATCH, N_MEM, D_MODEL), 1.0),
        'g_norm_mix': gain((DEPTH, D_MODEL)),
        'w_in': nrm((DEPTH, D_MODEL, IN_COLS), D_MODEL ** -0.5),
        'b_fox_f': 3.0 + nrm((DEPTH, FOX_HEADS), 0.5),
        'gdn_conv_w': nrm((DEPTH, CONV_WIDTH, GDN_CONV_DIM), CONV_WIDTH ** -0.5),
        'gdn_a_log': a_log,
        'gdn_dt_bias': dt_bias,
        'gdn_norm_w': gain((DEPTH, GDN_HEAD_DIM)),
        'w_out': nrm((DEPTH, MIX_WIDTH, D_MODEL), MIX_WIDTH ** -0.5),
        'g_norm_memin': gain((DEPTH, D_MODEL)),
        'w_mem_kv': nrm((DEPTH, D_MODEL, 2 * MEM_WIDTH), D_MODEL ** -0.5),
        'g_norm_mem': gain((DEPTH, D_MODEL)),
        'w_mem_q': nrm((DEPTH, D_MODEL, MEM_WIDTH), D_MODEL ** -0.5),
        'w_mem_o': nrm((DEPTH, MEM_WIDTH, D_MODEL), MEM_WIDTH ** -0.5),
        'g_norm_ffn': gain((DEPTH, D_MODEL)),
        'w_ffn_in': nrm((DEPTH, D_MODEL, 2 * D_FF), D_MODEL ** -0.5),
        'w_ffn_out': nrm((DEPTH, D_FF, D_MODEL), D_FF ** -0.5),
        'g_final': gain((D_MODEL,)),
    }


def reference(x_prompt, x_sample, cache_fox_k, cache_fox_v, cache_fox_logf, state_gdn, state_gdn_conv,
              cache_mem_k, cache_mem_v, page_table, mem_prompt, g_norm_mix, w_in, b_fox_f, gdn_conv_w,
              gdn_a_log, gdn_dt_bias, gdn_norm_w, w_out, g_norm_memin, w_mem_kv, g_norm_mem, w_mem_q,
              w_mem_o, g_norm_ffn, w_ffn_in, w_ffn_out, g_final):
    bp, lp, _ = x_prompt.shape
    bs, ls, _ = x_sample.shape
    n_pages = page_table.shape[1]
    past_len = n_pages * cache_fox_k.shape[2]
    pos_prompt = jnp.arange(lp)
    pos_q_sample = past_len + jnp.arange(ls)
    pos_k_sample = jnp.arange(past_len + ls)
    yp, ys = x_prompt, x_sample
    fkp, fvp, flp, gsp, gcp, mkp, mvp = [], [], [], [], [], [], []
    fks, fvs, fls, gss, gcs = [], [], [], [], []
    for l in range(DEPTH):
        gdn_p = (gdn_conv_w[l], gdn_a_log[l], gdn_dt_bias[l], gdn_norm_w[l])
        fq, fk, fv, logf, qkv, ga, gb, gg = project_mix(yp, g_norm_mix[l], w_in[l], b_fox_f[l])
        fo = fox_prompt(fq, fk, fv, jnp.cumsum(logf, axis=1), pos_prompt)
        go, s_new, buf_new = gdn_mixer(qkv, ga, gb, gg,
                                       jnp.zeros((bp, CONV_WIDTH - 1, GDN_CONV_DIM), qkv.dtype),
                                       jnp.zeros((bp, GDN_HEADS, GDN_HEAD_DIM, GDN_HEAD_DIM), jnp.float32), *gdn_p)
        yp = yp + jnp.concatenate([fo, go], axis=-1) @ w_out[l]
        mk, mv = mem_kv(mem_prompt, g_norm_memin[l], w_mem_kv[l])
        yp = yp + mem_attend(yp, g_norm_mem[l], w_mem_q[l], w_mem_o[l], mk, mv)
        yp = yp + swiglu(yp, g_norm_ffn[l], w_ffn_in[l], w_ffn_out[l])
        fkp.append(fk); fvp.append(fv); flp.append(logf.astype(fk.dtype))
        gsp.append(s_new); gcp.append(buf_new); mkp.append(mk); mvp.append(mv)
        fq, fk, fv, logf, qkv, ga, gb, gg = project_mix(ys, g_norm_mix[l], w_in[l], b_fox_f[l])
        k_all = jnp.concatenate([cache_fox_k[l][page_table].reshape(bs, past_len, FOX_HEADS, FOX_HEAD_DIM), fk], axis=1)
        v_all = jnp.concatenate([cache_fox_v[l][page_table].reshape(bs, past_len, FOX_HEADS, FOX_HEAD_DIM), fv], axis=1)
        lf_all = jnp.concatenate([cache_fox_logf[l][page_table].reshape(bs, past_len, FOX_HEADS).astype(jnp.float32), logf], axis=1)
        c_all = jnp.cumsum(lf_all, axis=1)
        fo = fox_attend(fq, k_all, v_all, c_all[:, past_len:], c_all, pos_q_sample, pos_k_sample).reshape(bs, ls, FOX_WIDTH)
        go, s_new, buf_new = gdn_mixer(qkv, ga, gb, gg, state_gdn_conv[l], state_gdn[l], *gdn_p)
        ys = ys + jnp.concatenate([fo, go], axis=-1) @ w_out[l]
        ys = ys + mem_attend(ys, g_norm_mem[l], w_mem_q[l], w_mem_o[l], cache_mem_k[l], cache_mem_v[l])
        ys = ys + swiglu(ys, g_norm_ffn[l], w_ffn_in[l], w_ffn_out[l])
        fks.append(fk); fvs.append(fv); fls.append(logf.astype(fk.dtype))
        gss.append(s_new); gcs.append(buf_new)
    yp = rmsnorm(yp, g_final)
    ys = rmsnorm(ys, g_final)
    return (yp, ys,
            jnp.stack(fkp), jnp.stack(fvp), jnp.stack(flp), jnp.stack(gsp), jnp.stack(gcp), jnp.stack(mkp), jnp.stack(mvp),
            jnp.stack(fks), jnp.stack(fvs), jnp.stack(fls), jnp.stack(gss), jnp.stack(gcs))
```

```python
import contextlib
import math
import numpy as np
import concourse.bass as bass
import concourse.mybir as mybir
from concourse.bass import IndirectOffsetOnAxis
from concourse.bass_utils import run_bass_kernel_spmd

F32 = mybir.dt.float32; BF16 = mybir.dt.bfloat16; I32 = mybir.dt.int32
AF = mybir.ActivationFunctionType; ALU = mybir.AluOpType; AX = mybir.AxisListType

D = 1024; DEPTH = 2
FH = 8; FD = 64; FW = 512
GH = 4; GD = 128; GW = 512; GC3 = 1536
NMEM = 256; MH = 4; MD = 128; MW = 512
DFF = 2816
INC = 3600
EPS = 1e-6
NEG = -30000.0
DEBUG = False
NLAYERS = DEPTH
STOP_AFTER = None


class Prog:
    NDMA = 12

    def __init__(self, nc, es):
        self.nc = nc
        self.engs = {"pe": nc.tensor, "act": nc.scalar, "dve": nc.vector, "pool": nc.gpsimd, "sp": nc.sync}
        self.sem = {k: es.enter_context(nc.semaphore("c_" + k)) for k in ("pe", "act", "dve", "pool")}
        self.cnt = {k: 0 for k in self.sem}
        self.dsem = {q: [es.enter_context(nc.semaphore(f"d_{q}{i}")) for i in range(self.NDMA)] for q in ("sp", "pool", "act")}
        self.dval = {q: [0] * self.NDMA for q in self.dsem}
        self.dnext = {q: 0 for q in self.dsem}
        self.known = {k: {} for k in self.engs}
        self.lastw = {}
        self.readers = {}
        self.n = 0

    def _wait(self, eng, tok):
        if tok is None:
            return
        sem, val, src = tok
        if src == "pe" and eng == "pe":
            return
        kn = self.known[eng]
        if kn.get(sem.name, 0) >= val:
            return
        self.engs[eng].wait_ge(sem, val)
        kn[sem.name] = val

    def emit(self, eng, fn, reads=(), writes=(), dma=False):
        for r in reads:
            self._wait(eng, self.lastw.get(r))
        for w in writes:
            self._wait(eng, self.lastw.get(w))
            for t in self.readers.get(w, ()):
                self._wait(eng, t)
        if dma:
            i = self.dnext[eng]; self.dnext[eng] = (i + 1) % self.NDMA
            sem = self.dsem[eng][i]
            prev = self.dval[eng][i]
            if prev:
                self._wait(eng, (sem, prev, "dma"))
            ins = fn(self.engs[eng])
            val = prev + 16
            self.dval[eng][i] = val
            ins.then_inc(sem, 16)
            tok = (sem, val, "dma")
        else:
            ins = fn(self.engs[eng])
            self.cnt[eng] += 1
            ins.then_inc(self.sem[eng], 1)
            tok = (self.sem[eng], self.cnt[eng], eng)
        for w in writes:
            self.lastw[w] = tok; self.readers[w] = []
        for r in reads:
            self.readers.setdefault(r, []).append(tok)
        self.n += 1
        return tok

    def barrier(self):
        toks = [(self.sem[k], self.cnt[k], k) for k in self.sem if self.cnt[k]]
        for q in self.dsem:
            for i, s in enumerate(self.dsem[q]):
                if self.dval[q][i]:
                    toks.append((s, self.dval[q][i], "dma"))
        for eng in self.engs:
            for t in toks:
                if not (t[2] == eng):
                    self._wait(eng, t)

    def finish(self):
        for q in self.dsem:
            for i, sem in enumerate(self.dsem[q]):
                v = self.dval[q][i]
                if v:
                    self._wait("sp", (sem, v, "dma"))


def _consts(cs):
    t = np.arange(128)
    same = (t[:, None] // cs) == (t[None, :] // cs)
    up_incl = same & (t[:, None] <= t[None, :])
    c = {}
    c["negmt"] = np.where(up_incl, 0.0, NEG).astype(np.float32)
    c["posm"] = np.where(up_incl.T, 0.0, -NEG).astype(np.float32)
    c["su"] = (same & (t[:, None] < t[None, :])).astype(np.float32)
    c["sl"] = c["su"].T.copy()
    c["ub"] = up_incl.astype(np.float32)
    last = (t // cs) * cs + cs - 1
    c["bl"] = (t[:, None] == last[None, :]).astype(np.float32)
    nb = 128 // cs
    el = np.zeros((128, nb, 128), np.float32)
    for b in range(nb):
        el[b * cs + cs - 1, b, :] = 1.0
    c["elast"] = el
    cm = np.zeros((128, nb, 128), np.float32)
    rm = np.zeros((128, nb), np.float32)
    for b in range(nb):
        cm[:, b, b * cs:(b + 1) * cs] = 1.0
        rm[b * cs:(b + 1) * cs, b] = 1.0
    c["colmask"] = cm
    c["rowmask"] = rm
    sh = np.zeros((128, 3, 128), np.float32)
    for s in (1, 2, 3):
        for tt in range(128):
            if tt - s >= 0 and (tt - s) // cs == tt // cs:
                sh[tt - s, s - 1, tt] = 1.0
    c["shc"] = sh
    return c


def build_program(S, NPG, NPHYS, cs_p, nlayers=DEPTH, stop_after=None):
    NT = S // 128
    nc = bass.Bass("TRN2", target_bir_lowering=False)
    es = contextlib.ExitStack()

    def din(name, shape, dt=F32):
        return nc.dram_tensor(name, list(shape), dt, kind="ExternalInput").ap()

    def dout(name, shape, dt=F32):
        return nc.dram_tensor(name, list(shape), dt, kind="ExternalOutput").ap()

    def dscr(name, shape, dt=F32):
        return nc.dram_tensor(name, list(shape), dt, kind="Internal").ap()

    xp = din("xp", [S, D]); xs = din("xs", [128, D])
    ck = din("ck", [DEPTH, NPHYS * 128, FW]); cv = din("cv", [DEPTH, NPHYS * 128, FW])
    clf = din("clf", [DEPTH, NPHYS * 128, FH])
    sgd = din("sgd", [DEPTH, 16, GH, GD, GD]); scv = din("scv", [DEPTH, 16, 3, GC3])
    cmk = din("cmk", [DEPTH, 16, NMEM, MW]); cmv = din("cmv", [DEPTH, 16, NMEM, MW])
    ptab = din("ptab", [16, NPG], I32)
    memp = din("memp", [NMEM, D])
    g_mix = din("g_mix", [DEPTH, D]); w_in = din("w_in", [DEPTH, D, INC]); b_f = din("b_f", [DEPTH, FH])
    conv_w = din("conv_w", [DEPTH, 4, GC3]); a_log = din("a_log", [DEPTH, GH]); dt_b = din("dt_b", [DEPTH, GH])
    gnw = din("gnw", [DEPTH, GD]); w_out = din("w_out", [DEPTH, D, D])
    g_memin = din("g_memin", [DEPTH, D]); w_mkv = din("w_mkv", [DEPTH, D, 2 * MW])
    g_mem = din("g_mem", [DEPTH, D]); w_mq = din("w_mq", [DEPTH, D, MW]); w_mo = din("w_mo", [DEPTH, MW, D])
    g_ffn = din("g_ffn", [DEPTH, D]); w_fi = din("w_fi", [DEPTH, D, 2 * DFF]); w_fo = din("w_fo", [DEPTH, DFF, D])
    g_fin = din("g_fin", [D])
    k_ident = din("k_ident", [128, 128])
    k_p = {k: din("kp_" + k, v.shape) for k, v in _consts(cs_p).items()}
    k_s = {k: din("ks_" + k, v.shape) for k, v in _consts(8).items()}
    k_negq = din("k_negq", [128, 4, 512])
    k_iota = din("k_iota", [128, 1])
    k_ub128 = din("k_ub128", [128, 128]); k_ls128 = din("k_ls128", [128, 128])
    k_shc = din("k_shc", [128, 3, 128]); k_shp = din("k_shp", [128, 3, 128]); k_shs = din("k_shs", [48, 3, 128])

    yp = dout("yp", [S, D]); ys = dout("ys", [128, D])
    o_fkp = dout("o_fkp", [DEPTH, S, FW]); o_fvp = dout("o_fvp", [DEPTH, S, FW]); o_flp = dout("o_flp", [DEPTH, S, FH])
    o_gsp = dout("o_gsp", [DEPTH, GH, GD, GD]); o_gcp = dout("o_gcp", [DEPTH, 3, GC3])
    o_mkp = dout("o_mkp", [DEPTH, NMEM, MW]); o_mvp = dout("o_mvp", [DEPTH, NMEM, MW])
    o_fks = dout("o_fks", [DEPTH, 128, FW]); o_fvs = dout("o_fvs", [DEPTH, 128, FW]); o_fls = dout("o_fls", [DEPTH, 128, FH])
    o_gss = dout("o_gss", [DEPTH, 16, GH, GD, GD]); o_gcs = dout("o_gcs", [DEPTH, 16, 3, GC3])

    if DEBUG:
        dbg_fos = dout("dbg_fos", [128, 4, 128], BF16); dbg_gos = dout("dbg_gos", [128, 4, 128], BF16); dbg_xs1 = dout("dbg_xs1", [128, D])
    h_a = dscr("h_a", [S, D]); h_b = dscr("h_b", [S, D])
    FOd = dscr("FOd", [128, 4, S + 128], BF16); GOd = dscr("GOd", [128, 4, S + 128], BF16)

    with es:
        P = Prog(nc, es)
        uid = [0]

        def mk_sb(stack):
            def f(name, shape, dt=F32):
                uid[0] += 1
                return stack.enter_context(nc.sbuf_tensor(f"{name}_{uid[0]}", list(shape), dt))
            return f

        sb = mk_sb(es)

        def MM(out, lhsT, rhs, start, stop, r, w):
            P.emit("pe", lambda e: e.matmul(out, lhsT, rhs, start=start, stop=stop), reads=r, writes=w)

        def TR(out, in_, idn, r, w):
            P.emit("pe", lambda e: e.transpose(out, in_, idn), reads=r, writes=w)

        def ACTF(out, in_, func, r, w, **kw):
            P.emit("act", lambda e: e.activation(out=out, in_=in_, func=func, **kw), reads=r, writes=w)

        def CP(eng, out, in_, r, w):
            if eng == "act":
                P.emit(eng, lambda e: e.copy(out=out, in_=in_), reads=r, writes=w)
            else:
                P.emit(eng, lambda e: e.tensor_copy(out=out, in_=in_), reads=r, writes=w)

        def TT(eng, out, in0, in1, op, r, w):
            P.emit(eng, lambda e: e.tensor_tensor(out=out, in0=in0, in1=in1, op=op), reads=r, writes=w)

        def TS(eng, out, in0, s1, s2, op0, op1, r, w):
            if s2 is None:
                P.emit(eng, lambda e: e.tensor_scalar(out=out, in0=in0, scalar1=s1, scalar2=None, op0=op0), reads=r, writes=w)
            else:
                P.emit(eng, lambda e: e.tensor_scalar(out=out, in0=in0, scalar1=s1, scalar2=s2, op0=op0, op1=op1), reads=r, writes=w)

        def STT(out, in0, scalar, in1, op0, op1, r, w):
            P.emit("dve", lambda e: e.scalar_tensor_tensor(out=out, in0=in0, scalar=scalar, in1=in1, op0=op0, op1=op1), reads=r, writes=w)

        def MSET(eng, ap, val, w):
            P.emit(eng, lambda e: e.memset(ap, val), writes=w)

        def DMA(q, out, in_, r, w):
            P.emit(q, lambda e: e.dma_start(out=out, in_=in_), reads=r, writes=w, dma=True)

        def GATHER(out, table, idx_ap, r, w):
            P.emit("pool", lambda e: e.indirect_dma_start(out=out, out_offset=None, in_=table,
                                                          in_offset=IndirectOffsetOnAxis(ap=idx_ap, axis=0)), reads=r, writes=w, dma=True)

        def RECIP(out, in_, r, w):
            P.emit("dve", lambda e: e.reciprocal(out=out, in_=in_), reads=r, writes=w)

        def REDUCE(out, in_, r, w):
            P.emit("dve", lambda e: e.tensor_reduce(out=out, in_=in_, axis=AX.X, op=ALU.add), reads=r, writes=w)

        ident = sb("ident", [128, 128]); DMA("sp", ident[:], k_ident[:, :], [], ["ident"])
        identb = sb("identb", [128, 128], BF16); CP("dve", identb[:], ident[:], ["ident"], ["identb"])
        ones = sb("ones", [128, 128]); MSET("dve", ones[:], 1.0, ["ones"])
        onesb = sb("onesb", [128, 128], BF16); MSET("dve", onesb[:], 1.0, ["onesb"])
        epsT = sb("epsT", [128, 1]); MSET("dve", epsT[:], EPS, ["epsT"])
        gbc = sb("gbc", [128, 4, D])
        xt = [sb(f"xt{i}", [128, D]) for i in range(2)]
        nb16 = sb("nb16", [128, D], BF16)
        nT = sb("nT", [128, 8, 128], BF16)
        rs = sb("rs", [128, 8])
        XS = sb("XS", [128, D])
        stg = [sb(f"stg{i}", [128, 1032]) for i in range(2)]
        ps = [es.enter_context(nc.psum_tensor(f"ps{i}", [128, 512], F32)) for i in range(8)]
        psb = ps[7].bitcast(BF16)
        DMA("sp", XS[:], xs[:, :], [], ["XS"])

        def psr(i):
            return f"ps{i}"

        def load_gains(l):
            for j, src in enumerate((g_mix[l], g_mem[l], g_ffn[l], g_memin[l])):
                DMA("sp", gbc[:, j, :], src.partition_broadcast(128), [], [f"gbc{j}"])

        def load_w(dst, dst_key, src_ap, ncols, col0=0):
            K = src_ap.shape[0]
            for kc in range(K // 128):
                P.emit("pool", lambda e, kc=kc: e.dma_start(out=dst[:, kc, col0:col0 + ncols], in_=src_ap[kc * 128:(kc + 1) * 128, :]),
                       writes=[dst_key], dma=True)

        def rmsnorm_tile(x_ap, xkey, gidx, dstT=None, dkey="nT", toff=0):
            dstT = nT if dstT is None else dstT
            ACTF(stg[0][:, 0:D], x_ap, AF.Square, [xkey], ["stg0", "rs"], accum_out=rs[:, 0:1])
            ACTF(rs[:, 1:2], rs[:, 0:1], AF.Sqrt, ["rs", "epsT"], ["rs"], bias=epsT[:, 0:1], scale=1.0 / D)
            RECIP(rs[:, 2:3], rs[:, 1:2], ["rs"], ["rs"])
            STT(nb16[:], x_ap, rs[:, 2:3], gbc[:, gidx, :], ALU.mult, ALU.mult, [xkey, "rs", f"gbc{gidx}"], ["nb16"])
            for kc in range(8):
                TR(psb[:, kc * 128:(kc + 1) * 128], nb16[:, kc * 128:(kc + 1) * 128], identb[:], ["nb16", "identb"], [psr(7)])
            CP("dve", dstT[:, :, toff:toff + 128], psb[:, 0:1024].rearrange("p (k t) -> p k t", k=8), [psr(7)], [dkey])

        def load_x(l, i, src_h):
            if i == NT:
                return XS[:], "XS"
            xb_ = xt[i % 2]; xkey = f"xt{i % 2}"
            DMA("sp", xb_[:], src_h[i * 128:(i + 1) * 128, :], [], [xkey])
            return xb_[:], xkey

        def sweep_fox(l, src_h):
            with contextlib.ExitStack() as ses:
                sb2 = mk_sb(ses)
                P.barrier()
                WA = sb2("WAf", [128, 8, 1544], BF16)
                KT = sb2("KT", [128, 4, S], BF16); VR = sb2("VR", [128, NT, FH, 65], BF16); rrow = sb2("rrow", [65, 512])
                QT = sb2("QT", [128, 4, 512], BF16); FOg = sb2("FOg", [128, 4, 512], BF16)
                LOGF = sb2("LOGF", [128, 8]); CC = sb2("CC", [128, NT + 1, FH]); TOT = sb2("TOT", [128, NT + 2, FH])
                BIAS = sb2("BIAS", [128, NT, FH]); bfb = sb2("bfb", [128, FH])
                PT = [sb2(f"PT{i}", [128, 512], BF16) for i in range(3)]
                rl = sb2("rl", [64, 512])
                negq = sb2("negq", [128, 4, 512], BF16)
                ub128 = sb2("ub128", [128, 128]); ls128 = sb2("ls128", [128, 128]); ub8 = sb2("ub8", [128, 128])
                negm8 = sb2("negm8", [128, 128], BF16); iota_p = sb2("iota_p", [128, 1])
                P.emit("pool", lambda e: e.dma_start(out=negq[:], in_=k_negq[:, :, :]), writes=["negq"], dma=True)
                P.emit("pool", lambda e: e.dma_start(out=negm8[:], in_=k_s["negmt"]), writes=["negm8"], dma=True)
                DMA("sp", ub128[:], k_ub128[:, :], [], ["ub128"]); DMA("sp", ls128[:], k_ls128[:, :], [], ["ls128"])
                DMA("sp", ub8[:], k_s["ub"], [], ["ub8"]); DMA("sp", iota_p[:], k_iota[:, :], [], ["iota_p"])
                DMA("sp", bfb[:], b_f[l].partition_broadcast(128), [], ["bfb"])
                load_w(WA, "WA", w_in[l, :, 0:1544], 1544)
                MSET("dve", TOT[:, 0, :], 0.0, ["TOT"])
                MSET("pool", VR[:], 1.0, ["VR"])
                QTs = sb2("QTs", [128, 4, 128], BF16); KTs = sb2("KTs", [128, 4, 128], BF16)
                VAs = sb2("VAs", [128, 8, 65], BF16)
                Kf = [sb2(f"Kf{i}", [128, 512]) for i in range(2)]; Vf = [sb2(f"Vf{i}", [128, 512]) for i in range(2)]
                LFp = [sb2(f"LFp{i}", [128, 8]) for i in range(2)]
                Kb = sb2("Kb", [128, 512], BF16); KpT = sb2("KpT", [128, 4, 128], BF16)
                VA = [sb2(f"VA{i}", [128, 8, 65], BF16) for i in range(2)]
                PTp = [sb2(f"PTp{i}", [128, 8, 128], BF16) for i in range(2)]
                SUF = sb2("SUF", [128, 8]); TOTL = sb2("TOTL", [128, 8]); sc64 = sb2("sc64", [128, 64])
                PTI = sb2("PTI", [128, 16 * NPG], I32); PTF = sb2("PTF", [128, 16 * NPG]); IDX = sb2("IDX", [128, 16 * NPG], I32)
                fos = sb2("fos", [128, 512]); fosb = sb2("fosb", [128, 512], BF16); FOs = sb2("FOs", [128, 4, 128], BF16)
                rden = sb2("rden", [128, 8])

                def fox_group(j):
                    nk = 4 * j + 4
                    for kt in range(nk):
                        STT(BIAS[:, kt, :], CC[:, kt, :], -1.0, TOT[:, 4 * j + 2, :], ALU.mult, ALU.add, ["CC", "TOT"], ["BIAS"])
                    steps = [(h, kt) for h in range(FH) for kt in range(nk)]

                    def qk(s):
                        h, kt = steps[s]
                        pair, base = h // 2, 64 * (h % 2)
                        si = s % 2
                        diag = kt >= 4 * j
                        MM(ps[si][:, :], KT[base:base + 64, pair, kt * 128:(kt + 1) * 128], QT[base:base + 64, pair, :], True, not diag,
                           ["KT", "QT"], [psr(si)])
                        if diag:
                            MM(ps[si][:, :], identb[:], negq[:, kt - 4 * j, :], False, True, ["identb", "negq"], [psr(si)])

                    def tail(h):
                        pair, base = h // 2, 64 * (h % 2)
                        bo_, bb_ = (2, 3) if h % 2 == 0 else (4, 5)
                        RECIP(rrow[64:65, :], ps[bo_][64:65, :], [psr(bo_)], ["rrow"])
                        MM(ps[bb_][0:64, :], ones[64:65, 0:64], rrow[64:65, :], True, True, ["ones", "rrow"], [psr(bb_)])
                        CP("act", rl[:], ps[bb_][0:64, :], [psr(bb_)], ["rl"])
                        TT("dve", FOg[base:base + 64, pair, :], ps[bo_][0:64, :], rl[:], ALU.mult, [psr(bo_), "rl"], ["FOg"])

                    qk(0)
                    for s, (h, kt) in enumerate(steps):
                        if s + 1 < len(steps):
                            qk(s + 1)
                        pk = f"PT{s % 3}"; pt = PT[s % 3]
                        ACTF(pt[:], ps[s % 2][:, :], AF.Exp, [psr(s % 2), "BIAS"], [pk], bias=BIAS[:, kt, h:h + 1])
                        bo_ = 2 if h % 2 == 0 else 4
                        MM(ps[bo_][0:65, :], VR[:, kt, h, :], pt[:], kt == 0, kt == nk - 1, ["VR", pk], [psr(bo_)])
                        if h > 0 and kt == min(1, nk - 1):
                            tail(h - 1)
                    tail(FH - 1)
                    DMA("sp", FOd[:, :, j * 512:(j + 1) * 512], FOg[:], ["FOg"], ["FOd"])

                def fox_sample(st, sk):
                    DMA("sp", PTI[:], ptab.rearrange("b p -> (b p)").partition_broadcast(128), [], ["PTI"])
                    CP("dve", PTF[:], PTI[:], ["PTI"], ["PTF"])
                    TS("dve", PTF[:], PTF[:], 128.0, iota_p[:, 0:1], ALU.mult, ALU.add, ["PTF", "iota_p"], ["PTF"])
                    TS("dve", PTF[:], PTF[:], float(l * NPHYS * 128), None, ALU.add, None, ["PTF"], ["PTF"])
                    CP("dve", IDX[:], PTF[:], ["PTF"], ["IDX"])
                    MSET("dve", VAs[:], 1.0, ["VAs"])
                    CP("dve", VAs[:, :, 0:64], st[:, 512:1024].rearrange("p (h d) -> p h d", h=8), [sk], ["VAs"])
                    for u in range(2):
                        MSET("pool", VA[u][:], 1.0, [f"VA{u}"])
                        MSET("pool", PTp[u][:], 0.0, [f"PTp{u}"])
                    A_, B_ = 5, 6
                    first = True
                    cnt_pg = 0
                    for b in range(16):
                        MSET("dve", TOTL[:], 0.0, ["TOTL"])
                        for pg in reversed(range(NPG)):
                            u = cnt_pg % 2; cnt_pg += 1
                            col = b * NPG + pg
                            GATHER(Kf[u][:], ck.rearrange("l r c -> (l r) c"), IDX[:, col:col + 1], ["IDX"], [f"Kf{u}"])
                            GATHER(Vf[u][:], cv.rearrange("l r c -> (l r) c"), IDX[:, col:col + 1], ["IDX"], [f"Vf{u}"])
                            GATHER(LFp[u][:], clf.rearrange("l r c -> (l r) c"), IDX[:, col:col + 1], ["IDX"], [f"LFp{u}"])
                            CP("dve", Kb[:], Kf[u][:], [f"Kf{u}"], ["Kb"])
                            for pr in range(4):
                                TR(psb[:, pr * 128:(pr + 1) * 128], Kb[:, pr * 128:(pr + 1) * 128], identb[:], ["Kb", "identb"], [psr(7)])
                            CP("act", KpT[:], psb[:, 0:512].rearrange("p (k t) -> p k t", k=4), [psr(7)], ["KpT"])
                            for h in range(FH):
                                pair, base = h // 2, 64 * (h % 2)
                                MM(ps[0][:, h * 8:(h + 1) * 8], KpT[base:base + 64, pair, :], QTs[base:base + 64, pair, 8 * b:8 * b + 8], True, True,
                                   ["KpT", "QTs"], [psr(0)])
                            MM(ps[1][:, 0:8], ls128[:], LFp[u][:], True, True, ["ls128", f"LFp{u}"], [psr(1)])
                            MM(ps[1][:, 8:16], ones[:], LFp[u][:], True, True, ["ones", f"LFp{u}"], [psr(1)])
                            TT("dve", SUF[:], ps[1][:, 0:8], TOTL[:], ALU.add, [psr(1), "TOTL"], ["SUF"])
                            TT("dve", TOTL[:], ps[1][:, 8:16], TOTL[:], ALU.add, [psr(1), "TOTL"], ["TOTL"])
                            TT("dve", sc64[:].rearrange("p (h q) -> p h q", h=8), ps[0][:, 0:64].rearrange("p (h q) -> p h q", h=8),
                               SUF[:].unsqueeze(2).to_broadcast([128, 8, 8]), ALU.add, [psr(0), "SUF"], ["sc64"])
                            ACTF(PTp[u][:, :, 8 * b:8 * b + 8], sc64[:].rearrange("p (h q) -> p h q", h=8), AF.Exp, ["sc64"], [f"PTp{u}"])
                            CP("pool", VA[u][:, :, 0:64], Vf[u][:].rearrange("p (h d) -> p h d", h=8), [f"Vf{u}"], [f"VA{u}"])
                            for h in range(FH):
                                bk = A_ if h < 4 else B_
                                MM(ps[bk][:, (h % 4) * 65:(h % 4 + 1) * 65], PTp[u][:, h, :], VA[u][:, h, :], first and h % 4 == 0, False,
                                   [f"PTp{u}", f"VA{u}"], [psr(bk)])
                            first = False
                        for u in range(2):
                            MSET("pool", PTp[u][:, :, 8 * b:8 * b + 8], 0.0, [f"PTp{u}"])
                    MM(ps[1][:, 0:8], ub8[:], st[:, 1024:1032], True, True, ["ub8", sk], [psr(1)])
                    TS("dve", SUF[:], ps[1][:, 0:8], -1.0, None, ALU.mult, None, [psr(1)], ["SUF"])
                    for h in range(FH):
                        pair, base = h // 2, 64 * (h % 2)
                        si = h % 2
                        MM(ps[si][:, 0:128], KTs[base:base + 64, pair, :], QTs[base:base + 64, pair, :], True, False, ["KTs", "QTs"], [psr(si)])
                        MM(ps[si][:, 0:128], identb[:], negm8[:], False, True, ["identb", "negm8"], [psr(si)])
                        pk = f"PT{h % 3}"; pt = PT[h % 3]
                        ACTF(pt[:, 0:128], ps[si][:, 0:128], AF.Exp, [psr(si), "SUF"], [pk], bias=SUF[:, h:h + 1])
                        bk = A_ if h < 4 else B_
                        MM(ps[bk][:, (h % 4) * 65:(h % 4 + 1) * 65], pt[:, 0:128], VAs[:, h, :], first, True, [pk, "VAs"], [psr(bk)])
                    for half, bk in ((0, A_), (1, B_)):
                        v = ps[bk][:, 0:260].rearrange("p (h e) -> p h e", h=4)
                        CP("dve", rden[:, half * 4:(half + 1) * 4], v[:, :, 64], [psr(bk)], ["rden"])
                    RECIP(rden[:], rden[:], ["rden"], ["rden"])
                    for half, bk in ((0, A_), (1, B_)):
                        v = ps[bk][:, 0:260].rearrange("p (h e) -> p h e", h=4)
                        TT("dve", fos[:, half * 256:(half + 1) * 256].rearrange("p (h d) -> p h d", h=4), v[:, :, 0:64],
                           rden[:, half * 4:(half + 1) * 4].unsqueeze(2).to_broadcast([128, 4, 64]), ALU.mult, [psr(bk), "rden"], ["fos"])
                    CP("dve", fosb[:], fos[:], ["fos"], ["fosb"])
                    for pr in range(4):
                        TR(psb[:, pr * 128:(pr + 1) * 128], fosb[:, pr * 128:(pr + 1) * 128], identb[:], ["fosb", "identb"], [psr(7)])
                    CP("dve", FOs[:], psb[:, 0:512].rearrange("p (k t) -> p k t", k=4), [psr(7)], ["FOs"])
                    DMA("sp", FOd[:, :, S:S + 128], FOs[:], ["FOs"], ["FOd"])

                for i in range(NT + 1):
                    is_s = (i == NT)
                    xa, xkey = load_x(l, i, src_h)
                    goff = (i % 4) * 128 if not is_s else 0
                    rmsnorm_tile(xa, xkey, 0)
                    for cb, (c0, cw, pb) in enumerate(((512, 512, 0), (1024, 512, 1), (1536, 8, 2))):
                        for kc in range(8):
                            MM(ps[pb][:, 0:cw], nT[:, kc, :], WA[:, kc, c0:c0 + cw], kc == 0, kc == 7, ["nT", "WA"], [psr(pb)])
                    st = stg[i % 2]; sk = f"stg{i % 2}"
                    CP("act", st[:, 0:512], ps[0][:, 0:512], [psr(0)], [sk])
                    CP("dve", st[:, 512:1024], ps[1][:, 0:512], [psr(1)], [sk])
                    if not is_s:
                        CP("pool", VR[:, i, :, 0:64], st[:, 512:1024].rearrange("p (h d) -> p h d", h=8), [sk], ["VR"])
                    TT("dve", LOGF[:], ps[2][:, 0:8], bfb[:], ALU.add, [psr(2), "bfb"], ["LOGF"])
                    ACTF(LOGF[:], LOGF[:], AF.Exp, ["LOGF"], ["LOGF"], scale=-1.0)
                    ACTF(LOGF[:], LOGF[:], AF.Ln, ["LOGF"], ["LOGF"], bias=1.0)
                    TS("dve", st[:, 1024:1032], LOGF[:], -1.0, None, ALU.mult, None, ["LOGF"], [sk])
                    ok_, ov_, ol_ = (o_fks, o_fvs, o_fls) if is_s else (o_fkp, o_fvp, o_flp)
                    r0 = 0 if is_s else i * 128
                    DMA("sp", ok_[l, r0:r0 + 128, :], st[:, 0:512], [sk], [])
                    DMA("sp", ov_[l, r0:r0 + 128, :], st[:, 512:1024], [sk], [])
                    DMA("sp", ol_[l, r0:r0 + 128, :], st[:, 1024:1032], [sk], [])
                    for cj in range(8):
                        pb = 3 + cj // 4
                        for kc in range(8):
                            MM(ps[pb][:, (cj % 4) * 128:(cj % 4 + 1) * 128], WA[:, kc, cj * 128:(cj + 1) * 128], nT[:, kc, :],
                               kc == 0, kc == 7, ["nT", "WA"], [psr(pb)])
                    qdst, qkey = (QTs[:, :, :], "QTs") if is_s else (QT[:, :, goff:goff + 128], "QT")
                    kdst, kkey = (KTs[:, :, :], "KTs") if is_s else (KT[:, :, i * 128:(i + 1) * 128], "KT")
                    P.emit("act", lambda e, qdst=qdst: e.mul(out=qdst, in_=ps[3][:, 0:512].rearrange("p (k t) -> p k t", k=4), mul=0.125),
                           reads=[psr(3)], writes=[qkey])
                    CP("dve", kdst, ps[4][:, 0:512].rearrange("p (k t) -> p k t", k=4), [psr(4)], [kkey])
                    if not is_s:
                        MM(ps[5][:, 0:8], ub128[:], st[:, 1024:1032], True, True, ["ub128", sk], [psr(5)])
                        MM(ps[5][:, 8:16], ones[:], st[:, 1024:1032], True, True, ["ones", sk], [psr(5)])
                        TT("dve", CC[:, i, :], ps[5][:, 0:8], TOT[:, i, :], ALU.add, [psr(5), "TOT"], ["CC"])
                        TT("dve", TOT[:, i + 1, :], ps[5][:, 8:16], TOT[:, i, :], ALU.add, [psr(5), "TOT"], ["TOT"])
                        if i % 4 == 3:
                            fox_group(i // 4)
                    else:
                        fox_sample(st, sk)

        def sweep_gdn(l, src_h):
            with contextlib.ExitStack() as ses:
                sb2 = mk_sb(ses)
                P.barrier()
                WA = sb2("WAg", [128, 8, 2056], BF16)
                load_w(WA, "WA", w_in[l, :, 1544:3600], 2056)
                KP = {}; KS = {}
                for nm, src in k_p.items():
                    KP[nm] = sb2("kp_" + nm, src.shape); DMA("sp", KP[nm][:], src, [], ["kp_" + nm])
                for nm, src in k_s.items():
                    KS[nm] = sb2("ks_" + nm, src.shape); DMA("sp", KS[nm][:], src, [], ["ks_" + nm])
                shc = sb2("shc", [128, 3, 128]); shp = sb2("shp", [128, 3, 128]); shs = sb2("shs", [48, 3, 128])
                DMA("sp", shc[:], k_shc[:, :, :], [], ["shc"]); DMA("sp", shp[:], k_shp[:, :, :], [], ["shp"]); DMA("sp", shs[:], k_shs[:, :, :], [], ["shs"])
                qkvf = [sb2(f"qkvf{i}", [128, GC3]) for i in range(2)]
                cbuf = sb2("cbuf", [48, GC3]); cwb = sb2("cwb", [128, 4, GC3], BF16)
                actv = sb2("actv", [128, GC3]); gsm = sb2("gsm", [128, 64]); alb = sb2("alb", [128, 2 * GH]); gnb = sb2("gnb", [128, GD])
                tq = sb2("tq", [128, 512]); tk = sb2("tk", [128, 512]); tkb = sb2("tkb", [128, 512]); trv = sb2("trv", [128, 512])
                trk = sb2("trk", [128, 512]); tqd = sb2("tqd", [128, 512]); tkd = sb2("tkd", [128, 512]); tgg = sb2("tgg", [128, 512])
                fkT = sb2("fkT", [128, 512]); fkbT = sb2("fkbT", [128, 512]); fqT = sb2("fqT", [128, 512]); fqdT = sb2("fqdT", [128, 512])
                DG = sb2("DG", [128, 512]); decT = sb2("decT", [128, 512]); dec = sb2("dec", [128, 512])
                Mk = [sb2(f"Mk{i}", [128, 512]) for i in range(2)]; MkT = [sb2(f"MkT{i}", [128, 512]) for i in range(2)]
                Pk = [sb2(f"Pk{i}", [128, 512]) for i in range(2)]; qkTm = sb2("qkTm", [128, 512])
                uacc = sb2("uacc", [128, 512]); kcT = sb2("kcT", [128, 512])
                kcTm = sb2("kcTm", [128, 4, 128]); qdTm = sb2("qdTm", [128, 4, 128]); kdm = sb2("kdm", [128, 4, 128])
                GT = sb2("GT", [128, 16, GH]); SG = sb2("SG", [128, GH, GD]); SBs = [sb2(f"SBs{i}", [128, GD]) for i in range(2)]
                osb = sb2("osb", [128, 512]); gob = sb2("gob", [128, 512], BF16); GOt = sb2("GOt", [128, 4, 128], BF16)
                DMA("sp", alb[:, 0:4], a_log[l].partition_broadcast(128), [], ["alb"])
                DMA("sp", alb[:, 4:8], dt_b[l].partition_broadcast(128), [], ["alb"])
                ACTF(alb[:, 0:4], alb[:, 0:4], AF.Exp, ["alb"], ["alb"])
                TS("dve", alb[:, 0:4], alb[:, 0:4], -1.0, None, ALU.mult, None, ["alb"], ["alb"])
                DMA("sp", gnb[:], gnw[l].partition_broadcast(128), [], ["gnb"])
                P.emit("pool", lambda e: e.dma_start(out=cwb[:].rearrange("p w c -> p (w c)"), in_=conv_w[l].rearrange("w c -> (w c)").partition_broadcast(128)),
                       writes=["cwb"], dma=True)
                DMA("sp", cbuf[:], scv[l].rearrange("b r c -> (b r) c"), [], ["cbuf"])
                MSET("dve", SG[:], 0.0, ["SG"])
                bank = [0]

                def nbk():
                    bank[0] = (bank[0] + 1) % 7
                    return bank[0]

                hs = lambda t, h: t[:, h * 128:(h + 1) * 128]
                v4 = lambda t: t[:, :].rearrange("p (h d) -> p h d", h=4)

                def bc(ap4):
                    return ap4.unsqueeze(2).to_broadcast([128, 4, 128])

                def gdn_tile(i, is_s):
                    K_ = KS if is_s else KP
                    kk = "ks_" if is_s else "kp_"
                    cs = 8 if is_s else cs_p
                    nb = 128 // cs
                    L = int(math.log2(cs)) - 1
                    qv = qkvf[i % 2]; qk_ = f"qkvf{i % 2}"
                    b_q = [nbk(), nbk(), nbk()]
                    for j in range(3):
                        for kc in range(8):
                            MM(ps[b_q[j]][:, :], nT[:, kc, :], WA[:, kc, j * 512:(j + 1) * 512], kc == 0, kc == 7, ["nT", "WA"], [psr(b_q[j])])
                        CP("act" if j == 1 else "dve", qv[:, j * 512:(j + 1) * 512], ps[b_q[j]][:, :], [psr(b_q[j])], [qk_])
                    b_g = nbk()
                    for kc in range(8):
                        MM(ps[b_g][:, 0:8], nT[:, kc, :], WA[:, kc, 1536:1544], kc == 0, kc == 7, ["nT", "WA"], [psr(b_g)])
                    b_gg = nbk()
                    for kc in range(8):
                        MM(ps[b_gg][:, :], nT[:, kc, :], WA[:, kc, 1544:2056], kc == 0, kc == 7, ["nT", "WA"], [psr(b_gg)])
                    ACTF(tgg[:], ps[b_gg][:, :], AF.Silu, [psr(b_gg)], ["tgg"])
                    ACTF(gsm[:, 0:4], ps[b_g][:, 4:8], AF.Sigmoid, [psr(b_g)], ["gsm"])
                    TT("dve", gsm[:, 4:8], ps[b_g][:, 0:4], alb[:, 4:8], ALU.add, [psr(b_g), "alb"], ["gsm"])
                    ACTF(gsm[:, 4:8], gsm[:, 4:8], AF.Exp, ["gsm"], ["gsm"])
                    ACTF(gsm[:, 4:8], gsm[:, 4:8], AF.Ln, ["gsm"], ["gsm"], bias=1.0)
                    TT("dve", gsm[:, 4:8], gsm[:, 4:8], alb[:, 0:4], ALU.mult, ["gsm", "alb"], ["gsm"])
                    TT("pool", actv[:], qv[:], cwb[:, 3, :], ALU.mult, [qk_, "cwb"], ["actv"])
                    for s in (1, 2, 3):
                        for j in range(3):
                            bs_ = nbk()
                            cs_ = slice(j * 512, (j + 1) * 512)
                            if is_s:
                                MM(ps[bs_][:, :], K_["shc"][:, s - 1, :], qv[:, cs_], True, False, [kk + "shc", qk_], [psr(bs_)])
                                MM(ps[bs_][:, :], shs[:, s - 1, :], cbuf[:, cs_], False, True, ["shs", "cbuf"], [psr(bs_)])
                            else:
                                first = (i == 0)
                                MM(ps[bs_][:, :], shc[:, s - 1, :], qv[:, cs_], True, first, ["shc", qk_], [psr(bs_)])
                                if not first:
                                    MM(ps[bs_][:, :], shp[:, s - 1, :], qkvf[(i - 1) % 2][:, cs_], False, True, ["shp", f"qkvf{(i - 1) % 2}"], [psr(bs_)])
                            TT("dve", stg[0][:, 0:512], ps[bs_][:, :], cwb[:, 3 - s, cs_], ALU.mult, [psr(bs_), "cwb"], ["stg0"])
                            TT("pool", actv[:, cs_], actv[:, cs_], stg[0][:, 0:512], ALU.add, ["actv", "stg0"], ["actv"])
                    if is_s:
                        for b in range(16):
                            DMA("sp", o_gcs[l, b, :, :], qv[8 * b + 5:8 * b + 8, :], [qk_], [])
                    elif i == NT - 1:
                        DMA("sp", o_gcp[l, :, :], qv[125:128, :], [qk_], [])
                    ACTF(actv[:], actv[:], AF.Silu, ["actv"], ["actv"])
                    TT("dve", stg[1][:, 0:1024], actv[:, 0:1024], actv[:, 0:1024], ALU.mult, ["actv"], ["stg1"])
                    REDUCE(gsm[:, 8:16], stg[1][:, 0:1024].rearrange("p (h d) -> p h d", h=8), ["stg1"], ["gsm"])
                    ACTF(gsm[:, 8:16], gsm[:, 8:16], AF.Sqrt, ["gsm", "epsT"], ["gsm"], bias=epsT[:, 0:1])
                    RECIP(gsm[:, 8:16], gsm[:, 8:16], ["gsm"], ["gsm"])
                    TS("dve", gsm[:, 8:12], gsm[:, 8:12], GD ** -0.5, None, ALU.mult, None, ["gsm"], ["gsm"])
                    a3 = actv[:, 0:512].rearrange("p (h d) -> p h d", h=4)
                    k3 = actv[:, 512:1024].rearrange("p (h d) -> p h d", h=4)
                    v3 = actv[:, 1024:1536].rearrange("p (h d) -> p h d", h=4)
                    TT("dve", v4(tq), a3, bc(gsm[:, 8:12]), ALU.mult, ["actv", "gsm"], ["tq"])
                    TT("dve", v4(tk), k3, bc(gsm[:, 12:16]), ALU.mult, ["actv", "gsm"], ["tk"])
                    b1 = nbk()
                    MM(ps[b1][:, 0:4], K_["ub"][:], gsm[:, 4:8], True, True, [kk + "ub", "gsm"], [psr(b1)])
                    CP("dve", gsm[:, 16:20], ps[b1][:, 0:4], [psr(b1)], ["gsm"])
                    MM(ps[b1][:, 8:12], K_["bl"][:], gsm[:, 16:20], True, True, [kk + "bl", "gsm"], [psr(b1)])
                    for b in range(nb):
                        MM(ps[b1][:, 16 + 4 * b:20 + 4 * b], K_["elast"][:, b, :], gsm[:, 16:20], True, True, [kk + "elast", "gsm"], [psr(b1)])
                    ACTF(gsm[:, 24:28], gsm[:, 16:20], AF.Exp, ["gsm"], ["gsm"])
                    TT("dve", gsm[:, 28:32], ps[b1][:, 8:12], gsm[:, 16:20], ALU.subtract, [psr(b1), "gsm"], ["gsm"])
                    ACTF(gsm[:, 28:32], gsm[:, 28:32], AF.Exp, ["gsm"], ["gsm"])
                    TS("dve", gsm[:, 32:36], gsm[:, 16:20], -1.0, None, ALU.mult, None, ["gsm"], ["gsm"])
                    ACTF(GT[:, 0:nb, :], ps[b1][:, 16:16 + 4 * nb].rearrange("p (b h) -> p b h", h=4), AF.Exp, [psr(b1)], ["GT"])
                    TT("dve", v4(tkb), v4(tk), bc(gsm[:, 0:4]), ALU.mult, ["tk", "gsm"], ["tkb"])
                    TT("pool", v4(trv), v3, bc(gsm[:, 0:4]), ALU.mult, ["actv", "gsm"], ["trv"])
                    TT("dve", v4(trk), v4(tkb), bc(gsm[:, 24:28]), ALU.mult, ["tkb", "gsm"], ["trk"])
                    TT("pool", v4(tqd), v4(tq), bc(gsm[:, 24:28]), ALU.mult, ["tq", "gsm"], ["tqd"])
                    TT("dve", v4(tkd), v4(tk), bc(gsm[:, 28:32]), ALU.mult, ["tk", "gsm"], ["tkd"])
                    for src_, sk_, dst_, dk_ in ((tk, "tk", fkT, "fkT"), (tkb, "tkb", fkbT, "fkbT"), (tq, "tq", fqT, "fqT"), (tqd, "tqd", fqdT, "fqdT")):
                        bt = nbk()
                        for h in range(4):
                            TR(hs(ps[bt], h), hs(src_, h), ident[:], [sk_, "ident"], [psr(bt)])
                        CP("act", dst_[:], ps[bt][:, :], [psr(bt)], [dk_])
                    for h in range(4):
                        TS("dve", hs(DG, h), ident[:], gsm[:, 16 + h:17 + h], None, ALU.mult, None, ["ident", "gsm"], ["DG"])
                    bd, be = nbk(), nbk()
                    for h in range(4):
                        MM(hs(ps[bd], h), ones[:], hs(DG, h), True, False, ["ones", "DG"], [psr(bd)])
                        MM(hs(ps[bd], h), ident[:], K_["negmt"][:], False, True, ["ident", kk + "negmt"], [psr(bd)])
                        MM(hs(ps[be], h), ones[:], hs(DG, h), True, False, ["ones", "DG"], [psr(be)])
                        MM(hs(ps[be], h), ident[:], K_["posm"][:], False, True, ["ident", kk + "posm"], [psr(be)])
                    for h in range(4):
                        ACTF(hs(decT, h), hs(ps[bd], h), AF.Exp, [psr(bd), "gsm"], ["decT"], bias=gsm[:, 32 + h:33 + h])
                        ACTF(hs(dec, h), hs(ps[be], h), AF.Exp, [psr(be), "gsm"], ["dec"], bias=gsm[:, 16 + h:17 + h], scale=-1.0)
                    ba, bb_, bc_ = nbk(), nbk(), nbk()
                    for h in range(4):
                        MM(hs(ps[ba], h), hs(fkT, h), hs(fkbT, h), True, True, ["fkT", "fkbT"], [psr(ba)])
                        MM(hs(ps[bb_], h), hs(fkbT, h), hs(fkT, h), True, True, ["fkT", "fkbT"], [psr(bb_)])
                        MM(hs(ps[bc_], h), hs(fkT, h), hs(fqT, h), True, True, ["fkT", "fqT"], [psr(bc_)])
                    STT(Mk[0][:], ps[ba][:, :], -1.0, decT[:], ALU.mult, ALU.mult, [psr(ba), "decT"], ["Mk0"])
                    STT(MkT[0][:], ps[bb_][:, :], -1.0, dec[:], ALU.mult, ALU.mult, [psr(bb_), "dec"], ["MkT0"])
                    TT("dve", qkTm[:], ps[bc_][:, :], decT[:], ALU.mult, [psr(bc_), "decT"], ["qkTm"])
                    su4 = K_["su"][:].unsqueeze(1).to_broadcast([128, 4, 128]); sl4 = K_["sl"][:].unsqueeze(1).to_broadcast([128, 4, 128])
                    id4 = ident[:].unsqueeze(1).to_broadcast([128, 4, 128])
                    TT("dve", v4(Mk[0]), v4(Mk[0]), su4, ALU.mult, ["Mk0", kk + "su"], ["Mk0"])
                    TT("pool", v4(MkT[0]), v4(MkT[0]), sl4, ALU.mult, ["MkT0", kk + "sl"], ["MkT0"])
                    TT("dve", v4(Pk[0]), v4(Mk[0]), id4, ALU.add, ["Mk0", "ident"], ["Pk0"])
                    cur = 0
                    for lev in range(L):
                        nx = 1 - cur
                        last = (lev == L - 1)
                        bm, bmt = nbk(), nbk()
                        for h in range(4):
                            if not last:
                                MM(hs(ps[bm], h), hs(MkT[cur], h), hs(Mk[cur], h), True, True, [f"Mk{cur}", f"MkT{cur}"], [psr(bm)])
                            MM(hs(ps[bmt], h), hs(Mk[cur], h), hs(MkT[cur], h), True, True, [f"Mk{cur}", f"MkT{cur}"], [psr(bmt)])
                        if not last:
                            CP("act", Mk[nx][:], ps[bm][:, :], [psr(bm)], [f"Mk{nx}"])
                        CP("dve", MkT[nx][:], ps[bmt][:, :], [psr(bmt)], [f"MkT{nx}"])
                        bp = nbk()
                        for h in range(4):
                            MM(hs(ps[bp], h), hs(MkT[nx], h), hs(Pk[cur], h), True, True, [f"MkT{nx}", f"Pk{cur}"], [psr(bp)])
                        TT("dve", Pk[nx][:], ps[bp][:, :], Pk[cur][:], ALU.add, [psr(bp), f"Pk{cur}"], [f"Pk{nx}"])
                        cur = nx
                    TTm = Pk[cur]; ttk = f"Pk{cur}"
                    bu, bk2 = nbk(), nbk()
                    for h in range(4):
                        MM(hs(ps[bu], h), hs(TTm, h), hs(trv, h), True, True, [ttk, "trv"], [psr(bu)])
                        MM(hs(ps[bk2], h), hs(trk, h), hs(TTm, h), True, True, [ttk, "trk"], [psr(bk2)])
                    CP("dve", uacc[:], ps[bu][:, :], [psr(bu)], ["uacc"])
                    CP("act", kcT[:], ps[bk2][:, :], [psr(bk2)], ["kcT"])
                    bo = 7 if False else nbk()
                    cm_ = K_["colmask"]; rm_ = K_["rowmask"]
                    ngrp = (nb + 3) // 4
                    for h in range(4):
                        for g in range(ngrp):
                            b0 = 4 * g; nbg = min(4, nb - b0)
                            TT("dve", kcTm[:, 0:nbg, :], hs(kcT, h).unsqueeze(1).to_broadcast([128, nbg, 128]), cm_[:, b0:b0 + nbg, :], ALU.mult,
                               ["kcT", kk + "colmask"], ["kcTm"])
                            TT("pool", qdTm[:, 0:nbg, :], hs(fqdT, h).unsqueeze(1).to_broadcast([128, nbg, 128]), cm_[:, b0:b0 + nbg, :], ALU.mult,
                               ["fqdT", kk + "colmask"], ["qdTm"])
                            TT("pool", kdm[:, 0:nbg, :], hs(tkd, h).unsqueeze(1).to_broadcast([128, nbg, 128]),
                               rm_[:, b0:b0 + nbg].unsqueeze(2).to_broadcast([128, nbg, 128]), ALU.mult, ["tkd", kk + "rowmask"], ["kdm"])
                            for jj in range(nbg):
                                b = b0 + jj
                                if is_s:
                                    skey = f"SBs{b % 2}"
                                    S_h = SBs[b % 2][:, :]
                                    DMA("sp", S_h, sgd[l, b, h, :, :], [], [skey])
                                else:
                                    skey = "SG"
                                    S_h = SG[:, h, :]
                                bx = nbk()
                                while bx == bo:
                                    bx = nbk()
                                MM(ps[bx][:, 0:128], kcTm[:, jj, :], S_h, True, True, ["kcTm", skey], [psr(bx)])
                                TT("dve", hs(uacc, h), hs(uacc, h), ps[bx][:, 0:128], ALU.subtract, ["uacc", psr(bx)], ["uacc"])
                                MM(hs(ps[bo], h), qdTm[:, jj, :], S_h, b == 0, False, ["qdTm", skey], [psr(bo)])
                                MM(ps[bx][:, 128:256], kdm[:, jj, :], hs(uacc, h), True, True, ["kdm", "uacc"], [psr(bx)])
                                STT(S_h, S_h, GT[:, b, h:h + 1], ps[bx][:, 128:256], ALU.mult, ALU.add, [skey, "GT", psr(bx)], [skey])
                                if is_s:
                                    DMA("sp", o_gss[l, b, h, :, :], S_h, [skey], [])
                        MM(hs(ps[bo], h), hs(qkTm, h), hs(uacc, h), False, True, ["qkTm", "uacc"], [psr(bo)])
                    if (not is_s) and i == NT - 1:
                        DMA("sp", o_gsp[l].rearrange("h d e -> d h e"), SG[:], ["SG"], [])
                    CP("dve", osb[:], ps[bo][:, :], [psr(bo)], ["osb"])
                    TT("dve", stg[1][:, 0:512], osb[:], osb[:], ALU.mult, ["osb"], ["stg1"])
                    REDUCE(gsm[:, 40:44], stg[1][:, 0:512].rearrange("p (h d) -> p h d", h=4), ["stg1"], ["gsm"])
                    ACTF(gsm[:, 40:44], gsm[:, 40:44], AF.Sqrt, ["gsm", "epsT"], ["gsm"], bias=epsT[:, 0:1], scale=1.0 / GD)
                    RECIP(gsm[:, 40:44], gsm[:, 40:44], ["gsm"], ["gsm"])
                    TT("dve", v4(osb), v4(osb), bc(gsm[:, 40:44]), ALU.mult, ["osb", "gsm"], ["osb"])
                    TT("dve", v4(osb), v4(osb), gnb[:].unsqueeze(1).to_broadcast([128, 4, 128]), ALU.mult, ["osb", "gnb"], ["osb"])
                    TT("dve", gob[:], osb[:], tgg[:], ALU.mult, ["osb", "tgg"], ["gob"])
                    for pr in range(4):
                        TR(psb[:, pr * 128:(pr + 1) * 128], gob[:, pr * 128:(pr + 1) * 128], identb[:], ["gob", "identb"], [psr(7)])
                    CP("dve", GOt[:], psb[:, 0:512].rearrange("p (k t) -> p k t", k=4), [psr(7)], ["GOt"])
                    c0 = S if is_s else i * 128
                    DMA("sp", GOd[:, :, c0:c0 + 128], GOt[:], ["GOt"], ["GOd"])

                for i in range(NT + 1):
                    xa, xkey = load_x(l, i, src_h)
                    rmsnorm_tile(xa, xkey, 0)
                    gdn_tile(i, i == NT)

        def sweep_mem(l, src_h, dst_h):
            with contextlib.ExitStack() as ses:
                sb2 = mk_sb(ses)
                P.barrier()
                WO = sb2("WO", [128, 8, D], BF16); WQ = sb2("WQ", [128, 8, MW], BF16); WM = sb2("WM", [128, 4, D], BF16)
                WKV = sb2("WKV", [128, 8, 2 * MW], BF16)
                load_w(WO, "WO", w_out[l], D); load_w(WQ, "WQ", w_mq[l], MW); load_w(WM, "WM", w_mo[l], D); load_w(WKV, "WKV", w_mkv[l], 2 * MW)
                mixT = sb2("mixT", [128, 8, 128], BF16)
                mkT = sb2("mkT", [128, 8, 128], BF16)
                mvA = sb2("mvA", [128, 2, 4, 129], BF16)
                qT = sb2("qT", [128, 4, 128], BF16)
                PTm = sb2("PTm", [128, 8, 128], BF16)
                PTpad = [sb2(f"PTpad{i}", [128, 8, 128], BF16) for i in range(2)]
                Kf2 = sb2("Kf2", [128, 2, MW]); Vf2 = sb2("Vf2", [128, 2, MW]); Kb2 = sb2("Kb2", [128, 2, MW], BF16)
                mkTb = sb2("mkTb", [128, 8, 128], BF16); mvAb = [sb2(f"mvAb{i}", [128, 2, 4, 129], BF16) for i in range(2)]
                osm = sb2("osm", [128, 512]); osmb = sb2("osmb", [128, 512], BF16); oT = sb2("oT", [128, 4, 128], BF16)
                rden = sb2("rdenm", [128, 4]); sc64 = sb2("sc64m", [128, 64])
                MSET("dve", mvA[:], 1.0, ["mvA"])
                for mt in range(2):
                    xb_ = xt[mt]; xkey = f"xt{mt}"
                    DMA("sp", xb_[:], memp[mt * 128:(mt + 1) * 128, :], [], [xkey])
                    rmsnorm_tile(xb_[:], xkey, 3)
                    for nbk_ in range(2):
                        for kc in range(8):
                            MM(ps[nbk_][:, :], nT[:, kc, :], WKV[:, kc, nbk_ * 512:(nbk_ + 1) * 512], kc == 0, kc == 7, ["nT", "WKV"], [psr(nbk_)])
                    st = stg[mt]; sk = f"stg{mt}"
                    CP("act", st[:, 0:512], ps[0][:, :], [psr(0)], [sk])
                    CP("dve", st[:, 512:1024], ps[1][:, :], [psr(1)], [sk])
                    DMA("sp", o_mkp[l, mt * 128:(mt + 1) * 128, :], st[:, 0:512], [sk], [])
                    DMA("sp", o_mvp[l, mt * 128:(mt + 1) * 128, :], st[:, 512:1024], [sk], [])
                    CP("pool", mvA[:, mt, :, 0:128], st[:, 512:1024].rearrange("p (h d) -> p h d", h=4), [sk], ["mvA"])
                    for h in range(4):
                        for kc in range(8):
                            MM(ps[2][:, h * 128:(h + 1) * 128], WKV[:, kc, h * 128:(h + 1) * 128], nT[:, kc, :], kc == 0, kc == 7, ["nT", "WKV"], [psr(2)])
                    CP("dve", mkT[:, mt * 4:(mt + 1) * 4, :], ps[2][:, :].rearrange("p (h t) -> p h t", h=4), [psr(2)], ["mkT"])
                for u in range(2):
                    MSET("pool", mvAb[u][:], 1.0, [f"mvAb{u}"])
                    MSET("pool", PTpad[u][:], 0.0, [f"PTpad{u}"])

                for i in range(NT + 1):
                    is_s = (i == NT)
                    xa, xkey = load_x(l, i, src_h)
                    c0 = S if is_s else i * 128
                    DMA("sp", mixT[:, 0:4, :], FOd[:, :, c0:c0 + 128], ["FOd"], ["mixT"])
                    DMA("sp", mixT[:, 4:8, :], GOd[:, :, c0:c0 + 128], ["GOd"], ["mixT"])
                    for nb_ in range(2):
                        for kc in range(8):
                            MM(ps[nb_][:, :], mixT[:, kc, :], WO[:, kc, nb_ * 512:(nb_ + 1) * 512], kc == 0, kc == 7, ["mixT", "WO"], [psr(nb_)])
                        TT("dve", xa[:, nb_ * 512:(nb_ + 1) * 512], xa[:, nb_ * 512:(nb_ + 1) * 512], ps[nb_][:, :], ALU.add, [xkey, psr(nb_)], [xkey])
                    rmsnorm_tile(xa, xkey, 1)
                    for h in range(4):
                        for kc in range(8):
                            MM(ps[2][:, h * 128:(h + 1) * 128], WQ[:, kc, h * 128:(h + 1) * 128], nT[:, kc, :], kc == 0, kc == 7, ["nT", "WQ"], [psr(2)])
                    P.emit("act", lambda e: e.mul(out=qT[:], in_=ps[2][:, :].rearrange("p (h t) -> p h t", h=4), mul=MD ** -0.5), reads=[psr(2)], writes=["qT"])
                    A_, B_ = 5, 6
                    if not is_s:
                        for mt in range(2):
                            for h in range(4):
                                MM(ps[3 + mt][:, h * 128:(h + 1) * 128], mkT[:, mt * 4 + h, :], qT[:, h, :], True, True, ["mkT", "qT"], [psr(3 + mt)])
                            ACTF(PTm[:, mt * 4:(mt + 1) * 4, :], ps[3 + mt][:, :].rearrange("p (h t) -> p h t", h=4), AF.Exp, [psr(3 + mt)], ["PTm"])
                        for h in range(4):
                            bk = A_ if h < 2 else B_
                            for mt in range(2):
                                MM(ps[bk][:, (h % 2) * 129:(h % 2 + 1) * 129], PTm[:, mt * 4 + h, :], mvA[:, mt, h, :], mt == 0, mt == 1, ["PTm", "mvA"], [psr(bk)])
                    else:
                        for b in range(16):
                            u = b % 2
                            DMA("sp", Kf2[:], cmk[l, b].rearrange("(t p) c -> p t c", p=128), [], ["Kf2"])
                            DMA("sp", Vf2[:], cmv[l, b].rearrange("(t p) c -> p t c", p=128), [], ["Vf2"])
                            CP("dve", Kb2[:], Kf2[:], ["Kf2"], ["Kb2"])
                            for mt in range(2):
                                for h in range(4):
                                    TR(psb[:, (mt * 4 + h) * 128:(mt * 4 + h + 1) * 128], Kb2[:, mt, h * 128:(h + 1) * 128], identb[:], ["Kb2", "identb"], [psr(7)])
                            CP("act", mkTb[:], psb[:, 0:1024].rearrange("p (k t) -> p k t", k=8), [psr(7)], ["mkTb"])
                            CP("pool", mvAb[u][:, :, :, 0:128], Vf2[:].rearrange("p t (h d) -> p t h d", h=4), ["Vf2"], [f"mvAb{u}"])
                            for mt in range(2):
                                for h in range(4):
                                    j = mt * 4 + h
                                    MM(ps[3][:, j * 8:(j + 1) * 8], mkTb[:, j, :], qT[:, h, 8 * b:8 * b + 8], True, True, ["mkTb", "qT"], [psr(3)])
                            ACTF(PTpad[u][:, :, 8 * b:8 * b + 8], ps[3][:, 0:64].rearrange("p (j q) -> p j q", j=8), AF.Exp, [psr(3)], [f"PTpad{u}"])
                            for h in range(4):
                                bk = A_ if h < 2 else B_
                                for mt in range(2):
                                    MM(ps[bk][:, (h % 2) * 129:(h % 2 + 1) * 129], PTpad[u][:, mt * 4 + h, :], mvAb[u][:, mt, h, :],
                                       b == 0 and mt == 0 and h % 2 == 0, b == 15 and mt == 1, [f"PTpad{u}", f"mvAb{u}"], [psr(bk)])
                            MSET("pool", PTpad[u][:, :, 8 * b:8 * b + 8], 0.0, [f"PTpad{u}"])
                    for half, bk in ((0, A_), (1, B_)):
                        v = ps[bk][:, 0:258].rearrange("p (h e) -> p h e", h=2)
                        CP("dve", rden[:, half * 2:(half + 1) * 2], v[:, :, 128], [psr(bk)], ["rden"])
                    RECIP(rden[:], rden[:], ["rden"], ["rden"])
                    for half, bk in ((0, A_), (1, B_)):
                        v = ps[bk][:, 0:258].rearrange("p (h e) -> p h e", h=2)
                        TT("dve", osm[:, half * 256:(half + 1) * 256].rearrange("p (h d) -> p h d", h=2), v[:, :, 0:128],
                           rden[:, half * 2:(half + 1) * 2].unsqueeze(2).to_broadcast([128, 2, 128]), ALU.mult, [psr(bk), "rden"], ["osm"])
                    CP("dve", osmb[:], osm[:], ["osm"], ["osmb"])
                    for pr in range(4):
                        TR(psb[:, pr * 128:(pr + 1) * 128], osmb[:, pr * 128:(pr + 1) * 128], identb[:], ["osmb", "identb"], [psr(7)])
                    CP("dve", oT[:], psb[:, 0:512].rearrange("p (k t) -> p k t", k=4), [psr(7)], ["oT"])
                    for nb_ in range(2):
                        for kc in range(4):
                            MM(ps[nb_][:, :], oT[:, kc, :], WM[:, kc, nb_ * 512:(nb_ + 1) * 512], kc == 0, kc == 3, ["oT", "WM"], [psr(nb_)])
                        TT("dve", xa[:, nb_ * 512:(nb_ + 1) * 512], xa[:, nb_ * 512:(nb_ + 1) * 512], ps[nb_][:, :], ALU.add, [xkey, psr(nb_)], [xkey])
                    if not is_s:
                        DMA("sp", dst_h[i * 128:(i + 1) * 128, :], xa, [xkey], [])

        def sweep_ffn(l, src_h, dst_h, final):
            with contextlib.ExitStack() as ses:
                sb2 = mk_sb(ses)
                P.barrier()
                WI = sb2("WI", [128, 8, 2 * DFF], BF16); WF = sb2("WF", [128, 22, D], BF16)
                load_w(WI, "WI", w_fi[l], 2 * DFF); load_w(WF, "WF", w_fo[l], D)
                hT = sb2("hT", [128, 22, 128], BF16); sil = sb2("sil", [128, 128])
                if final:
                    DMA("sp", gbc[:, 0, :], g_fin.partition_broadcast(128), [], ["gbc0"])
                for i in range(NT + 1):
                    is_s = (i == NT)
                    xa, xkey = load_x(l, i, src_h)
                    rmsnorm_tile(xa, xkey, 2)
                    for c in range(22):
                        ba, bu = (c % 2) * 2, (c % 2) * 2 + 1
                        for kc in range(8):
                            MM(ps[ba][:, 0:128], WI[:, kc, c * 128:(c + 1) * 128], nT[:, kc, :], kc == 0, kc == 7, ["nT", "WI"], [psr(ba)])
                        for kc in range(8):
                            MM(ps[bu][:, 0:128], WI[:, kc, DFF + c * 128:DFF + (c + 1) * 128], nT[:, kc, :], kc == 0, kc == 7, ["nT", "WI"], [psr(bu)])
                        ACTF(sil[:], ps[ba][:, 0:128], AF.Silu, [psr(ba)], ["sil"])
                        TT("dve", hT[:, c, :], sil[:], ps[bu][:, 0:128], ALU.mult, ["sil", psr(bu)], ["hT"])
                    for nb_ in range(2):
                        bk = 4 + nb_
                        for c in range(22):
                            MM(ps[bk][:, :], hT[:, c, :], WF[:, c, nb_ * 512:(nb_ + 1) * 512], c == 0, c == 21, ["hT", "WF"], [psr(bk)])
                        TT("dve", xa[:, nb_ * 512:(nb_ + 1) * 512], xa[:, nb_ * 512:(nb_ + 1) * 512], ps[bk][:, :], ALU.add, [xkey, psr(bk)], [xkey])
                    if final:
                        ACTF(stg[0][:, 0:D], xa, AF.Square, [xkey], ["stg0", "rs"], accum_out=rs[:, 0:1])
                        ACTF(rs[:, 1:2], rs[:, 0:1], AF.Sqrt, ["rs", "epsT"], ["rs"], bias=epsT[:, 0:1], scale=1.0 / D)
                        RECIP(rs[:, 2:3], rs[:, 1:2], ["rs"], ["rs"])
                        STT(stg[1][:, 0:D], xa, rs[:, 2:3], gbc[:, 0, :], ALU.mult, ALU.mult, [xkey, "rs", "gbc0"], ["stg1"])
                        if is_s:
                            DMA("sp", ys[:, :], stg[1][:, 0:D], ["stg1"], [])
                        else:
                            DMA("sp", yp[i * 128:(i + 1) * 128, :], stg[1][:, 0:D], ["stg1"], [])
                    elif not is_s:
                        DMA("sp", dst_h[i * 128:(i + 1) * 128, :], xa, [xkey], [])

        for l in range(nlayers):
            src_h = xp if l == 0 else h_b
            load_gains(l)
            sweep_fox(l, src_h)
            if DEBUG and l == 0:
                DMA("sp", dbg_fos[:, :, :], FOd[:, :, S:S + 128], ["FOd"], [])
            if stop_after == "fox":
                break
            sweep_gdn(l, src_h)
            if DEBUG and l == 0:
                DMA("sp", dbg_gos[:, :, :], GOd[:, :, S:S + 128], ["GOd"], [])
            if stop_after == "gdn":
                break
            sweep_mem(l, src_h, h_a)
            if DEBUG and l == 0:
                DMA("sp", dbg_xs1[:, :], XS[:], ["XS"], [])
            if stop_after == "mem":
                break
            sweep_ffn(l, h_a, h_b, final=(l == DEPTH - 1))
        P.finish()
    return nc

def _host_consts(cs_p):
    c = {}
    c["k_ident"] = np.eye(128, dtype=np.float32)
    for k, v in _consts(cs_p).items():
        c["kp_" + k] = v
    for k, v in _consts(8).items():
        c["ks_" + k] = v
    key = np.arange(128)[:, None, None]; r = np.arange(4)[None, :, None]; q = np.arange(512)[None, None, :]
    c["k_negq"] = np.where(q >= 128 * r + key, 0.0, NEG).astype(np.float32)
    c["k_iota"] = np.arange(128, dtype=np.float32).reshape(128, 1)
    t = np.arange(128)
    c["k_ub128"] = (t[:, None] <= t[None, :]).astype(np.float32)
    c["k_ls128"] = (t[:, None] > t[None, :]).astype(np.float32)
    shc = np.zeros((128, 3, 128), np.float32); shp = np.zeros((128, 3, 128), np.float32); shs = np.zeros((48, 3, 128), np.float32)
    for s in (1, 2, 3):
        for tt in range(128):
            if tt - s >= 0:
                shc[tt - s, s - 1, tt] = 1.0
            else:
                shp[128 + tt - s, s - 1, tt] = 1.0
        for b in range(16):
            for tl in range(8):
                if tl - s < 0:
                    shs[3 * b + 3 + tl - s, s - 1, 8 * b + tl] = 1.0
    c["k_shc"] = shc; c["k_shp"] = shp; c["k_shs"] = shs
    return c


_IN_ORDER = ("x_prompt", "x_sample", "cache_fox_k", "cache_fox_v", "cache_fox_logf", "state_gdn", "state_gdn_conv",
             "cache_mem_k", "cache_mem_v", "page_table", "mem_prompt")


def kernel(**inp):
    f = lambda a: np.ascontiguousarray(np.asarray(a))
    x_prompt = f(inp["x_prompt"]); x_sample = f(inp["x_sample"])
    B, S, _ = x_prompt.shape
    BS, LS, _ = x_sample.shape
    assert B == 4 and BS == 128 and LS == 8
    page_table = f(inp["page_table"]).astype(np.int32)
    NPG = page_table.shape[1]
    ckf = f(inp["cache_fox_k"]); NPHYS = ckf.shape[1]
    ck = ckf.reshape(DEPTH, NPHYS * 128, FW); cv = f(inp["cache_fox_v"]).reshape(DEPTH, NPHYS * 128, FW)
    clf = f(inp["cache_fox_logf"]).reshape(DEPTH, NPHYS * 128, FH)
    sgd = f(inp["state_gdn"]); scv = f(inp["state_gdn_conv"])
    cmk = f(inp["cache_mem_k"]).reshape(DEPTH, BS, NMEM, MW); cmv = f(inp["cache_mem_v"]).reshape(DEPTH, BS, NMEM, MW)
    memp = f(inp["mem_prompt"])
    cs_p = math.gcd(S, 64)
    nc = build_program(S, NPG, NPHYS, cs_p, nlayers=NLAYERS, stop_after=STOP_AFTER)
    consts = _host_consts(cs_p)
    shared = {
        "ck": ck, "cv": cv, "clf": clf,
        "g_mix": f(inp["g_norm_mix"]), "w_in": f(inp["w_in"]), "b_f": f(inp["b_fox_f"]), "conv_w": f(inp["gdn_conv_w"]),
        "a_log": f(inp["gdn_a_log"]), "dt_b": f(inp["gdn_dt_bias"]), "gnw": f(inp["gdn_norm_w"]), "w_out": f(inp["w_out"]),
        "g_memin": f(inp["g_norm_memin"]), "w_mkv": f(inp["w_mem_kv"]), "g_mem": f(inp["g_norm_mem"]),
        "w_mq": f(inp["w_mem_q"]), "w_mo": f(inp["w_mem_o"]), "g_ffn": f(inp["g_norm_ffn"]), "w_fi": f(inp["w_ffn_in"]),
        "w_fo": f(inp["w_ffn_out"]), "g_fin": f(inp["g_final"]),
    }
    shared.update(consts)
    in_maps = []
    for c in range(8):
        b = c // 2
        m = dict(shared)
        m.update({
            "xp": x_prompt[b], "xs": x_sample[16 * c:16 * c + 16].reshape(128, D),
            "sgd": sgd[:, 16 * c:16 * c + 16], "scv": scv[:, 16 * c:16 * c + 16],
            "cmk": cmk[:, 16 * c:16 * c + 16], "cmv": cmv[:, 16 * c:16 * c + 16],
            "ptab": page_table[16 * c:16 * c + 16], "memp": memp[b],
        })
        in_maps.append({k: np.ascontiguousarray(v) for k, v in m.items()})
    res = run_bass_kernel_spmd(nc, in_maps, core_ids=list(range(8)))
    R = res.results
    ev = [R[2 * b] for b in range(4)]
    cat_b = lambda key: np.stack([r[key] for r in ev], axis=0)
    cat_s = lambda key, ax: np.concatenate([r[key] for r in R], axis=ax)
    yp = cat_b("yp")
    ys = cat_s("ys", 0).reshape(BS, LS, D)
    def pl(key, tail):
        return np.stack([r[key] for r in ev], axis=1).reshape((DEPTH, 4) + tail)
    def sl(key, tail):
        return np.concatenate([r[key].reshape((DEPTH, 16) + tail) for r in R], axis=1)
    outs = (yp, ys,
            pl("o_fkp", (S, FH, FD)), pl("o_fvp", (S, FH, FD)), pl("o_flp", (S, FH)),
            pl("o_gsp", (GH, GD, GD)), pl("o_gcp", (3, GC3)), pl("o_mkp", (NMEM, MH, MD)), pl("o_mvp", (NMEM, MH, MD)),
            sl("o_fks", (LS, FH, FD)), sl("o_fvs", (LS, FH, FD)), sl("o_fls", (LS, FH)),
            sl("o_gss", (GH, GD, GD)), sl("o_gcs", (3, GC3)))
    global _LAST
    _LAST = R
    return tuple(np.ascontiguousarray(o.astype(np.float32)) for o in outs)
```

```python
import contextlib
import math
import numpy as np
import concourse.bass as bass
import concourse.mybir as mybir
from concourse.bass import IndirectOffsetOnAxis
from concourse.bass_utils import run_bass_kernel_spmd

F32 = mybir.dt.float32; BF16 = mybir.dt.bfloat16; I32 = mybir.dt.int32
AF = mybir.ActivationFunctionType; ALU = mybir.AluOpType; AX = mybir.AxisListType

D = 1024; DEPTH = 2
FH = 8; FD = 64; FW = 512
GH = 4; GD = 128; GW = 512; GC3 = 1536
NMEM = 256; MH = 4; MD = 128; MW = 512
DFF = 2816
INC = 3600
EPS = 1e-6
NEG = -30000.0
DEBUG = False
NLAYERS = DEPTH
STOP_AFTER = None


class Prog:
    NDMA = 12

    def __init__(self, nc, es):
        self.nc = nc
        self.engs = {"pe": nc.tensor, "act": nc.scalar, "dve": nc.vector, "pool": nc.gpsimd, "sp": nc.sync}
        self.sem = {k: es.enter_context(nc.semaphore("c_" + k)) for k in ("pe", "act", "dve", "pool")}
        self.cnt = {k: 0 for k in self.sem}
        self.dsem = {q: [es.enter_context(nc.semaphore(f"d_{q}{i}")) for i in range(self.NDMA)] for q in ("sp", "pool", "act")}
        self.dval = {q: [0] * self.NDMA for q in self.dsem}
        self.dnext = {q: 0 for q in self.dsem}
        self.known = {k: {} for k in self.engs}
        self.lastw = {}
        self.readers = {}
        self.n = 0

    def _wait(self, eng, tok):
        if tok is None:
            return
        sem, val, src = tok
        if src == "pe" and eng == "pe":
            return
        kn = self.known[eng]
        if kn.get(sem.name, 0) >= val:
            return
        self.engs[eng].wait_ge(sem, val)
        kn[sem.name] = val

    def emit(self, eng, fn, reads=(), writes=(), dma=False):
        for r in reads:
            self._wait(eng, self.lastw.get(r))
        for w in writes:
            self._wait(eng, self.lastw.get(w))
            for t in self.readers.get(w, ()):
                self._wait(eng, t)
        if dma:
            i = self.dnext[eng]; self.dnext[eng] = (i + 1) % self.NDMA
            sem = self.dsem[eng][i]
            prev = self.dval[eng][i]
            if prev:
                self._wait(eng, (sem, prev, "dma"))
            ins = fn(self.engs[eng])
            val = prev + 16
            self.dval[eng][i] = val
            ins.then_inc(sem, 16)
            tok = (sem, val, "dma")
        else:
            ins = fn(self.engs[eng])
            self.cnt[eng] += 1
            ins.then_inc(self.sem[eng], 1)
            tok = (self.sem[eng], self.cnt[eng], eng)
        for w in writes:
            self.lastw[w] = tok; self.readers[w] = []
        for r in reads:
            self.readers.setdefault(r, []).append(tok)
        self.n += 1
        return tok

    def barrier(self):
        toks = [(self.sem[k], self.cnt[k], k) for k in self.sem if self.cnt[k]]
        for q in self.dsem:
            for i, s in enumerate(self.dsem[q]):
                if self.dval[q][i]:
                    toks.append((s, self.dval[q][i], "dma"))
        for eng in self.engs:
            for t in toks:
                if not (t[2] == eng):
                    self._wait(eng, t)

    def finish(self):
        for q in self.dsem:
            for i, sem in enumerate(self.dsem[q]):
                v = self.dval[q][i]
                if v:
                    self._wait("sp", (sem, v, "dma"))


def _consts(cs):
    t = np.arange(128)
    same = (t[:, None] // cs) == (t[None, :] // cs)
    up_incl = same & (t[:, None] <= t[None, :])
    c = {}
    c["negmt"] = np.where(up_incl, 0.0, NEG).astype(np.float32)
    c["posm"] = np.where(up_incl.T, 0.0, -NEG).astype(np.float32)
    c["su"] = (same & (t[:, None] < t[None, :])).astype(np.float32)
    c["sl"] = c["su"].T.copy()
    c["ub"] = up_incl.astype(np.float32)
    last = (t // cs) * cs + cs - 1
    c["bl"] = (t[:, None] == last[None, :]).astype(np.float32)
    nb = 128 // cs
    el = np.zeros((128, nb, 128), np.float32)
    for b in range(nb):
        el[b * cs + cs - 1, b, :] = 1.0
    il = np.zeros((128, nb), np.float32)
    for b in range(nb):
        il[b * cs + cs - 1, b] = 1.0
    c["islast"] = il
    cm = np.zeros((128, nb, 128), np.float32)
    rm = np.zeros((128, nb), np.float32)
    for b in range(nb):
        cm[:, b, b * cs:(b + 1) * cs] = 1.0
        rm[b * cs:(b + 1) * cs, b] = 1.0
    c["colmask"] = cm
    c["rowmask"] = rm
    sh = np.zeros((128, 3, 128), np.float32)
    for s in (1, 2, 3):
        for tt in range(128):
            if tt - s >= 0 and (tt - s) // cs == tt // cs:
                sh[tt - s, s - 1, tt] = 1.0
    c["shc"] = sh
    return c


def build_program(S, NPG, NPHYS, cs_p, nlayers=DEPTH, stop_after=None):
    NT = S // 128
    nc = bass.Bass("TRN2", target_bir_lowering=False)
    es = contextlib.ExitStack()

    def din(name, shape, dt=F32):
        return nc.dram_tensor(name, list(shape), dt, kind="ExternalInput").ap()

    def dout(name, shape, dt=F32):
        return nc.dram_tensor(name, list(shape), dt, kind="ExternalOutput").ap()

    def dscr(name, shape, dt=F32):
        return nc.dram_tensor(name, list(shape), dt, kind="Internal").ap()

    xp = din("xp", [S, D]); xs = din("xs", [128, D])
    ckv = din("ckv", [DEPTH * NPHYS * 128, 2 * FW + FH])
    sgd = din("sgd", [DEPTH, 16, GH, GD, GD]); scv = din("scv", [DEPTH, 16, 3, GC3])
    cmk = din("cmk", [DEPTH, 16, NMEM, MW]); cmv = din("cmv", [DEPTH, 16, NMEM, MW])
    ptab = din("ptab", [16, NPG], I32)
    memp = din("memp", [NMEM, D])
    g_mix = din("g_mix", [DEPTH, D]); w_in = din("w_in", [DEPTH, D, INC]); b_f = din("b_f", [DEPTH, FH])
    conv_w = din("conv_w", [DEPTH, 4, GC3]); a_log = din("a_log", [DEPTH, GH]); dt_b = din("dt_b", [DEPTH, GH])
    gnw = din("gnw", [DEPTH, GD]); w_out = din("w_out", [DEPTH, D, D])
    g_memin = din("g_memin", [DEPTH, D]); w_mkv = din("w_mkv", [DEPTH, D, 2 * MW])
    g_mem = din("g_mem", [DEPTH, D]); w_mq = din("w_mq", [DEPTH, D, MW]); w_mo = din("w_mo", [DEPTH, MW, D])
    g_ffn = din("g_ffn", [DEPTH, D]); w_fi = din("w_fi", [DEPTH, D, 2 * DFF]); w_fo = din("w_fo", [DEPTH, DFF, D])
    g_fin = din("g_fin", [D])
    k_ident = din("k_ident", [128, 128])
    k_p = {k: din("kp_" + k, v.shape) for k, v in _consts(cs_p).items()}
    k_s = {k: din("ks_" + k, v.shape) for k, v in _consts(8).items()}
    k_negq = din("k_negq", [128, 4, 512])
    k_iota = din("k_iota", [128, 1])
    k_ub128 = din("k_ub128", [128, 128]); k_ls128 = din("k_ls128", [128, 128])
    k_shc = din("k_shc", [128, 3, 128]); k_shp = din("k_shp", [128, 3, 128]); k_shs = din("k_shs", [48, 3, 128])

    yp = dout("yp", [S, D]); ys = dout("ys", [128, D])
    o_fkp = dout("o_fkp", [DEPTH, S, FW]); o_fvp = dout("o_fvp", [DEPTH, S, FW]); o_flp = dout("o_flp", [DEPTH, S, FH])
    o_gsp = dout("o_gsp", [DEPTH, GH, GD, GD]); o_gcp = dout("o_gcp", [DEPTH, 3, GC3])
    o_mkp = dout("o_mkp", [DEPTH, NMEM, MW]); o_mvp = dout("o_mvp", [DEPTH, NMEM, MW])
    o_fks = dout("o_fks", [DEPTH, 128, FW]); o_fvs = dout("o_fvs", [DEPTH, 128, FW]); o_fls = dout("o_fls", [DEPTH, 128, FH])
    o_gss = dout("o_gss", [DEPTH, 16, GH, GD, GD]); o_gcs = dout("o_gcs", [DEPTH, 16, 3, GC3])

    if DEBUG:
        dbg_fos = dout("dbg_fos", [128, 4, 128], BF16); dbg_gos = dout("dbg_gos", [128, 4, 128], BF16); dbg_xs1 = dout("dbg_xs1", [128, D])
    h_a = dscr("h_a", [S, D]); h_b = dscr("h_b", [S, D])
    FOd = dscr("FOd", [128, 4, S + 128], BF16); GOd = dscr("GOd", [128, 4, S + 128], BF16)

    with es:
        P = Prog(nc, es)
        uid = [0]

        def mk_sb(stack):
            def f(name, shape, dt=F32):
                uid[0] += 1
                return stack.enter_context(nc.sbuf_tensor(f"{name}_{uid[0]}", list(shape), dt))
            return f

        sb = mk_sb(es)

        def MM(out, lhsT, rhs, start, stop, r, w):
            P.emit("pe", lambda e: e.matmul(out, lhsT, rhs, start=start, stop=stop), reads=r, writes=w)

        def TR(out, in_, idn, r, w):
            P.emit("pe", lambda e: e.transpose(out, in_, idn), reads=r, writes=w)

        def ACTF(out, in_, func, r, w, **kw):
            P.emit("act", lambda e: e.activation(out=out, in_=in_, func=func, **kw), reads=r, writes=w)

        def CP(eng, out, in_, r, w):
            if eng == "act":
                P.emit(eng, lambda e: e.copy(out=out, in_=in_), reads=r, writes=w)
            else:
                P.emit(eng, lambda e: e.tensor_copy(out=out, in_=in_), reads=r, writes=w)

        def TT(eng, out, in0, in1, op, r, w):
            P.emit(eng, lambda e: e.tensor_tensor(out=out, in0=in0, in1=in1, op=op), reads=r, writes=w)

        def TS(eng, out, in0, s1, s2, op0, op1, r, w):
            if s2 is None:
                P.emit(eng, lambda e: e.tensor_scalar(out=out, in0=in0, scalar1=s1, scalar2=None, op0=op0), reads=r, writes=w)
            else:
                P.emit(eng, lambda e: e.tensor_scalar(out=out, in0=in0, scalar1=s1, scalar2=s2, op0=op0, op1=op1), reads=r, writes=w)

        def STT(out, in0, scalar, in1, op0, op1, r, w):
            P.emit("dve", lambda e: e.scalar_tensor_tensor(out=out, in0=in0, scalar=scalar, in1=in1, op0=op0, op1=op1), reads=r, writes=w)

        def MSET(eng, ap, val, w):
            P.emit(eng, lambda e: e.memset(ap, val), writes=w)

        def DMA(q, out, in_, r, w):
            P.emit(q, lambda e: e.dma_start(out=out, in_=in_), reads=r, writes=w, dma=True)

        def GATHER(out, table, idx_ap, r, w):
            P.emit("pool", lambda e: e.indirect_dma_start(out=out, out_offset=None, in_=table,
                                                          in_offset=IndirectOffsetOnAxis(ap=idx_ap, axis=0)), reads=r, writes=w, dma=True)

        def RECIP(out, in_, r, w):
            P.emit("dve", lambda e: e.reciprocal(out=out, in_=in_), reads=r, writes=w)

        def REDUCE(out, in_, r, w):
            P.emit("dve", lambda e: e.tensor_reduce(out=out, in_=in_, axis=AX.X, op=ALU.add), reads=r, writes=w)

        ident = sb("ident", [128, 128]); DMA("sp", ident[:], k_ident[:, :], [], ["ident"])
        identb = sb("identb", [128, 128], BF16); CP("dve", identb[:], ident[:], ["ident"], ["identb"])
        ones = sb("ones", [128, 128]); MSET("dve", ones[:], 1.0, ["ones"])
        onesb = sb("onesb", [128, 128], BF16); MSET("dve", onesb[:], 1.0, ["onesb"])
        epsT = sb("epsT", [128, 1]); MSET("dve", epsT[:], EPS, ["epsT"])
        gbc = sb("gbc", [128, 4, D])
        xt = [sb(f"xt{i}", [128, D]) for i in range(2)]
        nb16 = sb("nb16", [128, D], BF16)
        nT = sb("nT", [128, 8, 128], BF16)
        rs = sb("rs", [128, 8])
        XS = sb("XS", [128, D])
        stg = [sb(f"stg{i}", [128, 1032]) for i in range(2)]
        ps = [es.enter_context(nc.psum_tensor(f"ps{i}", [128, 512], F32)) for i in range(8)]
        psb = ps[7].bitcast(BF16)
        DMA("sp", XS[:], xs[:, :], [], ["XS"])

        def psr(i):
            return f"ps{i}"

        def load_gains(l):
            for j, src in enumerate((g_mix[l], g_mem[l], g_ffn[l], g_memin[l])):
                DMA("sp", gbc[:, j, :], src.partition_broadcast(128), [], [f"gbc{j}"])

        def load_w(dst, dst_key, src_ap, ncols, col0=0):
            K = src_ap.shape[0]
            for kc in range(K // 128):
                P.emit("pool", lambda e, kc=kc: e.dma_start(out=dst[:, kc, col0:col0 + ncols], in_=src_ap[kc * 128:(kc + 1) * 128, :]),
                       writes=[dst_key], dma=True)

        def rmsnorm_tile(x_ap, xkey, gidx, dstT=None, dkey="nT", toff=0):
            dstT = nT if dstT is None else dstT
            ACTF(stg[0][:, 0:D], x_ap, AF.Square, [xkey], ["stg0", "rs"], accum_out=rs[:, 0:1])
            ACTF(rs[:, 1:2], rs[:, 0:1], AF.Sqrt, ["rs", "epsT"], ["rs"], bias=epsT[:, 0:1], scale=1.0 / D)
            RECIP(rs[:, 2:3], rs[:, 1:2], ["rs"], ["rs"])
            STT(nb16[:], x_ap, rs[:, 2:3], gbc[:, gidx, :], ALU.mult, ALU.mult, [xkey, "rs", f"gbc{gidx}"], ["nb16"])
            for kc in range(8):
                TR(psb[:, kc * 128:(kc + 1) * 128], nb16[:, kc * 128:(kc + 1) * 128], identb[:], ["nb16", "identb"], [psr(7)])
            CP("dve", dstT[:, :, toff:toff + 128], psb[:, 0:1024].rearrange("p (k t) -> p k t", k=8), [psr(7)], [dkey])

        def load_x(l, i, src_h):
            if i == NT:
                return XS[:], "XS"
            xb_ = xt[i % 2]; xkey = f"xt{i % 2}"
            DMA("sp", xb_[:], src_h[i * 128:(i + 1) * 128, :], [], [xkey])
            return xb_[:], xkey

        def sweep_fox(l, src_h):
            with contextlib.ExitStack() as ses:
                sb2 = mk_sb(ses)
                P.barrier()
                WA = sb2("WAf", [128, 8, 1544], BF16)
                KT = sb2("KT", [128, 4, S], BF16); VR = sb2("VR", [128, NT, FH, 65], BF16); rrow = sb2("rrow", [65, 512])
                QT = sb2("QT", [128, 4, 512], BF16); FOg = sb2("FOg", [128, 4, 512], BF16)
                LOGF = sb2("LOGF", [128, 8]); CC = sb2("CC", [128, NT + 1, FH]); TOT = sb2("TOT", [128, NT + 2, FH])
                BIAS = sb2("BIAS", [128, NT, FH]); bfb = sb2("bfb", [128, FH])
                PT = [sb2(f"PT{i}", [128, 512], BF16) for i in range(3)]
                rl = sb2("rl", [64, 512])
                negq = sb2("negq", [128, 4, 512], BF16)
                ub128 = sb2("ub128", [128, 128]); ls128 = sb2("ls128", [128, 128]); ub8 = sb2("ub8", [128, 128])
                negm8 = sb2("negm8", [128, 128], BF16); iota_p = sb2("iota_p", [128, 1])
                P.emit("pool", lambda e: e.dma_start(out=negq[:], in_=k_negq[:, :, :]), writes=["negq"], dma=True)
                P.emit("pool", lambda e: e.dma_start(out=negm8[:], in_=k_s["negmt"]), writes=["negm8"], dma=True)
                DMA("sp", ub128[:], k_ub128[:, :], [], ["ub128"]); DMA("sp", ls128[:], k_ls128[:, :], [], ["ls128"])
                DMA("sp", ub8[:], k_s["ub"], [], ["ub8"]); DMA("sp", iota_p[:], k_iota[:, :], [], ["iota_p"])
                DMA("sp", bfb[:], b_f[l].partition_broadcast(128), [], ["bfb"])
                load_w(WA, "WA", w_in[l, :, 0:1544], 1544)
                MSET("dve", TOT[:, 0, :], 0.0, ["TOT"])
                MSET("pool", VR[:], 1.0, ["VR"])
                QTs = sb2("QTs", [128, 4, 128], BF16); KTs = sb2("KTs", [128, 4, 128], BF16)
                VAs = sb2("VAs", [128, 8, 65], BF16)
                KV = [sb2(f"KV{i}", [128, 1032]) for i in range(2)]
                Kb = sb2("Kb", [128, 512], BF16); KpT = sb2("KpT", [128, 4, 128], BF16)
                VA = [sb2(f"VA{i}", [128, 8, 65], BF16) for i in range(2)]
                PTp = [sb2(f"PTp{i}", [128, 8, 128], BF16) for i in range(2)]
                SUF = sb2("SUF", [128, 8]); TOTL = sb2("TOTL", [128, 8]); sc64 = sb2("sc64", [128, 64])
                PTI = sb2("PTI", [128, 16 * NPG], I32); PTF = sb2("PTF", [128, 16 * NPG]); IDX = sb2("IDX", [128, 16 * NPG], I32)
                fos = sb2("fos", [128, 512]); fosb = sb2("fosb", [128, 512], BF16); FOs = sb2("FOs", [128, 4, 128], BF16)
                rden = sb2("rden", [128, 8])

                def fox_group(j):
                    nk = 4 * j + 4
                    for kt in range(nk):
                        STT(BIAS[:, kt, :], CC[:, kt, :], -1.0, TOT[:, 4 * j + 2, :], ALU.mult, ALU.add, ["CC", "TOT"], ["BIAS"])
                    steps = [(h, kt) for h in range(FH) for kt in range(nk)]

                    def qk(s):
                        h, kt = steps[s]
                        pair, base = h // 2, 64 * (h % 2)
                        si = s % 2
                        diag = kt >= 4 * j
                        MM(ps[si][:, :], KT[base:base + 64, pair, kt * 128:(kt + 1) * 128], QT[base:base + 64, pair, :], True, not diag,
                           ["KT", "QT"], [psr(si)])
                        if diag:
                            MM(ps[si][:, :], identb[:], negq[:, kt - 4 * j, :], False, True, ["identb", "negq"], [psr(si)])

                    def tail(h):
                        pair, base = h // 2, 64 * (h % 2)
                        bo_, bb_ = (2, 3) if h % 2 == 0 else (4, 5)
                        RECIP(rrow[64:65, :], ps[bo_][64:65, :], [psr(bo_)], ["rrow"])
                        MM(ps[bb_][0:64, :], ones[64:65, 0:64], rrow[64:65, :], True, True, ["ones", "rrow"], [psr(bb_)])
                        CP("act", rl[:], ps[bb_][0:64, :], [psr(bb_)], ["rl"])
                        TT("dve", FOg[base:base + 64, pair, :], ps[bo_][0:64, :], rl[:], ALU.mult, [psr(bo_), "rl"], ["FOg"])

                    qk(0)
                    for s, (h, kt) in enumerate(steps):
                        if s + 1 < len(steps):
                            qk(s + 1)
                        pk = f"PT{s % 3}"; pt = PT[s % 3]
                        ACTF(pt[:], ps[s % 2][:, :], AF.Exp, [psr(s % 2), "BIAS"], [pk], bias=BIAS[:, kt, h:h + 1])
                        bo_ = 2 if h % 2 == 0 else 4
                        MM(ps[bo_][0:65, :], VR[:, kt, h, :], pt[:], kt == 0, kt == nk - 1, ["VR", pk], [psr(bo_)])
                        if h > 0 and kt == min(1, nk - 1):
                            tail(h - 1)
                    tail(FH - 1)
                    DMA("sp", FOd[:, :, j * 512:(j + 1) * 512], FOg[:], ["FOg"], ["FOd"])

                def fox_sample(st, sk):
                    DMA("sp", PTI[:], ptab.rearrange("b p -> (b p)").partition_broadcast(128), [], ["PTI"])
                    CP("dve", PTF[:], PTI[:], ["PTI"], ["PTF"])
                    TS("dve", PTF[:], PTF[:], 128.0, iota_p[:, 0:1], ALU.mult, ALU.add, ["PTF", "iota_p"], ["PTF"])
                    TS("dve", PTF[:], PTF[:], float(l * NPHYS * 128), None, ALU.add, None, ["PTF"], ["PTF"])
                    CP("dve", IDX[:], PTF[:], ["PTF"], ["IDX"])
                    MSET("dve", VAs[:], 1.0, ["VAs"])
                    CP("dve", VAs[:, :, 0:64], st[:, 512:1024].rearrange("p (h d) -> p h d", h=8), [sk], ["VAs"])
                    for u in range(2):
                        MSET("pool", VA[u][:], 1.0, [f"VA{u}"])
                        MSET("pool", PTp[u][:], 0.0, [f"PTp{u}"])
                    A_, B_ = 5, 6
                    first = True
                    cnt_pg = 0
                    for b in range(16):
                        MSET("dve", TOTL[:], 0.0, ["TOTL"])
                        for pg in reversed(range(NPG)):
                            u = cnt_pg % 2; cnt_pg += 1
                            col = b * NPG + pg
                            GATHER(KV[u][:], ckv, IDX[:, col:col + 1], ["IDX"], [f"KV{u}"])
                            CP("dve", Kb[:], KV[u][:, 0:512], [f"KV{u}"], ["Kb"])
                            for pr in range(4):
                                TR(psb[:, pr * 128:(pr + 1) * 128], Kb[:, pr * 128:(pr + 1) * 128], identb[:], ["Kb", "identb"], [psr(7)])
                            CP("act", KpT[:], psb[:, 0:512].rearrange("p (k t) -> p k t", k=4), [psr(7)], ["KpT"])
                            for h in range(FH):
                                pair, base = h // 2, 64 * (h % 2)
                                MM(ps[0][:, h * 8:(h + 1) * 8], KpT[base:base + 64, pair, :], QTs[base:base + 64, pair, 8 * b:8 * b + 8], True, True,
                                   ["KpT", "QTs"], [psr(0)])
                            MM(ps[1][:, 0:8], ls128[:], KV[u][:, 1024:1032], True, True, ["ls128", f"KV{u}"], [psr(1)])
                            MM(ps[1][:, 8:16], ones[:], KV[u][:, 1024:1032], True, True, ["ones", f"KV{u}"], [psr(1)])
                            TT("dve", SUF[:], ps[1][:, 0:8], TOTL[:], ALU.add, [psr(1), "TOTL"], ["SUF"])
                            TT("dve", TOTL[:], ps[1][:, 8:16], TOTL[:], ALU.add, [psr(1), "TOTL"], ["TOTL"])
                            TT("dve", sc64[:].rearrange("p (h q) -> p h q", h=8), ps[0][:, 0:64].rearrange("p (h q) -> p h q", h=8),
                               SUF[:].unsqueeze(2).to_broadcast([128, 8, 8]), ALU.add, [psr(0), "SUF"], ["sc64"])
                            ACTF(PTp[u][:, :, 8 * b:8 * b + 8], sc64[:].rearrange("p (h q) -> p h q", h=8), AF.Exp, ["sc64"], [f"PTp{u}"])
                            CP("act", VA[u][:, :, 0:64], KV[u][:, 512:1024].rearrange("p (h d) -> p h d", h=8), [f"KV{u}"], [f"VA{u}"])
                            for h in range(FH):
                                bk = A_ if h < 4 else B_
                                MM(ps[bk][:, (h % 4) * 65:(h % 4 + 1) * 65], PTp[u][:, h, :], VA[u][:, h, :], first and h % 4 == 0, False,
                                   [f"PTp{u}", f"VA{u}"], [psr(bk)])
                            first = False
                        for u in range(2):
                            MSET("pool", PTp[u][:, :, 8 * b:8 * b + 8], 0.0, [f"PTp{u}"])
                    MM(ps[1][:, 0:8], ub8[:], st[:, 1024:1032], True, True, ["ub8", sk], [psr(1)])
                    TS("dve", SUF[:], ps[1][:, 0:8], -1.0, None, ALU.mult, None, [psr(1)], ["SUF"])
                    for h in range(FH):
                        pair, base = h // 2, 64 * (h % 2)
                        si = h % 2
                        MM(ps[si][:, 0:128], KTs[base:base + 64, pair, :], QTs[base:base + 64, pair, :], True, False, ["KTs", "QTs"], [psr(si)])
                        MM(ps[si][:, 0:128], identb[:], negm8[:], False, True, ["identb", "negm8"], [psr(si)])
                        pk = f"PT{h % 3}"; pt = PT[h % 3]
                        ACTF(pt[:, 0:128], ps[si][:, 0:128], AF.Exp, [psr(si), "SUF"], [pk], bias=SUF[:, h:h + 1])
                        bk = A_ if h < 4 else B_
                        MM(ps[bk][:, (h % 4) * 65:(h % 4 + 1) * 65], pt[:, 0:128], VAs[:, h, :], first, True, [pk, "VAs"], [psr(bk)])
                    for half, bk in ((0, A_), (1, B_)):
                        v = ps[bk][:, 0:260].rearrange("p (h e) -> p h e", h=4)
                        CP("dve", rden[:, half * 4:(half + 1) * 4], v[:, :, 64], [psr(bk)], ["rden"])
                    RECIP(rden[:], rden[:], ["rden"], ["rden"])
                    for half, bk in ((0, A_), (1, B_)):
                        v = ps[bk][:, 0:260].rearrange("p (h e) -> p h e", h=4)
                        TT("dve", fos[:, half * 256:(half + 1) * 256].rearrange("p (h d) -> p h d", h=4), v[:, :, 0:64],
                           rden[:, half * 4:(half + 1) * 4].unsqueeze(2).to_broadcast([128, 4, 64]), ALU.mult, [psr(bk), "rden"], ["fos"])
                    CP("dve", fosb[:], fos[:], ["fos"], ["fosb"])
                    for pr in range(4):
                        TR(psb[:, pr * 128:(pr + 1) * 128], fosb[:, pr * 128:(pr + 1) * 128], identb[:], ["fosb", "identb"], [psr(7)])
                    CP("dve", FOs[:], psb[:, 0:512].rearrange("p (k t) -> p k t", k=4), [psr(7)], ["FOs"])
                    DMA("sp", FOd[:, :, S:S + 128], FOs[:], ["FOs"], ["FOd"])

                for i in range(NT + 1):
                    is_s = (i == NT)
                    xa, xkey = load_x(l, i, src_h)
                    goff = (i % 4) * 128 if not is_s else 0
                    rmsnorm_tile(xa, xkey, 0)
                    for cb, (c0, cw, pb) in enumerate(((512, 512, 0), (1024, 512, 1), (1536, 8, 2))):
                        for kc in range(8):
                            MM(ps[pb][:, 0:cw], nT[:, kc, :], WA[:, kc, c0:c0 + cw], kc == 0, kc == 7, ["nT", "WA"], [psr(pb)])
                    st = stg[i % 2]; sk = f"stg{i % 2}"
                    CP("act", st[:, 0:512], ps[0][:, 0:512], [psr(0)], [sk])
                    CP("dve", st[:, 512:1024], ps[1][:, 0:512], [psr(1)], [sk])
                    if not is_s:
                        CP("pool", VR[:, i, :, 0:64], st[:, 512:1024].rearrange("p (h d) -> p h d", h=8), [sk], ["VR"])
                    TT("dve", LOGF[:], ps[2][:, 0:8], bfb[:], ALU.add, [psr(2), "bfb"], ["LOGF"])
                    ACTF(LOGF[:], LOGF[:], AF.Exp, ["LOGF"], ["LOGF"], scale=-1.0)
                    ACTF(LOGF[:], LOGF[:], AF.Ln, ["LOGF"], ["LOGF"], bias=1.0)
                    TS("dve", st[:, 1024:1032], LOGF[:], -1.0, None, ALU.mult, None, ["LOGF"], [sk])
                    ok_, ov_, ol_ = (o_fks, o_fvs, o_fls) if is_s else (o_fkp, o_fvp, o_flp)
                    r0 = 0 if is_s else i * 128
                    DMA("sp", ok_[l, r0:r0 + 128, :], st[:, 0:512], [sk], [])
                    DMA("sp", ov_[l, r0:r0 + 128, :], st[:, 512:1024], [sk], [])
                    DMA("sp", ol_[l, r0:r0 + 128, :], st[:, 1024:1032], [sk], [])
                    for cj in range(8):
                        pb = 3 + cj // 4
                        for kc in range(8):
                            MM(ps[pb][:, (cj % 4) * 128:(cj % 4 + 1) * 128], WA[:, kc, cj * 128:(cj + 1) * 128], nT[:, kc, :],
                               kc == 0, kc == 7, ["nT", "WA"], [psr(pb)])
                    qdst, qkey = (QTs[:, :, :], "QTs") if is_s else (QT[:, :, goff:goff + 128], "QT")
                    kdst, kkey = (KTs[:, :, :], "KTs") if is_s else (KT[:, :, i * 128:(i + 1) * 128], "KT")
                    P.emit("act", lambda e, qdst=qdst: e.mul(out=qdst, in_=ps[3][:, 0:512].rearrange("p (k t) -> p k t", k=4), mul=0.125),
                           reads=[psr(3)], writes=[qkey])
                    CP("dve", kdst, ps[4][:, 0:512].rearrange("p (k t) -> p k t", k=4), [psr(4)], [kkey])
                    if not is_s:
                        MM(ps[5][:, 0:8], ub128[:], st[:, 1024:1032], True, True, ["ub128", sk], [psr(5)])
                        MM(ps[5][:, 8:16], ones[:], st[:, 1024:1032], True, True, ["ones", sk], [psr(5)])
                        TT("dve", CC[:, i, :], ps[5][:, 0:8], TOT[:, i, :], ALU.add, [psr(5), "TOT"], ["CC"])
                        TT("dve", TOT[:, i + 1, :], ps[5][:, 8:16], TOT[:, i, :], ALU.add, [psr(5), "TOT"], ["TOT"])
                        if i % 4 == 3:
                            fox_group(i // 4)
                    else:
                        fox_sample(st, sk)

        def sweep_gdn(l, src_h):
            with contextlib.ExitStack() as ses:
                sb2 = mk_sb(ses)
                P.barrier()
                WA = sb2("WAg", [128, 8, 2056], BF16)
                load_w(WA, "WA", w_in[l, :, 1544:3600], 2056)
                KP = {}; KS = {}
                for kd_, ksrc, pre in ((KP, k_p, "kp_"), (KS, k_s, "ks_")):
                    for nm, src in ksrc.items():
                        if nm == "colmask":
                            kd_[nm] = sb2(pre + nm, src.shape, BF16)
                            P.emit("pool", lambda e, d=kd_[nm], s_=src: e.dma_start(out=d[:], in_=s_), writes=[pre + nm], dma=True)
                        else:
                            kd_[nm] = sb2(pre + nm, src.shape); DMA("sp", kd_[nm][:], src, [], [pre + nm])
                shc = sb2("shc", [128, 3, 128]); shp = sb2("shp", [128, 3, 128]); shs = sb2("shs", [48, 3, 128])
                DMA("sp", shc[:], k_shc[:, :, :], [], ["shc"]); DMA("sp", shp[:], k_shp[:, :, :], [], ["shp"]); DMA("sp", shs[:], k_shs[:, :, :], [], ["shs"])
                qkvf = [sb2(f"qkvf{i}", [128, GC3]) for i in range(2)]
                cbuf = sb2("cbuf", [48, GC3]); cwb = sb2("cwb", [128, 4, GC3], BF16)
                actv = sb2("actv", [128, GC3]); alb = sb2("alb", [128, 2 * GH]); gnb = sb2("gnb", [128, GD])
                tq = sb2("tq", [128, 512]); tk = sb2("tk", [128, 512]); tkb = sb2("tkb", [128, 512]); tqd = sb2("tqd", [128, 512])
                trvP = [sb2(f"trv{i}", [128, 512]) for i in range(2)]; trkP = [sb2(f"trk{i}", [128, 512]) for i in range(2)]
                tkdP = [sb2(f"tkd{i}", [128, 512]) for i in range(2)]; tggP = [sb2(f"tgg{i}", [128, 512]) for i in range(2)]
                fkTP = [sb2(f"fkT{i}", [128, 512]) for i in range(2)]; fkbTP = [sb2(f"fkbT{i}", [128, 512]) for i in range(2)]
                fqTP = [sb2(f"fqT{i}", [128, 512]) for i in range(2)]; fqdTP = [sb2(f"fqdT{i}", [128, 512]) for i in range(2)]
                gl = sb2("gl", [128, 16, GH]); gsmP = [sb2(f"gsm{i}", [128, 64]) for i in range(2)]; GTP = [sb2(f"GT{i}", [128, 16, GH]) for i in range(2)]
                decT = sb2("decT", [128, 512]); dec = sb2("dec", [128, 512])
                Mk = [sb2(f"Mk{i}", [128, 512]) for i in range(2)]; MkT = [sb2(f"MkT{i}", [128, 512]) for i in range(2)]
                Pk = [sb2(f"Pk{i}", [128, 512]) for i in range(2)]; qkTm = sb2("qkTm", [128, 512])
                uacc = sb2("uacc", [128, 512]); kcT = sb2("kcT", [128, 512])
                kcTm = sb2("kcTm", [128, 4, 128]); qdTm = sb2("qdTm", [128, 4, 128]); kdm = sb2("kdm", [128, 4, 128])
                SG = sb2("SG", [128, GH, GD]); SBs = [sb2(f"SBs{i}", [128, GD]) for i in range(2)]
                osb = sb2("osb", [128, 512]); gob = sb2("gob", [128, 512], BF16); GOt = sb2("GOt", [128, 4, 128], BF16)
                DMA("sp", alb[:, 0:4], a_log[l].partition_broadcast(128), [], ["alb"])
                DMA("sp", alb[:, 4:8], dt_b[l].partition_broadcast(128), [], ["alb"])
                ACTF(alb[:, 0:4], alb[:, 0:4], AF.Exp, ["alb"], ["alb"])
                TS("dve", alb[:, 0:4], alb[:, 0:4], -1.0, None, ALU.mult, None, ["alb"], ["alb"])
                DMA("sp", gnb[:], gnw[l].partition_broadcast(128), [], ["gnb"])
                P.emit("pool", lambda e: e.dma_start(out=cwb[:].rearrange("p w c -> p (w c)"), in_=conv_w[l].rearrange("w c -> (w c)").partition_broadcast(128)),
                       writes=["cwb"], dma=True)
                DMA("sp", cbuf[:], scv[l].rearrange("b r c -> (b r) c"), [], ["cbuf"])
                MSET("dve", SG[:], 0.0, ["SG"])
                bank = [0]

                def nbk():
                    bank[0] = (bank[0] + 1) % 6
                    return bank[0]

                hs = lambda t, h: t[:, h * 128:(h + 1) * 128]
                v4 = lambda t: t[:, :].rearrange("p (h d) -> p h d", h=4)

                def bc(ap4):
                    return ap4.unsqueeze(2).to_broadcast([128, 4, 128])

                def genF(i, is_s, src_h):
                    q = i % 2
                    K_ = KS if is_s else KP
                    kk = "ks_" if is_s else "kp_"
                    cs = 8 if is_s else cs_p
                    nb = 128 // cs
                    gsm = gsmP[q]; gk = f"gsm{q}"
                    xa, xkey = load_x(l, i, src_h)
                    rmsnorm_tile(xa, xkey, 0)
                    yield
                    qv = qkvf[i % 2]; qk_ = f"qkvf{i % 2}"
                    for j in range(3):
                        bq = nbk()
                        for kc in range(8):
                            MM(ps[bq][:, :], nT[:, kc, :], WA[:, kc, j * 512:(j + 1) * 512], kc == 0, kc == 7, ["nT", "WA"], [psr(bq)])
                        CP("act" if j == 1 else "dve", qv[:, j * 512:(j + 1) * 512], ps[bq][:, :], [psr(bq)], [qk_])
                        yield
                    b_g = nbk()
                    for kc in range(8):
                        MM(ps[b_g][:, 0:8], nT[:, kc, :], WA[:, kc, 1536:1544], kc == 0, kc == 7, ["nT", "WA"], [psr(b_g)])
                    b_gg = nbk()
                    for kc in range(8):
                        MM(ps[b_gg][:, :], nT[:, kc, :], WA[:, kc, 1544:2056], kc == 0, kc == 7, ["nT", "WA"], [psr(b_gg)])
                    ACTF(tggP[q][:], ps[b_gg][:, :], AF.Silu, [psr(b_gg)], [f"tgg{q}"])
                    ACTF(gsm[:, 0:4], ps[b_g][:, 4:8], AF.Sigmoid, [psr(b_g)], [gk])
                    TT("dve", gsm[:, 4:8], ps[b_g][:, 0:4], alb[:, 4:8], ALU.add, [psr(b_g), "alb"], [gk])
                    yield
                    ACTF(gsm[:, 4:8], gsm[:, 4:8], AF.Exp, [gk], [gk])
                    ACTF(gsm[:, 4:8], gsm[:, 4:8], AF.Ln, [gk], [gk], bias=1.0)
                    TT("dve", gsm[:, 4:8], gsm[:, 4:8], alb[:, 0:4], ALU.mult, [gk, "alb"], [gk])
                    TT("pool", actv[:], qv[:], cwb[:, 3, :], ALU.mult, [qk_, "cwb"], ["actv"])
                    yield
                    for s in (1, 2, 3):
                        for j in range(3):
                            bs_ = nbk()
                            cs_ = slice(j * 512, (j + 1) * 512)
                            if is_s:
                                MM(ps[bs_][:, :], K_["shc"][:, s - 1, :], qv[:, cs_], True, False, [kk + "shc", qk_], [psr(bs_)])
                                MM(ps[bs_][:, :], shs[:, s - 1, :], cbuf[:, cs_], False, True, ["shs", "cbuf"], [psr(bs_)])
                            else:
                                first = (i == 0)
                                MM(ps[bs_][:, :], shc[:, s - 1, :], qv[:, cs_], True, first, ["shc", qk_], [psr(bs_)])
                                if not first:
                                    MM(ps[bs_][:, :], shp[:, s - 1, :], qkvf[(i - 1) % 2][:, cs_], False, True, ["shp", f"qkvf{(i - 1) % 2}"], [psr(bs_)])
                            TT("dve", stg[0][:, 0:512], ps[bs_][:, :], cwb[:, 3 - s, cs_], ALU.mult, [psr(bs_), "cwb"], ["stg0"])
                            TT("pool", actv[:, cs_], actv[:, cs_], stg[0][:, 0:512], ALU.add, ["actv", "stg0"], ["actv"])
                            yield
                    if is_s:
                        for b in range(16):
                            DMA("sp", o_gcs[l, b, :, :], qv[8 * b + 5:8 * b + 8, :], [qk_], [])
                    elif i == NT - 1:
                        DMA("sp", o_gcp[l, :, :], qv[125:128, :], [qk_], [])
                    ACTF(actv[:], actv[:], AF.Silu, ["actv"], ["actv"])
                    yield
                    TT("dve", stg[1][:, 0:1024], actv[:, 0:1024], actv[:, 0:1024], ALU.mult, ["actv"], ["stg1"])
                    REDUCE(gsm[:, 8:16], stg[1][:, 0:1024].rearrange("p (h d) -> p h d", h=8), ["stg1"], [gk])
                    ACTF(gsm[:, 8:16], gsm[:, 8:16], AF.Sqrt, [gk, "epsT"], [gk], bias=epsT[:, 0:1])
                    RECIP(gsm[:, 8:16], gsm[:, 8:16], [gk], [gk])
                    TS("dve", gsm[:, 8:12], gsm[:, 8:12], GD ** -0.5, None, ALU.mult, None, [gk], [gk])
                    yield
                    a3 = actv[:, 0:512].rearrange("p (h d) -> p h d", h=4)
                    k3 = actv[:, 512:1024].rearrange("p (h d) -> p h d", h=4)
                    v3 = actv[:, 1024:1536].rearrange("p (h d) -> p h d", h=4)
                    TT("dve", v4(tq), a3, bc(gsm[:, 8:12]), ALU.mult, ["actv", gk], ["tq"])
                    TT("dve", v4(tk), k3, bc(gsm[:, 12:16]), ALU.mult, ["actv", gk], ["tk"])
                    yield
                    b1 = nbk()
                    MM(ps[b1][:, 0:4], K_["ub"][:], gsm[:, 4:8], True, True, [kk + "ub", gk], [psr(b1)])
                    CP("dve", gsm[:, 16:20], ps[b1][:, 0:4], [psr(b1)], [gk])
                    MM(ps[b1][:, 8:12], K_["bl"][:], gsm[:, 16:20], True, True, [kk + "bl", gk], [psr(b1)])
                    TT("dve", gl[:, 0:nb, :], gsm[:, 16:20].unsqueeze(1).to_broadcast([128, nb, 4]),
                       K_["islast"][:, 0:nb].unsqueeze(2).to_broadcast([128, nb, 4]), ALU.mult, [gk, kk + "islast"], ["gl"])
                    MM(ps[b1][:, 16:16 + 4 * nb], ones[:], gl[:, 0:nb, :].rearrange("p b h -> p (b h)"), True, True, ["ones", "gl"], [psr(b1)])
                    ACTF(gsm[:, 24:28], gsm[:, 16:20], AF.Exp, [gk], [gk])
                    TT("dve", gsm[:, 28:32], ps[b1][:, 8:12], gsm[:, 16:20], ALU.subtract, [psr(b1), gk], [gk])
                    ACTF(GTP[q][:, 0:nb, :], ps[b1][:, 16:16 + 4 * nb].rearrange("p (b h) -> p b h", h=4), AF.Exp, [psr(b1)], [f"GT{q}"])
                    yield
                    ACTF(gsm[:, 28:32], gsm[:, 28:32], AF.Exp, [gk], [gk])
                    TS("dve", gsm[:, 32:36], gsm[:, 16:20], -1.0, None, ALU.mult, None, [gk], [gk])
                    TT("dve", v4(tkb), v4(tk), bc(gsm[:, 0:4]), ALU.mult, ["tk", gk], ["tkb"])
                    TT("pool", v4(trvP[q]), v3, bc(gsm[:, 0:4]), ALU.mult, ["actv", gk], [f"trv{q}"])
                    yield
                    TT("dve", v4(trkP[q]), v4(tkb), bc(gsm[:, 24:28]), ALU.mult, ["tkb", gk], [f"trk{q}"])
                    TT("pool", v4(tqd), v4(tq), bc(gsm[:, 24:28]), ALU.mult, ["tq", gk], ["tqd"])
                    TT("dve", v4(tkdP[q]), v4(tk), bc(gsm[:, 28:32]), ALU.mult, ["tk", gk], [f"tkd{q}"])
                    yield
                    for src_, sk_, dst_, dk_ in ((tk, "tk", fkTP[q], f"fkT{q}"), (tkb, "tkb", fkbTP[q], f"fkbT{q}"),
                                                 (tq, "tq", fqTP[q], f"fqT{q}"), (tqd, "tqd", fqdTP[q], f"fqdT{q}")):
                        bt = nbk()
                        for h in range(4):
                            TR(hs(ps[bt], h), hs(src_, h), ident[:], [sk_, "ident"], [psr(bt)])
                        CP("act", dst_[:], ps[bt][:, :], [psr(bt)], [dk_])
                        yield

                def genC(i, is_s):
                    q = i % 2
                    K_ = KS if is_s else KP
                    kk = "ks_" if is_s else "kp_"
                    cs = 8 if is_s else cs_p
                    nb = 128 // cs
                    L = int(math.log2(cs)) - 1
                    gsm = gsmP[q]; gk = f"gsm{q}"
                    fkT, fkbT, fqT, fqdT = fkTP[q], fkbTP[q], fqTP[q], fqdTP[q]
                    kfk, kfkb, kfq, kfqd = f"fkT{q}", f"fkbT{q}", f"fqT{q}", f"fqdT{q}"
                    trv, trk, tkd, tgg, GT = trvP[q], trkP[q], tkdP[q], tggP[q], GTP[q]
                    ktrv, ktrk, ktkd, ktgg, kGT = f"trv{q}", f"trk{q}", f"tkd{q}", f"tgg{q}", f"GT{q}"
                    for h in range(4):
                        TS("dve", hs(osb, h), ident[:], gsm[:, 16 + h:17 + h], None, ALU.mult, None, ["ident", gk], ["osb"])
                    bd, be = nbk(), nbk()
                    for h in range(4):
                        MM(hs(ps[bd], h), ones[:], hs(osb, h), True, False, ["ones", "osb"], [psr(bd)])
                        MM(hs(ps[bd], h), ident[:], K_["negmt"][:], False, True, ["ident", kk + "negmt"], [psr(bd)])
                        MM(hs(ps[be], h), ones[:], hs(osb, h), True, False, ["ones", "osb"], [psr(be)])
                        MM(hs(ps[be], h), ident[:], K_["posm"][:], False, True, ["ident", kk + "posm"], [psr(be)])
                    for h in range(4):
                        ACTF(hs(decT, h), hs(ps[bd], h), AF.Exp, [psr(bd), gk], ["decT"], bias=gsm[:, 32 + h:33 + h])
                        ACTF(hs(dec, h), hs(ps[be], h), AF.Exp, [psr(be), gk], ["dec"], bias=gsm[:, 16 + h:17 + h], scale=-1.0)
                    yield
                    ba, bb_, bc_ = nbk(), nbk(), nbk()
                    for h in range(4):
                        MM(hs(ps[ba], h), hs(fkT, h), hs(fkbT, h), True, True, [kfk, kfkb], [psr(ba)])
                        MM(hs(ps[bb_], h), hs(fkbT, h), hs(fkT, h), True, True, [kfk, kfkb], [psr(bb_)])
                        MM(hs(ps[bc_], h), hs(fkT, h), hs(fqT, h), True, True, [kfk, kfq], [psr(bc_)])
                    STT(Mk[0][:], ps[ba][:, :], -1.0, decT[:], ALU.mult, ALU.mult, [psr(ba), "decT"], ["Mk0"])
                    STT(MkT[0][:], ps[bb_][:, :], -1.0, dec[:], ALU.mult, ALU.mult, [psr(bb_), "dec"], ["MkT0"])
                    TT("dve", qkTm[:], ps[bc_][:, :], decT[:], ALU.mult, [psr(bc_), "decT"], ["qkTm"])
                    yield
                    su4 = K_["su"][:].unsqueeze(1).to_broadcast([128, 4, 128]); sl4 = K_["sl"][:].unsqueeze(1).to_broadcast([128, 4, 128])
                    id4 = ident[:].unsqueeze(1).to_broadcast([128, 4, 128])
                    TT("dve", v4(Mk[0]), v4(Mk[0]), su4, ALU.mult, ["Mk0", kk + "su"], ["Mk0"])
                    TT("pool", v4(MkT[0]), v4(MkT[0]), sl4, ALU.mult, ["MkT0", kk + "sl"], ["MkT0"])
                    TT("dve", v4(Pk[0]), v4(Mk[0]), id4, ALU.add, ["Mk0", "ident"], ["Pk0"])
                    yield
                    cur = 0
                    for lev in range(L):
                        nx = 1 - cur
                        last = (lev == L - 1)
                        bm, bmt = nbk(), nbk()
                        for h in range(4):
                            if not last:
                                MM(hs(ps[bm], h), hs(MkT[cur], h), hs(Mk[cur], h), True, True, [f"Mk{cur}", f"MkT{cur}"], [psr(bm)])
                            MM(hs(ps[bmt], h), hs(Mk[cur], h), hs(MkT[cur], h), True, True, [f"Mk{cur}", f"MkT{cur}"], [psr(bmt)])
                        if not last:
                            CP("act", Mk[nx][:], ps[bm][:, :], [psr(bm)], [f"Mk{nx}"])
                        CP("dve", MkT[nx][:], ps[bmt][:, :], [psr(bmt)], [f"MkT{nx}"])
                        yield
                        bp = nbk()
                        for h in range(4):
                            MM(hs(ps[bp], h), hs(MkT[nx], h), hs(Pk[cur], h), True, True, [f"MkT{nx}", f"Pk{cur}"], [psr(bp)])
                        TT("dve", Pk[nx][:], ps[bp][:, :], Pk[cur][:], ALU.add, [psr(bp), f"Pk{cur}"], [f"Pk{nx}"])
                        yield
                        cur = nx
                    TTm = Pk[cur]; ttk = f"Pk{cur}"
                    bu, bk2 = nbk(), nbk()
                    for h in range(4):
                        MM(hs(ps[bu], h), hs(TTm, h), hs(trv, h), True, True, [ttk, ktrv], [psr(bu)])
                        MM(hs(ps[bk2], h), hs(trk, h), hs(TTm, h), True, True, [ttk, ktrk], [psr(bk2)])
                    CP("dve", uacc[:], ps[bu][:, :], [psr(bu)], ["uacc"])
                    CP("act", kcT[:], ps[bk2][:, :], [psr(bk2)], ["kcT"])
                    yield
                    bo = 6
                    cm_ = K_["colmask"]; rm_ = K_["rowmask"]
                    ngrp = (nb + 3) // 4
                    for h in range(4):
                        for g in range(ngrp):
                            b0 = 4 * g; nbg = min(4, nb - b0)
                            TT("dve", kcTm[:, 0:nbg, :], hs(kcT, h).unsqueeze(1).to_broadcast([128, nbg, 128]), cm_[:, b0:b0 + nbg, :], ALU.mult,
                               ["kcT", kk + "colmask"], ["kcTm"])
                            TT("pool", qdTm[:, 0:nbg, :], hs(fqdT, h).unsqueeze(1).to_broadcast([128, nbg, 128]), cm_[:, b0:b0 + nbg, :], ALU.mult,
                               [kfqd, kk + "colmask"], ["qdTm"])
                            TT("pool", kdm[:, 0:nbg, :], hs(tkd, h).unsqueeze(1).to_broadcast([128, nbg, 128]),
                               rm_[:, b0:b0 + nbg].unsqueeze(2).to_broadcast([128, nbg, 128]), ALU.mult, [ktkd, kk + "rowmask"], ["kdm"])
                            yield
                            for jj in range(nbg):
                                b = b0 + jj
                                if is_s:
                                    skey = f"SBs{b % 2}"
                                    S_h = SBs[b % 2][:, :]
                                    DMA("sp", S_h, sgd[l, b, h, :, :], [], [skey])
                                else:
                                    skey = "SG"
                                    S_h = SG[:, h, :]
                                bx = nbk()
                                MM(ps[bx][:, 0:128], kcTm[:, jj, :], S_h, True, True, ["kcTm", skey], [psr(bx)])
                                TT("dve", hs(uacc, h), hs(uacc, h), ps[bx][:, 0:128], ALU.subtract, ["uacc", psr(bx)], ["uacc"])
                                MM(hs(ps[bo], h), qdTm[:, jj, :], S_h, b == 0, False, ["qdTm", skey], [psr(bo)])
                                MM(ps[bx][:, 128:256], kdm[:, jj, :], hs(uacc, h), True, True, ["kdm", "uacc"], [psr(bx)])
                                STT(S_h, S_h, GT[:, b, h:h + 1], ps[bx][:, 128:256], ALU.mult, ALU.add, [skey, kGT, psr(bx)], [skey])
                                if is_s:
                                    DMA("sp", o_gss[l, b, h, :, :], S_h, [skey], [])
                                yield
                        MM(hs(ps[bo], h), hs(qkTm, h), hs(uacc, h), False, True, ["qkTm", "uacc"], [psr(bo)])
                    if (not is_s) and i == NT - 1:
                        DMA("sp", o_gsp[l].rearrange("h d e -> d h e"), SG[:], ["SG"], [])
                    CP("dve", osb[:], ps[bo][:, :], [psr(bo)], ["osb"])
                    yield
                    TT("dve", stg[1][:, 0:512], osb[:], osb[:], ALU.mult, ["osb"], ["stg1"])
                    REDUCE(gsm[:, 40:44], stg[1][:, 0:512].rearrange("p (h d) -> p h d", h=4), ["stg1"], [gk])
                    ACTF(gsm[:, 40:44], gsm[:, 40:44], AF.Sqrt, [gk, "epsT"], [gk], bias=epsT[:, 0:1], scale=1.0 / GD)
                    RECIP(gsm[:, 40:44], gsm[:, 40:44], [gk], [gk])
                    yield
                    TT("dve", v4(osb), v4(osb), bc(gsm[:, 40:44]), ALU.mult, ["osb", gk], ["osb"])
                    TT("dve", v4(osb), v4(osb), gnb[:].unsqueeze(1).to_broadcast([128, 4, 128]), ALU.mult, ["osb", "gnb"], ["osb"])
                    TT("dve", gob[:], osb[:], tgg[:], ALU.mult, ["osb", ktgg], ["gob"])
                    yield
                    for pr in range(4):
                        TR(psb[:, pr * 128:(pr + 1) * 128], gob[:, pr * 128:(pr + 1) * 128], identb[:], ["gob", "identb"], [psr(7)])
                    CP("dve", GOt[:], psb[:, 0:512].rearrange("p (k t) -> p k t", k=4), [psr(7)], ["GOt"])
                    c0 = S if is_s else i * 128
                    DMA("sp", GOd[:, :, c0:c0 + 128], GOt[:], ["GOt"], ["GOd"])
                    yield

                drainF = genF(0, NT == 0, src_h)
                for _ in drainF:
                    pass
                for i in range(NT + 1):
                    gc_ = genC(i, i == NT)
                    gf_ = genF(i + 1, i + 1 == NT, src_h) if i + 1 <= NT else iter(())
                    alive_c = alive_f = True
                    while alive_c or alive_f:
                        if alive_c:
                            try:
                                next(gc_)
                            except StopIteration:
                                alive_c = False
                        if alive_f:
                            try:
                                next(gf_)
                            except StopIteration:
                                alive_f = False

        def sweep_mem(l, src_h, dst_h):
            with contextlib.ExitStack() as ses:
                sb2 = mk_sb(ses)
                P.barrier()
                WO = sb2("WO", [128, 8, D], BF16); WQ = sb2("WQ", [128, 8, MW], BF16); WM = sb2("WM", [128, 4, D], BF16)
                WKV = sb2("WKV", [128, 8, 2 * MW], BF16)
                load_w(WO, "WO", w_out[l], D); load_w(WQ, "WQ", w_mq[l], MW); load_w(WM, "WM", w_mo[l], D); load_w(WKV, "WKV", w_mkv[l], 2 * MW)
                mixT = sb2("mixT", [128, 8, 128], BF16)
                mkT = sb2("mkT", [128, 8, 128], BF16)
                mvA = sb2("mvA", [128, 2, 4, 129], BF16)
                qT = sb2("qT", [128, 4, 128], BF16)
                PTm = sb2("PTm", [128, 8, 128], BF16)
                PTpad = [sb2(f"PTpad{i}", [128, 8, 128], BF16) for i in range(2)]
                Kf2 = sb2("Kf2", [128, 2, MW]); Vf2 = sb2("Vf2", [128, 2, MW]); Kb2 = sb2("Kb2", [128, 2, MW], BF16)
                mkTb = sb2("mkTb", [128, 8, 128], BF16); mvAb = [sb2(f"mvAb{i}", [128, 2, 4, 129], BF16) for i in range(2)]
                osm = sb2("osm", [128, 512]); osmb = sb2("osmb", [128, 512], BF16); oT = sb2("oT", [128, 4, 128], BF16)
                rden = sb2("rdenm", [128, 4]); sc64 = sb2("sc64m", [128, 64])
                MSET("dve", mvA[:], 1.0, ["mvA"])
                for mt in range(2):
                    xb_ = xt[mt]; xkey = f"xt{mt}"
                    DMA("sp", xb_[:], memp[mt * 128:(mt + 1) * 128, :], [], [xkey])
                    rmsnorm_tile(xb_[:], xkey, 3)
                    for nbk_ in range(2):
                        for kc in range(8):
                            MM(ps[nbk_][:, :], nT[:, kc, :], WKV[:, kc, nbk_ * 512:(nbk_ + 1) * 512], kc == 0, kc == 7, ["nT", "WKV"], [psr(nbk_)])
                    st = stg[mt]; sk = f"stg{mt}"
                    CP("act", st[:, 0:512], ps[0][:, :], [psr(0)], [sk])
                    CP("dve", st[:, 512:1024], ps[1][:, :], [psr(1)], [sk])
                    DMA("sp", o_mkp[l, mt * 128:(mt + 1) * 128, :], st[:, 0:512], [sk], [])
                    DMA("sp", o_mvp[l, mt * 128:(mt + 1) * 128, :], st[:, 512:1024], [sk], [])
                    CP("pool", mvA[:, mt, :, 0:128], st[:, 512:1024].rearrange("p (h d) -> p h d", h=4), [sk], ["mvA"])
                    for h in range(4):
                        for kc in range(8):
                            MM(ps[2][:, h * 128:(h + 1) * 128], WKV[:, kc, h * 128:(h + 1) * 128], nT[:, kc, :], kc == 0, kc == 7, ["nT", "WKV"], [psr(2)])
                    CP("dve", mkT[:, mt * 4:(mt + 1) * 4, :], ps[2][:, :].rearrange("p (h t) -> p h t", h=4), [psr(2)], ["mkT"])
                for u in range(2):
                    MSET("pool", mvAb[u][:], 1.0, [f"mvAb{u}"])
                    MSET("pool", PTpad[u][:], 0.0, [f"PTpad{u}"])

                for i in range(NT + 1):
                    is_s = (i == NT)
                    xa, xkey = load_x(l, i, src_h)
                    c0 = S if is_s else i * 128
                    DMA("sp", mixT[:, 0:4, :], FOd[:, :, c0:c0 + 128], ["FOd"], ["mixT"])
                    DMA("sp", mixT[:, 4:8, :], GOd[:, :, c0:c0 + 128], ["GOd"], ["mixT"])
                    for nb_ in range(2):
                        for kc in range(8):
                            MM(ps[nb_][:, :], mixT[:, kc, :], WO[:, kc, nb_ * 512:(nb_ + 1) * 512], kc == 0, kc == 7, ["mixT", "WO"], [psr(nb_)])
                        TT("dve", xa[:, nb_ * 512:(nb_ + 1) * 512], xa[:, nb_ * 512:(nb_ + 1) * 512], ps[nb_][:, :], ALU.add, [xkey, psr(nb_)], [xkey])
                    rmsnorm_tile(xa, xkey, 1)
                    for h in range(4):
                        for kc in range(8):
                            MM(ps[2][:, h * 128:(h + 1) * 128], WQ[:, kc, h * 128:(h + 1) * 128], nT[:, kc, :], kc == 0, kc == 7, ["nT", "WQ"], [psr(2)])
                    P.emit("act", lambda e: e.mul(out=qT[:], in_=ps[2][:, :].rearrange("p (h t) -> p h t", h=4), mul=MD ** -0.5), reads=[psr(2)], writes=["qT"])
                    A_, B_ = 5, 6
                    if not is_s:
                        for mt in range(2):
                            for h in range(4):
                                MM(ps[3 + mt][:, h * 128:(h + 1) * 128], mkT[:, mt * 4 + h, :], qT[:, h, :], True, True, ["mkT", "qT"], [psr(3 + mt)])
                            ACTF(PTm[:, mt * 4:(mt + 1) * 4, :], ps[3 + mt][:, :].rearrange("p (h t) -> p h t", h=4), AF.Exp, [psr(3 + mt)], ["PTm"])
                        for h in range(4):
                            bk = A_ if h < 2 else B_
                            for mt in range(2):
                                MM(ps[bk][:, (h % 2) * 129:(h % 2 + 1) * 129], PTm[:, mt * 4 + h, :], mvA[:, mt, h, :], mt == 0, mt == 1, ["PTm", "mvA"], [psr(bk)])
                    else:
                        for b in range(16):
                            u = b % 2
                            DMA("sp", Kf2[:], cmk[l, b].rearrange("(t p) c -> p t c", p=128), [], ["Kf2"])
                            DMA("sp", Vf2[:], cmv[l, b].rearrange("(t p) c -> p t c", p=128), [], ["Vf2"])
                            CP("dve", Kb2[:], Kf2[:], ["Kf2"], ["Kb2"])
                            for mt in range(2):
                                for h in range(4):
                                    TR(psb[:, (mt * 4 + h) * 128:(mt * 4 + h + 1) * 128], Kb2[:, mt, h * 128:(h + 1) * 128], identb[:], ["Kb2", "identb"], [psr(7)])
                            CP("act", mkTb[:], psb[:, 0:1024].rearrange("p (k t) -> p k t", k=8), [psr(7)], ["mkTb"])
                            CP("pool", mvAb[u][:, :, :, 0:128], Vf2[:].rearrange("p t (h d) -> p t h d", h=4), ["Vf2"], [f"mvAb{u}"])
                            for mt in range(2):
                                for h in range(4):
                                    j = mt * 4 + h
                                    MM(ps[3][:, j * 8:(j + 1) * 8], mkTb[:, j, :], qT[:, h, 8 * b:8 * b + 8], True, True, ["mkTb", "qT"], [psr(3)])
                            ACTF(PTpad[u][:, :, 8 * b:8 * b + 8], ps[3][:, 0:64].rearrange("p (j q) -> p j q", j=8), AF.Exp, [psr(3)], [f"PTpad{u}"])
                            for h in range(4):
                                bk = A_ if h < 2 else B_
                                for mt in range(2):
                                    MM(ps[bk][:, (h % 2) * 129:(h % 2 + 1) * 129], PTpad[u][:, mt * 4 + h, :], mvAb[u][:, mt, h, :],
                                       b == 0 and mt == 0 and h % 2 == 0, b == 15 and mt == 1, [f"PTpad{u}", f"mvAb{u}"], [psr(bk)])
                            MSET("pool", PTpad[u][:, :, 8 * b:8 * b + 8], 0.0, [f"PTpad{u}"])
                    for half, bk in ((0, A_), (1, B_)):
                        v = ps[bk][:, 0:258].rearrange("p (h e) -> p h e", h=2)
                        CP("dve", rden[:, half * 2:(half + 1) * 2], v[:, :, 128], [psr(bk)], ["rden"])
                    RECIP(rden[:], rden[:], ["rden"], ["rden"])
                    for half, bk in ((0, A_), (1, B_)):
                        v = ps[bk][:, 0:258].rearrange("p (h e) -> p h e", h=2)
                        TT("dve", osm[:, half * 256:(half + 1) * 256].rearrange("p (h d) -> p h d", h=2), v[:, :, 0:128],
                           rden[:, half * 2:(half + 1) * 2].unsqueeze(2).to_broadcast([128, 2, 128]), ALU.mult, [psr(bk), "rden"], ["osm"])
                    CP("dve", osmb[:], osm[:], ["osm"], ["osmb"])
                    for pr in range(4):
                        TR(psb[:, pr * 128:(pr + 1) * 128], osmb[:, pr * 128:(pr + 1) * 128], identb[:], ["osmb", "identb"], [psr(7)])
                    CP("dve", oT[:], psb[:, 0:512].rearrange("p (k t) -> p k t", k=4), [psr(7)], ["oT"])
                    for nb_ in range(2):
                        for kc in range(4):
                            MM(ps[nb_][:, :], oT[:, kc, :], WM[:, kc, nb_ * 512:(nb_ + 1) * 512], kc == 0, kc == 3, ["oT", "WM"], [psr(nb_)])
                        TT("dve", xa[:, nb_ * 512:(nb_ + 1) * 512], xa[:, nb_ * 512:(nb_ + 1) * 512], ps[nb_][:, :], ALU.add, [xkey, psr(nb_)], [xkey])
                    if not is_s:
                        DMA("sp", dst_h[i * 128:(i + 1) * 128, :], xa, [xkey], [])

        def sweep_ffn(l, src_h, dst_h, final):
            with contextlib.ExitStack() as ses:
                sb2 = mk_sb(ses)
                P.barrier()
                WI = sb2("WI", [128, 8, 2 * DFF], BF16); WF = sb2("WF", [128, 22, D], BF16)
                load_w(WI, "WI", w_fi[l], 2 * DFF); load_w(WF, "WF", w_fo[l], D)
                hT = sb2("hT", [128, 22, 128], BF16); sil = sb2("sil", [128, 128])
                if final:
                    DMA("sp", gbc[:, 0, :], g_fin.partition_broadcast(128), [], ["gbc0"])
                for i in range(NT + 1):
                    is_s = (i == NT)
                    xa, xkey = load_x(l, i, src_h)
                    rmsnorm_tile(xa, xkey, 2)
                    for c in range(22):
                        ba, bu = (c % 2) * 2, (c % 2) * 2 + 1
                        for kc in range(8):
                            MM(ps[ba][:, 0:128], WI[:, kc, c * 128:(c + 1) * 128], nT[:, kc, :], kc == 0, kc == 7, ["nT", "WI"], [psr(ba)])
                        for kc in range(8):
                            MM(ps[bu][:, 0:128], WI[:, kc, DFF + c * 128:DFF + (c + 1) * 128], nT[:, kc, :], kc == 0, kc == 7, ["nT", "WI"], [psr(bu)])
                        ACTF(sil[:], ps[ba][:, 0:128], AF.Silu, [psr(ba)], ["sil"])
                        TT("dve", hT[:, c, :], sil[:], ps[bu][:, 0:128], ALU.mult, ["sil", psr(bu)], ["hT"])
                    for nb_ in range(2):
                        bk = 4 + nb_
                        for c in range(22):
                            MM(ps[bk][:, :], hT[:, c, :], WF[:, c, nb_ * 512:(nb_ + 1) * 512], c == 0, c == 21, ["hT", "WF"], [psr(bk)])
                        TT("dve", xa[:, nb_ * 512:(nb_ + 1) * 512], xa[:, nb_ * 512:(nb_ + 1) * 512], ps[bk][:, :], ALU.add, [xkey, psr(bk)], [xkey])
                    if final:
                        ACTF(stg[0][:, 0:D], xa, AF.Square, [xkey], ["stg0", "rs"], accum_out=rs[:, 0:1])
                        ACTF(rs[:, 1:2], rs[:, 0:1], AF.Sqrt, ["rs", "epsT"], ["rs"], bias=epsT[:, 0:1], scale=1.0 / D)
                        RECIP(rs[:, 2:3], rs[:, 1:2], ["rs"], ["rs"])
                        STT(stg[1][:, 0:D], xa, rs[:, 2:3], gbc[:, 0, :], ALU.mult, ALU.mult, [xkey, "rs", "gbc0"], ["stg1"])
                        if is_s:
                            DMA("sp", ys[:, :], stg[1][:, 0:D], ["stg1"], [])
                        else:
                            DMA("sp", yp[i * 128:(i + 1) * 128, :], stg[1][:, 0:D], ["stg1"], [])
                    elif not is_s:
                        DMA("sp", dst_h[i * 128:(i + 1) * 128, :], xa, [xkey], [])

        for l in range(nlayers):
            src_h = xp if l == 0 else h_b
            load_gains(l)
            sweep_fox(l, src_h)
            if DEBUG and l == 0:
                DMA("sp", dbg_fos[:, :, :], FOd[:, :, S:S + 128], ["FOd"], [])
            if stop_after == "fox":
                break
            sweep_gdn(l, src_h)
            if DEBUG and l == 0:
                DMA("sp", dbg_gos[:, :, :], GOd[:, :, S:S + 128], ["GOd"], [])
            if stop_after == "gdn":
                break
            sweep_mem(l, src_h, h_a)
            if DEBUG and l == 0:
                DMA("sp", dbg_xs1[:, :], XS[:], ["XS"], [])
            if stop_after == "mem":
                break
            sweep_ffn(l, h_a, h_b, final=(l == DEPTH - 1))
        P.finish()
    return nc

def _host_consts(cs_p):
    c = {}
    c["k_ident"] = np.eye(128, dtype=np.float32)
    for k, v in _consts(cs_p).items():
        c["kp_" + k] = v
    for k, v in _consts(8).items():
        c["ks_" + k] = v
    key = np.arange(128)[:, None, None]; r = np.arange(4)[None, :, None]; q = np.arange(512)[None, None, :]
    c["k_negq"] = np.where(q >= 128 * r + key, 0.0, NEG).astype(np.float32)
    c["k_iota"] = np.arange(128, dtype=np.float32).reshape(128, 1)
    t = np.arange(128)
    c["k_ub128"] = (t[:, None] <= t[None, :]).astype(np.float32)
    c["k_ls128"] = (t[:, None] > t[None, :]).astype(np.float32)
    shc = np.zeros((128, 3, 128), np.float32); shp = np.zeros((128, 3, 128), np.float32); shs = np.zeros((48, 3, 128), np.float32)
    for s in (1, 2, 3):
        for tt in range(128):
            if tt - s >= 0:
                shc[tt - s, s - 1, tt] = 1.0
            else:
                shp[128 + tt - s, s - 1, tt] = 1.0
        for b in range(16):
            for tl in range(8):
                if tl - s < 0:
                    shs[3 * b + 3 + tl - s, s - 1, 8 * b + tl] = 1.0
    c["k_shc"] = shc; c["k_shp"] = shp; c["k_shs"] = shs
    return c


_IN_ORDER = ("x_prompt", "x_sample", "cache_fox_k", "cache_fox_v", "cache_fox_logf", "state_gdn", "state_gdn_conv",
             "cache_mem_k", "cache_mem_v", "page_table", "mem_prompt")


def kernel(**inp):
    f = lambda a: np.ascontiguousarray(np.asarray(a))
    x_prompt = f(inp["x_prompt"]); x_sample = f(inp["x_sample"])
    B, S, _ = x_prompt.shape
    BS, LS, _ = x_sample.shape
    assert B == 4 and BS == 128 and LS == 8
    page_table = f(inp["page_table"]).astype(np.int32)
    NPG = page_table.shape[1]
    ckf = f(inp["cache_fox_k"]); NPHYS = ckf.shape[1]
    ckv = np.concatenate([ckf.reshape(DEPTH * NPHYS * 128, FW), f(inp["cache_fox_v"]).reshape(DEPTH * NPHYS * 128, FW),
                          f(inp["cache_fox_logf"]).reshape(DEPTH * NPHYS * 128, FH)], axis=1)
    sgd = f(inp["state_gdn"]); scv = f(inp["state_gdn_conv"])
    cmk = f(inp["cache_mem_k"]).reshape(DEPTH, BS, NMEM, MW); cmv = f(inp["cache_mem_v"]).reshape(DEPTH, BS, NMEM, MW)
    memp = f(inp["mem_prompt"])
    cs_p = math.gcd(S, 64)
    nc = build_program(S, NPG, NPHYS, cs_p, nlayers=NLAYERS, stop_after=STOP_AFTER)
    consts = _host_consts(cs_p)
    shared = {
        "ckv": ckv,
        "g_mix": f(inp["g_norm_mix"]), "w_in": f(inp["w_in"]), "b_f": f(inp["b_fox_f"]), "conv_w": f(inp["gdn_conv_w"]),
        "a_log": f(inp["gdn_a_log"]), "dt_b": f(inp["gdn_dt_bias"]), "gnw": f(inp["gdn_norm_w"]), "w_out": f(inp["w_out"]),
        "g_memin": f(inp["g_norm_memin"]), "w_mkv": f(inp["w_mem_kv"]), "g_mem": f(inp["g_norm_mem"]),
        "w_mq": f(inp["w_mem_q"]), "w_mo": f(inp["w_mem_o"]), "g_ffn": f(inp["g_norm_ffn"]), "w_fi": f(inp["w_ffn_in"]),
        "w_fo": f(inp["w_ffn_out"]), "g_fin": f(inp["g_final"]),
    }
    shared.update(consts)
    in_maps = []
    for c in range(8):
        b = c // 2
        m = dict(shared)
        m.update({
            "xp": x_prompt[b], "xs": x_sample[16 * c:16 * c + 16].reshape(128, D),
            "sgd": sgd[:, 16 * c:16 * c + 16], "scv": scv[:, 16 * c:16 * c + 16],
            "cmk": cmk[:, 16 * c:16 * c + 16], "cmv": cmv[:, 16 * c:16 * c + 16],
            "ptab": page_table[16 * c:16 * c + 16], "memp": memp[b],
        })
        in_maps.append({k: np.ascontiguousarray(v) for k, v in m.items()})
    res = run_bass_kernel_spmd(nc, in_maps, core_ids=list(range(8)))
    R = res.results
    ev = [R[2 * b] for b in range(4)]
    cat_b = lambda key: np.stack([r[key] for r in ev], axis=0)
    cat_s = lambda key, ax: np.concatenate([r[key] for r in R], axis=ax)
    yp = cat_b("yp")
    ys = cat_s("ys", 0).reshape(BS, LS, D)
    def pl(key, tail):
        return np.stack([r[key] for r in ev], axis=1).reshape((DEPTH, 4) + tail)
    def sl(key, tail):
        return np.concatenate([r[key].reshape((DEPTH, 16) + tail) for r in R], axis=1)
    outs = (yp, ys,
            pl("o_fkp", (S, FH, FD)), pl("o_fvp", (S, FH, FD)), pl("o_flp", (S, FH)),
            pl("o_gsp", (GH, GD, GD)), pl("o_gcp", (3, GC3)), pl("o_mkp", (NMEM, MH, MD)), pl("o_mvp", (NMEM, MH, MD)),
            sl("o_fks", (LS, FH, FD)), sl("o_fvs", (LS, FH, FD)), sl("o_fls", (LS, FH)),
            sl("o_gss", (GH, GD, GD)), sl("o_gcs", (3, GC3)))
    global _LAST
    _LAST = R
    return tuple(np.ascontiguousarray(o.astype(np.float32)) for o in outs)
```

```python
import contextlib
import math
import numpy as np
import concourse.bass as bass
import concourse.mybir as mybir
from concourse.bass import IndirectOffsetOnAxis
from concourse.bass_utils import run_bass_kernel_spmd

F32 = mybir.dt.float32; BF16 = mybir.dt.bfloat16; I32 = mybir.dt.int32
AF = mybir.ActivationFunctionType; ALU = mybir.AluOpType; AX = mybir.AxisListType

D = 1024; DEPTH = 2
FH = 8; FD = 64; FW = 512
GH = 4; GD = 128; GW = 512; GC3 = 1536
NMEM = 256; MH = 4; MD = 128; MW = 512
DFF = 2816
INC = 3600
EPS = 1e-6
NEG = -30000.0
DEBUG = False
NLAYERS = DEPTH
STOP_AFTER = None


class Prog:
    NDMA = 12

    def __init__(self, nc, es):
        self.nc = nc
        self.engs = {"pe": nc.tensor, "act": nc.scalar, "dve": nc.vector, "pool": nc.gpsimd, "sp": nc.sync}
        self.sem = {k: es.enter_context(nc.semaphore("c_" + k)) for k in ("pe", "act", "dve", "pool")}
        self.cnt = {k: 0 for k in self.sem}
        self.dsem = {q: [es.enter_context(nc.semaphore(f"d_{q}{i}")) for i in range(self.NDMA)] for q in ("sp", "pool", "act")}
        self.dval = {q: [0] * self.NDMA for q in self.dsem}
        self.dnext = {q: 0 for q in self.dsem}
        self.known = {k: {} for k in self.engs}
        self.lastw = {}
        self.readers = {}
        self.n = 0

    def _wait(self, eng, tok):
        if tok is None:
            return
        sem, val, src = tok
        if src == "pe" and eng == "pe":
            return
        kn = self.known[eng]
        if kn.get(sem.name, 0) >= val:
            return
        self.engs[eng].wait_ge(sem, val)
        kn[sem.name] = val

    def emit(self, eng, fn, reads=(), writes=(), dma=False):
        for r in reads:
            self._wait(eng, self.lastw.get(r))
        for w in writes:
            self._wait(eng, self.lastw.get(w))
            for t in self.readers.get(w, ()):
                self._wait(eng, t)
        if dma:
            i = self.dnext[eng]; self.dnext[eng] = (i + 1) % self.NDMA
            sem = self.dsem[eng][i]
            prev = self.dval[eng][i]
            if prev:
                self._wait(eng, (sem, prev, "dma"))
            ins = fn(self.engs[eng])
            val = prev + 16
            self.dval[eng][i] = val
            ins.then_inc(sem, 16)
            tok = (sem, val, "dma")
        else:
            ins = fn(self.engs[eng])
            self.cnt[eng] += 1
            ins.then_inc(self.sem[eng], 1)
            tok = (self.sem[eng], self.cnt[eng], eng)
        for w in writes:
            self.lastw[w] = tok; self.readers[w] = []
        for r in reads:
            self.readers.setdefault(r, []).append(tok)
        self.n += 1
        return tok

    def barrier(self):
        toks = [(self.sem[k], self.cnt[k], k) for k in self.sem if self.cnt[k]]
        for q in self.dsem:
            for i, s in enumerate(self.dsem[q]):
                if self.dval[q][i]:
                    toks.append((s, self.dval[q][i], "dma"))
        for eng in self.engs:
            for t in toks:
                if not (t[2] == eng):
                    self._wait(eng, t)

    def finish(self):
        for q in self.dsem:
            for i, sem in enumerate(self.dsem[q]):
                v = self.dval[q][i]
                if v:
                    self._wait("sp", (sem, v, "dma"))


def _consts(cs):
    t = np.arange(128)
    same = (t[:, None] // cs) == (t[None, :] // cs)
    up_incl = same & (t[:, None] <= t[None, :])
    c = {}
    c["negmt"] = np.where(up_incl, 0.0, NEG).astype(np.float32)
    c["posm"] = np.where(up_incl.T, 0.0, -NEG).astype(np.float32)
    c["su"] = (same & (t[:, None] < t[None, :])).astype(np.float32)
    c["sl"] = c["su"].T.copy()
    c["ub"] = up_incl.astype(np.float32)
    last = (t // cs) * cs + cs - 1
    c["bl"] = (t[:, None] == last[None, :]).astype(np.float32)
    nb = 128 // cs
    el = np.zeros((128, nb, 128), np.float32)
    for b in range(nb):
        el[b * cs + cs - 1, b, :] = 1.0
    il = np.zeros((128, nb), np.float32)
    for b in range(nb):
        il[b * cs + cs - 1, b] = 1.0
    c["islast"] = il
    cm = np.zeros((128, nb, 128), np.float32)
    rm = np.zeros((128, nb), np.float32)
    for b in range(nb):
        cm[:, b, b * cs:(b + 1) * cs] = 1.0
        rm[b * cs:(b + 1) * cs, b] = 1.0
    c["colmask"] = cm
    c["rowmask"] = rm
    sh = np.zeros((128, 3, 128), np.float32)
    for s in (1, 2, 3):
        for tt in range(128):
            if tt - s >= 0 and (tt - s) // cs == tt // cs:
                sh[tt - s, s - 1, tt] = 1.0
    c["shc"] = sh
    return c


def build_program(S, NPG, NPHYS, cs_p, nlayers=DEPTH, stop_after=None):
    NT = S // 128
    nc = bass.Bass("TRN2", target_bir_lowering=False)
    es = contextlib.ExitStack()

    def din(name, shape, dt=F32):
        return nc.dram_tensor(name, list(shape), dt, kind="ExternalInput").ap()

    def dout(name, shape, dt=F32):
        return nc.dram_tensor(name, list(shape), dt, kind="ExternalOutput").ap()

    def dscr(name, shape, dt=F32):
        return nc.dram_tensor(name, list(shape), dt, kind="Internal").ap()

    xp = din("xp", [S, D]); xs = din("xs", [128, D])
    ckv = din("ckv", [DEPTH * NPHYS * 128, 2 * FW + FH])
    sgd = din("sgd", [DEPTH, 16, GH, GD, GD]); scv = din("scv", [DEPTH, 16, 3, GC3])
    cmk = din("cmk", [DEPTH, 16, NMEM, MW]); cmv = din("cmv", [DEPTH, 16, NMEM, MW])
    ptab = din("ptab", [16, NPG], I32)
    memp = din("memp", [NMEM, D])
    g_mix = din("g_mix", [DEPTH, D]); w_in = din("w_in", [DEPTH, D, INC]); b_f = din("b_f", [DEPTH, FH])
    conv_w = din("conv_w", [DEPTH, 4, GC3]); a_log = din("a_log", [DEPTH, GH]); dt_b = din("dt_b", [DEPTH, GH])
    gnw = din("gnw", [DEPTH, GD]); w_out = din("w_out", [DEPTH, D, D])
    g_memin = din("g_memin", [DEPTH, D]); w_mkv = din("w_mkv", [DEPTH, D, 2 * MW])
    g_mem = din("g_mem", [DEPTH, D]); w_mq = din("w_mq", [DEPTH, D, MW]); w_mo = din("w_mo", [DEPTH, MW, D])
    g_ffn = din("g_ffn", [DEPTH, D]); w_fi = din("w_fi", [DEPTH, D, 2 * DFF]); w_fo = din("w_fo", [DEPTH, DFF, D])
    g_fin = din("g_fin", [D])
    k_ident = din("k_ident", [128, 128])
    k_p = {k: din("kp_" + k, v.shape) for k, v in _consts(cs_p).items()}
    k_s = {k: din("ks_" + k, v.shape) for k, v in _consts(8).items()}
    k_negq = din("k_negq", [128, 4, 512])
    k_iota = din("k_iota", [128, 1])
    k_ub128 = din("k_ub128", [128, 128]); k_ls128 = din("k_ls128", [128, 128])
    k_shc = din("k_shc", [128, 3, 128]); k_shp = din("k_shp", [128, 3, 128]); k_shs = din("k_shs", [48, 3, 128])

    yp = dout("yp", [S, D]); ys = dout("ys", [128, D])
    o_fkp = dout("o_fkp", [DEPTH, S, FW]); o_fvp = dout("o_fvp", [DEPTH, S, FW]); o_flp = dout("o_flp", [DEPTH, S, FH])
    o_gsp = dout("o_gsp", [DEPTH, GH, GD, GD]); o_gcp = dout("o_gcp", [DEPTH, 3, GC3])
    o_mkp = dout("o_mkp", [DEPTH, NMEM, MW]); o_mvp = dout("o_mvp", [DEPTH, NMEM, MW])
    o_fks = dout("o_fks", [DEPTH, 128, FW]); o_fvs = dout("o_fvs", [DEPTH, 128, FW]); o_fls = dout("o_fls", [DEPTH, 128, FH])
    o_gss = dout("o_gss", [DEPTH, 16, GH, GD, GD]); o_gcs = dout("o_gcs", [DEPTH, 16, 3, GC3])

    if DEBUG:
        dbg_fos = dout("dbg_fos", [128, 4, 128], BF16); dbg_gos = dout("dbg_gos", [128, 4, 128], BF16); dbg_xs1 = dout("dbg_xs1", [128, D])
    h_a = dscr("h_a", [S, D]); h_b = dscr("h_b", [S, D])
    FOd = dscr("FOd", [128, 4, S + 128], BF16); GOd = dscr("GOd", [128, 4, S + 128], BF16)

    with es:
        P = Prog(nc, es)
        uid = [0]

        def mk_sb(stack):
            def f(name, shape, dt=F32):
                uid[0] += 1
                return stack.enter_context(nc.sbuf_tensor(f"{name}_{uid[0]}", list(shape), dt))
            return f

        sb = mk_sb(es)

        def MM(out, lhsT, rhs, start, stop, r, w):
            P.emit("pe", lambda e: e.matmul(out, lhsT, rhs, start=start, stop=stop), reads=r, writes=w)

        def TR(out, in_, idn, r, w):
            P.emit("pe", lambda e: e.transpose(out, in_, idn), reads=r, writes=w)

        def ACTF(out, in_, func, r, w, **kw):
            P.emit("act", lambda e: e.activation(out=out, in_=in_, func=func, **kw), reads=r, writes=w)

        def CP(eng, out, in_, r, w):
            if eng == "act":
                P.emit(eng, lambda e: e.copy(out=out, in_=in_), reads=r, writes=w)
            else:
                P.emit(eng, lambda e: e.tensor_copy(out=out, in_=in_), reads=r, writes=w)

        def TT(eng, out, in0, in1, op, r, w):
            P.emit(eng, lambda e: e.tensor_tensor(out=out, in0=in0, in1=in1, op=op), reads=r, writes=w)

        def TS(eng, out, in0, s1, s2, op0, op1, r, w):
            if s2 is None:
                P.emit(eng, lambda e: e.tensor_scalar(out=out, in0=in0, scalar1=s1, scalar2=None, op0=op0), reads=r, writes=w)
            else:
                P.emit(eng, lambda e: e.tensor_scalar(out=out, in0=in0, scalar1=s1, scalar2=s2, op0=op0, op1=op1), reads=r, writes=w)

        def STT(out, in0, scalar, in1, op0, op1, r, w):
            P.emit("dve", lambda e: e.scalar_tensor_tensor(out=out, in0=in0, scalar=scalar, in1=in1, op0=op0, op1=op1), reads=r, writes=w)

        def MSET(eng, ap, val, w):
            P.emit(eng, lambda e: e.memset(ap, val), writes=w)

        def DMA(q, out, in_, r, w):
            P.emit(q, lambda e: e.dma_start(out=out, in_=in_), reads=r, writes=w, dma=True)

        def GATHER(out, table, idx_ap, r, w):
            P.emit("pool", lambda e: e.indirect_dma_start(out=out, out_offset=None, in_=table,
                                                          in_offset=IndirectOffsetOnAxis(ap=idx_ap, axis=0)), reads=r, writes=w, dma=True)

        def RECIP(out, in_, r, w):
            P.emit("dve", lambda e: e.reciprocal(out=out, in_=in_), reads=r, writes=w)

        def REDUCE(out, in_, r, w):
            P.emit("dve", lambda e: e.tensor_reduce(out=out, in_=in_, axis=AX.X, op=ALU.add), reads=r, writes=w)

        ident = sb("ident", [128, 128]); DMA("sp", ident[:], k_ident[:, :], [], ["ident"])
        identb = sb("identb", [128, 128], BF16); CP("dve", identb[:], ident[:], ["ident"], ["identb"])
        ones = sb("ones", [128, 128]); MSET("dve", ones[:], 1.0, ["ones"])
        onesb = sb("onesb", [128, 128], BF16); MSET("dve", onesb[:], 1.0, ["onesb"])
        epsT = sb("epsT", [128, 1]); MSET("dve", epsT[:], EPS, ["epsT"])
        gbc = sb("gbc", [128, 4, D])
        xt = [sb(f"xt{i}", [128, D]) for i in range(2)]
        nb16 = sb("nb16", [128, D], BF16)
        nT = sb("nT", [128, 8, 128], BF16)
        rs = sb("rs", [128, 8])
        XS = sb("XS", [128, D])
        stg = [sb(f"stg{i}", [128, 1032]) for i in range(2)]
        ps = [es.enter_context(nc.psum_tensor(f"ps{i}", [128, 512], F32)) for i in range(8)]
        psb = ps[7].bitcast(BF16)
        DMA("sp", XS[:], xs[:, :], [], ["XS"])

        def psr(i):
            return f"ps{i}"

        def load_gains(l):
            for j, src in enumerate((g_mix[l], g_mem[l], g_ffn[l], g_memin[l])):
                DMA("sp", gbc[:, j, :], src.partition_broadcast(128), [], [f"gbc{j}"])

        def load_w(dst, dst_key, src_ap, ncols, col0=0):
            K = src_ap.shape[0]
            for kc in range(K // 128):
                P.emit("pool", lambda e, kc=kc: e.dma_start(out=dst[:, kc, col0:col0 + ncols], in_=src_ap[kc * 128:(kc + 1) * 128, :]),
                       writes=[dst_key], dma=True)

        def rmsnorm_tile(x_ap, xkey, gidx, dstT=None, dkey="nT", toff=0):
            dstT = nT if dstT is None else dstT
            ACTF(stg[0][:, 0:D], x_ap, AF.Square, [xkey], ["stg0", "rs"], accum_out=rs[:, 0:1])
            ACTF(rs[:, 1:2], rs[:, 0:1], AF.Ln, ["rs", "epsT"], ["rs"], bias=epsT[:, 0:1], scale=1.0 / D)
            ACTF(rs[:, 2:3], rs[:, 1:2], AF.Exp, ["rs"], ["rs"], scale=-0.5)
            STT(nb16[:], x_ap, rs[:, 2:3], gbc[:, gidx, :], ALU.mult, ALU.mult, [xkey, "rs", f"gbc{gidx}"], ["nb16"])
            for kc in range(8):
                TR(psb[:, kc * 128:(kc + 1) * 128], nb16[:, kc * 128:(kc + 1) * 128], identb[:], ["nb16", "identb"], [psr(7)])
            CP("dve", dstT[:, :, toff:toff + 128], psb[:, 0:1024].rearrange("p (k t) -> p k t", k=8), [psr(7)], [dkey])

        def load_x(l, i, src_h):
            if i == NT:
                return XS[:], "XS"
            xb_ = xt[i % 2]; xkey = f"xt{i % 2}"
            DMA("sp", xb_[:], src_h[i * 128:(i + 1) * 128, :], [], [xkey])
            return xb_[:], xkey

        def sweep_fox(l, src_h):
            with contextlib.ExitStack() as ses:
                sb2 = mk_sb(ses)
                P.barrier()
                WA = sb2("WAf", [128, 8, 1544], BF16)
                KT = sb2("KT", [128, 4, S], BF16); VR = sb2("VR", [128, NT, FH, 65], BF16); rrow = sb2("rrow", [65, 512])
                QT = sb2("QT", [128, 4, 512], BF16); FOg = sb2("FOg", [128, 4, 512], BF16)
                LOGF = sb2("LOGF", [128, 8]); CC = sb2("CC", [128, NT + 1, FH]); TOT = sb2("TOT", [128, NT + 2, FH])
                BIAS = sb2("BIAS", [128, NT, FH]); bfb = sb2("bfb", [128, FH])
                PT = [sb2(f"PT{i}", [128, 512], BF16) for i in range(3)]
                rl = sb2("rl", [64, 512])
                negq = sb2("negq", [128, 4, 512], BF16)
                ub128 = sb2("ub128", [128, 128]); ls128 = sb2("ls128", [128, 128]); ub8 = sb2("ub8", [128, 128])
                negm8 = sb2("negm8", [128, 128], BF16); iota_p = sb2("iota_p", [128, 1])
                P.emit("pool", lambda e: e.dma_start(out=negq[:], in_=k_negq[:, :, :]), writes=["negq"], dma=True)
                P.emit("pool", lambda e: e.dma_start(out=negm8[:], in_=k_s["negmt"]), writes=["negm8"], dma=True)
                DMA("sp", ub128[:], k_ub128[:, :], [], ["ub128"]); DMA("sp", ls128[:], k_ls128[:, :], [], ["ls128"])
                DMA("sp", ub8[:], k_s["ub"], [], ["ub8"]); DMA("sp", iota_p[:], k_iota[:, :], [], ["iota_p"])
                DMA("sp", bfb[:], b_f[l].partition_broadcast(128), [], ["bfb"])
                load_w(WA, "WA", w_in[l, :, 0:1544], 1544)
                MSET("dve", TOT[:, 0, :], 0.0, ["TOT"])
                MSET("pool", VR[:], 1.0, ["VR"])
                QTs = sb2("QTs", [128, 4, 128], BF16); KTs = sb2("KTs", [128, 4, 128], BF16)
                VAs = sb2("VAs", [128, 8, 65], BF16)
                KV = [sb2(f"KV{i}", [128, 1032]) for i in range(2)]
                Kb = sb2("Kb", [128, 512], BF16); KpT = sb2("KpT", [128, 4, 128], BF16)
                VA = [sb2(f"VA{i}", [128, 8, 65], BF16) for i in range(2)]
                PTp = [sb2(f"PTp{i}", [128, 8, 128], BF16) for i in range(2)]
                SUF = sb2("SUF", [128, 8]); TOTL = sb2("TOTL", [128, 8]); sc64 = sb2("sc64", [128, 64])
                PTI = sb2("PTI", [128, 16 * NPG], I32); PTF = sb2("PTF", [128, 16 * NPG]); IDX = sb2("IDX", [128, 16 * NPG], I32)
                fos = sb2("fos", [128, 512]); fosb = sb2("fosb", [128, 512], BF16); FOs = sb2("FOs", [128, 4, 128], BF16)
                rden = sb2("rden", [128, 8])

                def fox_group(j):
                    nk = 4 * j + 4
                    for kt in range(nk):
                        STT(BIAS[:, kt, :], CC[:, kt, :], -1.0, TOT[:, 4 * j + 2, :], ALU.mult, ALU.add, ["CC", "TOT"], ["BIAS"])
                    steps = [(h, kt) for h in range(FH) for kt in range(nk)]

                    def qk(s):
                        h, kt = steps[s]
                        pair, base = h // 2, 64 * (h % 2)
                        si = s % 2
                        diag = kt >= 4 * j
                        c0 = 128 * (kt - 4 * j) if diag else 0
                        MM(ps[si][:, c0:512], KT[base:base + 64, pair, kt * 128:(kt + 1) * 128], QT[base:base + 64, pair, c0:512], True, not diag,
                           ["KT", "QT"], [psr(si)])
                        if diag:
                            MM(ps[si][:, c0:512], identb[:], negq[:, kt - 4 * j, c0:512], False, True, ["identb", "negq"], [psr(si)])

                    def tail(h):
                        pair, base = h // 2, 64 * (h % 2)
                        bo_, bb_ = (2, 3) if h % 2 == 0 else (4, 5)
                        RECIP(rrow[64:65, :], ps[bo_][64:65, :], [psr(bo_)], ["rrow"])
                        MM(ps[bb_][0:64, :], ones[64:65, 0:64], rrow[64:65, :], True, True, ["ones", "rrow"], [psr(bb_)])
                        CP("act", rl[:], ps[bb_][0:64, :], [psr(bb_)], ["rl"])
                        TT("dve", FOg[base:base + 64, pair, :], ps[bo_][0:64, :], rl[:], ALU.mult, [psr(bo_), "rl"], ["FOg"])

                    qk(0)
                    for s, (h, kt) in enumerate(steps):
                        if s + 1 < len(steps):
                            qk(s + 1)
                        pk = f"PT{s % 3}"; pt = PT[s % 3]
                        c0 = 128 * (kt - 4 * j) if kt >= 4 * j else 0
                        ACTF(pt[:, c0:512], ps[s % 2][:, c0:512], AF.Exp, [psr(s % 2), "BIAS"], [pk], bias=BIAS[:, kt, h:h + 1])
                        bo_ = 2 if h % 2 == 0 else 4
                        MM(ps[bo_][0:65, c0:512], VR[:, kt, h, :], pt[:, c0:512], kt == 0, kt == nk - 1, ["VR", pk], [psr(bo_)])
                        if h > 0 and kt == min(1, nk - 1):
                            tail(h - 1)
                    tail(FH - 1)
                    DMA("sp", FOd[:, :, j * 512:(j + 1) * 512], FOg[:], ["FOg"], ["FOd"])

                def fox_sample(st, sk):
                    DMA("sp", PTI[:], ptab.rearrange("b p -> (b p)").partition_broadcast(128), [], ["PTI"])
                    CP("dve", PTF[:], PTI[:], ["PTI"], ["PTF"])
                    TS("dve", PTF[:], PTF[:], 128.0, iota_p[:, 0:1], ALU.mult, ALU.add, ["PTF", "iota_p"], ["PTF"])
                    TS("dve", PTF[:], PTF[:], float(l * NPHYS * 128), None, ALU.add, None, ["PTF"], ["PTF"])
                    CP("dve", IDX[:], PTF[:], ["PTF"], ["IDX"])
                    MSET("dve", VAs[:], 1.0, ["VAs"])
                    CP("dve", VAs[:, :, 0:64], st[:, 512:1024].rearrange("p (h d) -> p h d", h=8), [sk], ["VAs"])
                    for u in range(2):
                        MSET("pool", VA[u][:], 1.0, [f"VA{u}"])
                        MSET("pool", PTp[u][:], 0.0, [f"PTp{u}"])
                    A_, B_ = 5, 6
                    first = True
                    cnt_pg = 0
                    for b in range(16):
                        MSET("dve", TOTL[:], 0.0, ["TOTL"])
                        for pg in reversed(range(NPG)):
                            u = cnt_pg % 2; cnt_pg += 1
                            col = b * NPG + pg
                            GATHER(KV[u][:], ckv, IDX[:, col:col + 1], ["IDX"], [f"KV{u}"])
                            CP("dve", Kb[:], KV[u][:, 0:512], [f"KV{u}"], ["Kb"])
                            for pr in range(4):
                                TR(psb[:, pr * 128:(pr + 1) * 128], Kb[:, pr * 128:(pr + 1) * 128], identb[:], ["Kb", "identb"], [psr(7)])
                            CP("act", KpT[:], psb[:, 0:512].rearrange("p (k t) -> p k t", k=4), [psr(7)], ["KpT"])
                            for h in range(FH):
                                pair, base = h // 2, 64 * (h % 2)
                                MM(ps[0][:, h * 8:(h + 1) * 8], KpT[base:base + 64, pair, :], QTs[base:base + 64, pair, 8 * b:8 * b + 8], True, True,
                                   ["KpT", "QTs"], [psr(0)])
                            MM(ps[1][:, 0:8], ls128[:], KV[u][:, 1024:1032], True, True, ["ls128", f"KV{u}"], [psr(1)])
                            MM(ps[1][:, 8:16], ones[:], KV[u][:, 1024:1032], True, True, ["ones", f"KV{u}"], [psr(1)])
                            TT("dve", SUF[:], ps[1][:, 0:8], TOTL[:], ALU.add, [psr(1), "TOTL"], ["SUF"])
                            TT("dve", TOTL[:], ps[1][:, 8:16], TOTL[:], ALU.add, [psr(1), "TOTL"], ["TOTL"])
                            TT("dve", sc64[:].rearrange("p (h q) -> p h q", h=8), ps[0][:, 0:64].rearrange("p (h q) -> p h q", h=8),
                               SUF[:].unsqueeze(2).to_broadcast([128, 8, 8]), ALU.add, [psr(0), "SUF"], ["sc64"])
                            ACTF(PTp[u][:, :, 8 * b:8 * b + 8], sc64[:].rearrange("p (h q) -> p h q", h=8), AF.Exp, ["sc64"], [f"PTp{u}"])
                            CP("act", VA[u][:, :, 0:64], KV[u][:, 512:1024].rearrange("p (h d) -> p h d", h=8), [f"KV{u}"], [f"VA{u}"])
                            for h in range(FH):
                                bk = A_ if h < 4 else B_
                                MM(ps[bk][:, (h % 4) * 65:(h % 4 + 1) * 65], PTp[u][:, h, :], VA[u][:, h, :], first and h % 4 == 0, False,
                                   [f"PTp{u}", f"VA{u}"], [psr(bk)])
                            first = False
                        for u in range(2):
                            MSET("pool", PTp[u][:, :, 8 * b:8 * b + 8], 0.0, [f"PTp{u}"])
                    MM(ps[1][:, 0:8], ub8[:], st[:, 1024:1032], True, True, ["ub8", sk], [psr(1)])
                    TS("dve", SUF[:], ps[1][:, 0:8], -1.0, None, ALU.mult, None, [psr(1)], ["SUF"])
                    for h in range(FH):
                        pair, base = h // 2, 64 * (h % 2)
                        si = h % 2
                        MM(ps[si][:, 0:128], KTs[base:base + 64, pair, :], QTs[base:base + 64, pair, :], True, False, ["KTs", "QTs"], [psr(si)])
                        MM(ps[si][:, 0:128], identb[:], negm8[:], False, True, ["identb", "negm8"], [psr(si)])
                        pk = f"PT{h % 3}"; pt = PT[h % 3]
                        ACTF(pt[:, 0:128], ps[si][:, 0:128], AF.Exp, [psr(si), "SUF"], [pk], bias=SUF[:, h:h + 1])
                        bk = A_ if h < 4 else B_
                        MM(ps[bk][:, (h % 4) * 65:(h % 4 + 1) * 65], pt[:, 0:128], VAs[:, h, :], first, True, [pk, "VAs"], [psr(bk)])
                    for half, bk in ((0, A_), (1, B_)):
                        v = ps[bk][:, 0:260].rearrange("p (h e) -> p h e", h=4)
                        CP("dve", rden[:, half * 4:(half + 1) * 4], v[:, :, 64], [psr(bk)], ["rden"])
                    RECIP(rden[:], rden[:], ["rden"], ["rden"])
                    for half, bk in ((0, A_), (1, B_)):
                        v = ps[bk][:, 0:260].rearrange("p (h e) -> p h e", h=4)
                        TT("dve", fos[:, half * 256:(half + 1) * 256].rearrange("p (h d) -> p h d", h=4), v[:, :, 0:64],
                           rden[:, half * 4:(half + 1) * 4].unsqueeze(2).to_broadcast([128, 4, 64]), ALU.mult, [psr(bk), "rden"], ["fos"])
                    CP("dve", fosb[:], fos[:], ["fos"], ["fosb"])
                    for pr in range(4):
                        TR(psb[:, pr * 128:(pr + 1) * 128], fosb[:, pr * 128:(pr + 1) * 128], identb[:], ["fosb", "identb"], [psr(7)])
                    CP("dve", FOs[:], psb[:, 0:512].rearrange("p (k t) -> p k t", k=4), [psr(7)], ["FOs"])
                    DMA("sp", FOd[:, :, S:S + 128], FOs[:], ["FOs"], ["FOd"])

                for i in range(NT + 1):
                    is_s = (i == NT)
                    xa, xkey = load_x(l, i, src_h)
                    goff = (i % 4) * 128 if not is_s else 0
                    rmsnorm_tile(xa, xkey, 0)
                    for cb, (c0, cw, pb) in enumerate(((512, 512, 0), (1024, 512, 1), (1536, 8, 2))):
                        for kc in range(8):
                            MM(ps[pb][:, 0:cw], nT[:, kc, :], WA[:, kc, c0:c0 + cw], kc == 0, kc == 7, ["nT", "WA"], [psr(pb)])
                    st = stg[i % 2]; sk = f"stg{i % 2}"
                    CP("act", st[:, 0:512], ps[0][:, 0:512], [psr(0)], [sk])
                    CP("dve", st[:, 512:1024], ps[1][:, 0:512], [psr(1)], [sk])
                    if not is_s:
                        CP("pool", VR[:, i, :, 0:64], st[:, 512:1024].rearrange("p (h d) -> p h d", h=8), [sk], ["VR"])
                    TT("dve", LOGF[:], ps[2][:, 0:8], bfb[:], ALU.add, [psr(2), "bfb"], ["LOGF"])
                    ACTF(LOGF[:], LOGF[:], AF.Exp, ["LOGF"], ["LOGF"], scale=-1.0)
                    ACTF(LOGF[:], LOGF[:], AF.Ln, ["LOGF"], ["LOGF"], bias=1.0)
                    TS("dve", st[:, 1024:1032], LOGF[:], -1.0, None, ALU.mult, None, ["LOGF"], [sk])
                    ok_, ov_, ol_ = (o_fks, o_fvs, o_fls) if is_s else (o_fkp, o_fvp, o_flp)
                    r0 = 0 if is_s else i * 128
                    DMA("sp", ok_[l, r0:r0 + 128, :], st[:, 0:512], [sk], [])
                    DMA("sp", ov_[l, r0:r0 + 128, :], st[:, 512:1024], [sk], [])
                    DMA("sp", ol_[l, r0:r0 + 128, :], st[:, 1024:1032], [sk], [])
                    for cj in range(8):
                        pb = 3 + cj // 4
                        for kc in range(8):
                            MM(ps[pb][:, (cj % 4) * 128:(cj % 4 + 1) * 128], WA[:, kc, cj * 128:(cj + 1) * 128], nT[:, kc, :],
                               kc == 0, kc == 7, ["nT", "WA"], [psr(pb)])
                    qdst, qkey = (QTs[:, :, :], "QTs") if is_s else (QT[:, :, goff:goff + 128], "QT")
                    kdst, kkey = (KTs[:, :, :], "KTs") if is_s else (KT[:, :, i * 128:(i + 1) * 128], "KT")
                    P.emit("act", lambda e, qdst=qdst: e.mul(out=qdst, in_=ps[3][:, 0:512].rearrange("p (k t) -> p k t", k=4), mul=0.125),
                           reads=[psr(3)], writes=[qkey])
                    CP("dve", kdst, ps[4][:, 0:512].rearrange("p (k t) -> p k t", k=4), [psr(4)], [kkey])
                    if not is_s:
                        MM(ps[5][:, 0:8], ub128[:], st[:, 1024:1032], True, True, ["ub128", sk], [psr(5)])
                        MM(ps[5][:, 8:16], ones[:], st[:, 1024:1032], True, True, ["ones", sk], [psr(5)])
                        TT("dve", CC[:, i, :], ps[5][:, 0:8], TOT[:, i, :], ALU.add, [psr(5), "TOT"], ["CC"])
                        TT("dve", TOT[:, i + 1, :], ps[5][:, 8:16], TOT[:, i, :], ALU.add, [psr(5), "TOT"], ["TOT"])
                        if i % 4 == 3:
                            fox_group(i // 4)
                    else:
                        fox_sample(st, sk)

        def sweep_gdn(l, src_h):
            with contextlib.ExitStack() as ses:
                sb2 = mk_sb(ses)
                P.barrier()
                WA = sb2("WAg", [128, 8, 2056], BF16)
                load_w(WA, "WA", w_in[l, :, 1544:3600], 2056)
                KP = {}; KS = {}
                for kd_, ksrc, pre in ((KP, k_p, "kp_"), (KS, k_s, "ks_")):
                    for nm, src in ksrc.items():
                        if nm == "colmask":
                            kd_[nm] = sb2(pre + nm, src.shape, BF16)
                            P.emit("pool", lambda e, d=kd_[nm], s_=src: e.dma_start(out=d[:], in_=s_), writes=[pre + nm], dma=True)
                        else:
                            kd_[nm] = sb2(pre + nm, src.shape); DMA("sp", kd_[nm][:], src, [], [pre + nm])
                shc = sb2("shc", [128, 3, 128]); shp = sb2("shp", [128, 3, 128]); shs = sb2("shs", [48, 3, 128])
                DMA("sp", shc[:], k_shc[:, :, :], [], ["shc"]); DMA("sp", shp[:], k_shp[:, :, :], [], ["shp"]); DMA("sp", shs[:], k_shs[:, :, :], [], ["shs"])
                qkvf = [sb2(f"qkvf{i}", [128, GC3]) for i in range(2)]
                cbuf = sb2("cbuf", [48, GC3]); cwb = sb2("cwb", [128, 4, GC3], BF16)
                actv = sb2("actv", [128, GC3]); alb = sb2("alb", [128, 2 * GH]); gnb = sb2("gnb", [128, GD])
                tq = sb2("tq", [128, 512]); tk = sb2("tk", [128, 512]); tkb = sb2("tkb", [128, 512]); tqd = sb2("tqd", [128, 512])
                trvP = [sb2(f"trv{i}", [128, 512]) for i in range(2)]; trkP = [sb2(f"trk{i}", [128, 512]) for i in range(2)]
                tkdP = [sb2(f"tkd{i}", [128, 512]) for i in range(2)]; tggP = [sb2(f"tgg{i}", [128, 512]) for i in range(2)]
                fkTP = [sb2(f"fkT{i}", [128, 512]) for i in range(2)]; fkbTP = [sb2(f"fkbT{i}", [128, 512]) for i in range(2)]
                fqTP = [sb2(f"fqT{i}", [128, 512]) for i in range(2)]; fqdTP = [sb2(f"fqdT{i}", [128, 512]) for i in range(2)]
                gl = sb2("gl", [128, 16, GH]); gsmP = [sb2(f"gsm{i}", [128, 64]) for i in range(2)]; GTP = [sb2(f"GT{i}", [128, 16, GH]) for i in range(2)]
                decT = sb2("decT", [128, 512]); dec = sb2("dec", [128, 512])
                Mk = [sb2(f"Mk{i}", [128, 512]) for i in range(2)]; MkT = [sb2(f"MkT{i}", [128, 512]) for i in range(2)]
                Pk = [sb2(f"Pk{i}", [128, 512]) for i in range(2)]; qkTm = sb2("qkTm", [128, 512])
                uacc = sb2("uacc", [128, 512]); kcT = sb2("kcT", [128, 512])
                kcTm = sb2("kcTm", [128, 4, 128]); qdTm = sb2("qdTm", [128, 4, 128]); kdm = sb2("kdm", [128, 4, 128])
                SG = sb2("SG", [128, GH, GD]); SBs = [sb2(f"SBs{i}", [128, GD]) for i in range(2)]
                osb = sb2("osb", [128, 512]); gob = sb2("gob", [128, 512], BF16); GOt = sb2("GOt", [128, 4, 128], BF16)
                DMA("sp", alb[:, 0:4], a_log[l].partition_broadcast(128), [], ["alb"])
                DMA("sp", alb[:, 4:8], dt_b[l].partition_broadcast(128), [], ["alb"])
                ACTF(alb[:, 0:4], alb[:, 0:4], AF.Exp, ["alb"], ["alb"])
                TS("dve", alb[:, 0:4], alb[:, 0:4], -1.0, None, ALU.mult, None, ["alb"], ["alb"])
                DMA("sp", gnb[:], gnw[l].partition_broadcast(128), [], ["gnb"])
                P.emit("pool", lambda e: e.dma_start(out=cwb[:].rearrange("p w c -> p (w c)"), in_=conv_w[l].rearrange("w c -> (w c)").partition_broadcast(128)),
                       writes=["cwb"], dma=True)
                DMA("sp", cbuf[:], scv[l].rearrange("b r c -> (b r) c"), [], ["cbuf"])
                MSET("dve", SG[:], 0.0, ["SG"])
                bank = [0]

                def nbk():
                    bank[0] = (bank[0] + 1) % 6
                    return bank[0]

                hs = lambda t, h: t[:, h * 128:(h + 1) * 128]
                v4 = lambda t: t[:, :].rearrange("p (h d) -> p h d", h=4)

                def bc(ap4):
                    return ap4.unsqueeze(2).to_broadcast([128, 4, 128])

                def genF(i, is_s, src_h):
                    q = i % 2
                    K_ = KS if is_s else KP
                    kk = "ks_" if is_s else "kp_"
                    cs = 8 if is_s else cs_p
                    nb = 128 // cs
                    gsm = gsmP[q]; gk = f"gsm{q}"
                    xa, xkey = load_x(l, i, src_h)
                    rmsnorm_tile(xa, xkey, 0)
                    yield
                    qv = qkvf[i % 2]; qk_ = f"qkvf{i % 2}"
                    for j in range(3):
                        bq = nbk()
                        for kc in range(8):
                            MM(ps[bq][:, :], nT[:, kc, :], WA[:, kc, j * 512:(j + 1) * 512], kc == 0, kc == 7, ["nT", "WA"], [psr(bq)])
                        CP("act" if j == 1 else "dve", qv[:, j * 512:(j + 1) * 512], ps[bq][:, :], [psr(bq)], [qk_])
                        yield
                    b_g = nbk()
                    for kc in range(8):
                        MM(ps[b_g][:, 0:8], nT[:, kc, :], WA[:, kc, 1536:1544], kc == 0, kc == 7, ["nT", "WA"], [psr(b_g)])
                    b_gg = nbk()
                    for kc in range(8):
                        MM(ps[b_gg][:, :], nT[:, kc, :], WA[:, kc, 1544:2056], kc == 0, kc == 7, ["nT", "WA"], [psr(b_gg)])
                    ACTF(tggP[q][:], ps[b_gg][:, :], AF.Silu, [psr(b_gg)], [f"tgg{q}"])
                    ACTF(gsm[:, 0:4], ps[b_g][:, 4:8], AF.Sigmoid, [psr(b_g)], [gk])
                    TT("dve", gsm[:, 4:8], ps[b_g][:, 0:4], alb[:, 4:8], ALU.add, [psr(b_g), "alb"], [gk])
                    yield
                    ACTF(gsm[:, 4:8], gsm[:, 4:8], AF.Exp, [gk], [gk])
                    ACTF(gsm[:, 4:8], gsm[:, 4:8], AF.Ln, [gk], [gk], bias=1.0)
                    TT("dve", gsm[:, 4:8], gsm[:, 4:8], alb[:, 0:4], ALU.mult, [gk, "alb"], [gk])
                    TT("pool", actv[:], qv[:], cwb[:, 3, :], ALU.mult, [qk_, "cwb"], ["actv"])
                    yield
                    for s in (1, 2, 3):
                        for j in range(3):
                            bs_ = nbk()
                            cs_ = slice(j * 512, (j + 1) * 512)
                            if is_s:
                                MM(ps[bs_][:, :], K_["shc"][:, s - 1, :], qv[:, cs_], True, False, [kk + "shc", qk_], [psr(bs_)])
                                MM(ps[bs_][:, :], shs[:, s - 1, :], cbuf[:, cs_], False, True, ["shs", "cbuf"], [psr(bs_)])
                            else:
                                first = (i == 0)
                                MM(ps[bs_][:, :], shc[:, s - 1, :], qv[:, cs_], True, first, ["shc", qk_], [psr(bs_)])
                                if not first:
                                    MM(ps[bs_][:, :], shp[:, s - 1, :], qkvf[(i - 1) % 2][:, cs_], False, True, ["shp", f"qkvf{(i - 1) % 2}"], [psr(bs_)])
                            TT("dve", stg[0][:, 0:512], ps[bs_][:, :], cwb[:, 3 - s, cs_], ALU.mult, [psr(bs_), "cwb"], ["stg0"])
                            TT("pool", actv[:, cs_], actv[:, cs_], stg[0][:, 0:512], ALU.add, ["actv", "stg0"], ["actv"])
                            yield
                    if is_s:
                        for b in range(16):
                            DMA("sp", o_gcs[l, b, :, :], qv[8 * b + 5:8 * b + 8, :], [qk_], [])
                    elif i == NT - 1:
                        DMA("sp", o_gcp[l, :, :], qv[125:128, :], [qk_], [])
                    ACTF(actv[:], actv[:], AF.Silu, ["actv"], ["actv"])
                    yield
                    TT("dve", stg[1][:, 0:1024], actv[:, 0:1024], actv[:, 0:1024], ALU.mult, ["actv"], ["stg1"])
                    REDUCE(gsm[:, 8:16], stg[1][:, 0:1024].rearrange("p (h d) -> p h d", h=8), ["stg1"], [gk])
                    ACTF(gsm[:, 8:16], gsm[:, 8:16], AF.Sqrt, [gk, "epsT"], [gk], bias=epsT[:, 0:1])
                    RECIP(gsm[:, 8:16], gsm[:, 8:16], [gk], [gk])
                    TS("dve", gsm[:, 8:12], gsm[:, 8:12], GD ** -0.5, None, ALU.mult, None, [gk], [gk])
                    yield
                    a3 = actv[:, 0:512].rearrange("p (h d) -> p h d", h=4)
                    k3 = actv[:, 512:1024].rearrange("p (h d) -> p h d", h=4)
                    v3 = actv[:, 1024:1536].rearrange("p (h d) -> p h d", h=4)
                    TT("dve", v4(tq), a3, bc(gsm[:, 8:12]), ALU.mult, ["actv", gk], ["tq"])
                    TT("dve", v4(tk), k3, bc(gsm[:, 12:16]), ALU.mult, ["actv", gk], ["tk"])
                    yield
                    b1 = nbk()
                    MM(ps[b1][:, 0:4], K_["ub"][:], gsm[:, 4:8], True, True, [kk + "ub", gk], [psr(b1)])
                    CP("dve", gsm[:, 16:20], ps[b1][:, 0:4], [psr(b1)], [gk])
                    MM(ps[b1][:, 8:12], K_["bl"][:], gsm[:, 16:20], True, True, [kk + "bl", gk], [psr(b1)])
                    TT("dve", gl[:, 0:nb, :], gsm[:, 16:20].unsqueeze(1).to_broadcast([128, nb, 4]),
                       K_["islast"][:, 0:nb].unsqueeze(2).to_broadcast([128, nb, 4]), ALU.mult, [gk, kk + "islast"], ["gl"])
                    MM(ps[b1][:, 16:16 + 4 * nb], ones[:], gl[:, 0:nb, :].rearrange("p b h -> p (b h)"), True, True, ["ones", "gl"], [psr(b1)])
                    ACTF(gsm[:, 24:28], gsm[:, 16:20], AF.Exp, [gk], [gk])
                    TT("dve", gsm[:, 28:32], ps[b1][:, 8:12], gsm[:, 16:20], ALU.subtract, [psr(b1), gk], [gk])
                    ACTF(GTP[q][:, 0:nb, :], ps[b1][:, 16:16 + 4 * nb].rearrange("p (b h) -> p b h", h=4), AF.Exp, [psr(b1)], [f"GT{q}"])
                    yield
                    ACTF(gsm[:, 28:32], gsm[:, 28:32], AF.Exp, [gk], [gk])
                    TS("dve", gsm[:, 32:36], gsm[:, 16:20], -1.0, None, ALU.mult, None, [gk], [gk])
                    TT("dve", v4(tkb), v4(tk), bc(gsm[:, 0:4]), ALU.mult, ["tk", gk], ["tkb"])
                    TT("pool", v4(trvP[q]), v3, bc(gsm[:, 0:4]), ALU.mult, ["actv", gk], [f"trv{q}"])
                    yield
                    TT("dve", v4(trkP[q]), v4(tkb), bc(gsm[:, 24:28]), ALU.mult, ["tkb", gk], [f"trk{q}"])
                    TT("pool", v4(tqd), v4(tq), bc(gsm[:, 24:28]), ALU.mult, ["tq", gk], ["tqd"])
                    TT("dve", v4(tkdP[q]), v4(tk), bc(gsm[:, 28:32]), ALU.mult, ["tk", gk], [f"tkd{q}"])
                    yield
                    for src_, sk_, dst_, dk_ in ((tk, "tk", fkTP[q], f"fkT{q}"), (tkb, "tkb", fkbTP[q], f"fkbT{q}"),
                                                 (tq, "tq", fqTP[q], f"fqT{q}"), (tqd, "tqd", fqdTP[q], f"fqdT{q}")):
                        bt = nbk()
                        for h in range(4):
                            TR(hs(ps[bt], h), hs(src_, h), ident[:], [sk_, "ident"], [psr(bt)])
                        CP("act", dst_[:], ps[bt][:, :], [psr(bt)], [dk_])
                        yield

                def genC(i, is_s):
                    q = i % 2
                    K_ = KS if is_s else KP
                    kk = "ks_" if is_s else "kp_"
                    cs = 8 if is_s else cs_p
                    nb = 128 // cs
                    L = int(math.log2(cs)) - 1
                    gsm = gsmP[q]; gk = f"gsm{q}"
                    fkT, fkbT, fqT, fqdT = fkTP[q], fkbTP[q], fqTP[q], fqdTP[q]
                    kfk, kfkb, kfq, kfqd = f"fkT{q}", f"fkbT{q}", f"fqT{q}", f"fqdT{q}"
                    trv, trk, tkd, tgg, GT = trvP[q], trkP[q], tkdP[q], tggP[q], GTP[q]
                    ktrv, ktrk, ktkd, ktgg, kGT = f"trv{q}", f"trk{q}", f"tkd{q}", f"tgg{q}", f"GT{q}"
                    for h in range(4):
                        TS("dve", hs(osb, h), ident[:], gsm[:, 16 + h:17 + h], None, ALU.mult, None, ["ident", gk], ["osb"])
                    bd, be = nbk(), nbk()
                    for h in range(4):
                        MM(hs(ps[bd], h), ones[:], hs(osb, h), True, False, ["ones", "osb"], [psr(bd)])
                        MM(hs(ps[bd], h), ident[:], K_["negmt"][:], False, True, ["ident", kk + "negmt"], [psr(bd)])
                        MM(hs(ps[be], h), ones[:], hs(osb, h), True, False, ["ones", "osb"], [psr(be)])
                        MM(hs(ps[be], h), ident[:], K_["posm"][:], False, True, ["ident", kk + "posm"], [psr(be)])
                    for h in range(4):
                        ACTF(hs(decT, h), hs(ps[bd], h), AF.Exp, [psr(bd), gk], ["decT"], bias=gsm[:, 32 + h:33 + h])
                        ACTF(hs(dec, h), hs(ps[be], h), AF.Exp, [psr(be), gk], ["dec"], bias=gsm[:, 16 + h:17 + h], scale=-1.0)
                    yield
                    ba, bb_, bc_ = nbk(), nbk(), nbk()
                    for h in range(4):
                        MM(hs(ps[ba], h), hs(fkT, h), hs(fkbT, h), True, True, [kfk, kfkb], [psr(ba)])
                        MM(hs(ps[bb_], h), hs(fkbT, h), hs(fkT, h), True, True, [kfk, kfkb], [psr(bb_)])
                        MM(hs(ps[bc_], h), hs(fkT, h), hs(fqT, h), True, True, [kfk, kfq], [psr(bc_)])
                    STT(Mk[0][:], ps[ba][:, :], -1.0, decT[:], ALU.mult, ALU.mult, [psr(ba), "decT"], ["Mk0"])
                    STT(MkT[0][:], ps[bb_][:, :], -1.0, dec[:], ALU.mult, ALU.mult, [psr(bb_), "dec"], ["MkT0"])
                    TT("dve", qkTm[:], ps[bc_][:, :], decT[:], ALU.mult, [psr(bc_), "decT"], ["qkTm"])
                    yield
                    su4 = K_["su"][:].unsqueeze(1).to_broadcast([128, 4, 128]); sl4 = K_["sl"][:].unsqueeze(1).to_broadcast([128, 4, 128])
                    id4 = ident[:].unsqueeze(1).to_broadcast([128, 4, 128])
                    TT("dve", v4(Mk[0]), v4(Mk[0]), su4, ALU.mult, ["Mk0", kk + "su"], ["Mk0"])
                    TT("pool", v4(MkT[0]), v4(MkT[0]), sl4, ALU.mult, ["MkT0", kk + "sl"], ["MkT0"])
                    TT("dve", v4(Pk[0]), v4(Mk[0]), id4, ALU.add, ["Mk0", "ident"], ["Pk0"])
                    yield
                    cur = 0
                    for lev in range(L):
                        nx = 1 - cur
                        last = (lev == L - 1)
                        bm, bmt = nbk(), nbk()
                        for h in range(4):
                            if not last:
                                MM(hs(ps[bm], h), hs(MkT[cur], h), hs(Mk[cur], h), True, True, [f"Mk{cur}", f"MkT{cur}"], [psr(bm)])
                            MM(hs(ps[bmt], h), hs(Mk[cur], h), hs(MkT[cur], h), True, True, [f"Mk{cur}", f"MkT{cur}"], [psr(bmt)])
                        if not last:
                            CP("act", Mk[nx][:], ps[bm][:, :], [psr(bm)], [f"Mk{nx}"])
                        CP("dve", MkT[nx][:], ps[bmt][:, :], [psr(bmt)], [f"MkT{nx}"])
                        yield
                        bp = nbk()
                        for h in range(4):
                            MM(hs(ps[bp], h), hs(MkT[nx], h), hs(Pk[cur], h), True, True, [f"MkT{nx}", f"Pk{cur}"], [psr(bp)])
                        TT("dve", Pk[nx][:], ps[bp][:, :], Pk[cur][:], ALU.add, [psr(bp), f"Pk{cur}"], [f"Pk{nx}"])
                        yield
                        cur = nx
                    TTm = Pk[cur]; ttk = f"Pk{cur}"
                    bu, bk2 = nbk(), nbk()
                    for h in range(4):
                        MM(hs(ps[bu], h), hs(TTm, h), hs(trv, h), True, True, [ttk, ktrv], [psr(bu)])
                        MM(hs(ps[bk2], h), hs(trk, h), hs(TTm, h), True, True, [ttk, ktrk], [psr(bk2)])
                    CP("dve", uacc[:], ps[bu][:, :], [psr(bu)], ["uacc"])
                    CP("act", kcT[:], ps[bk2][:, :], [psr(bk2)], ["kcT"])
                    yield
                    bo = 6
                    cm_ = K_["colmask"]; rm_ = K_["rowmask"]
                    ngrp = (nb + 3) // 4
                    for h in range(4):
                        for g in range(ngrp):
                            b0 = 4 * g; nbg = min(4, nb - b0)
                            TT("dve", kcTm[:, 0:nbg, :], hs(kcT, h).unsqueeze(1).to_broadcast([128, nbg, 128]), cm_[:, b0:b0 + nbg, :], ALU.mult,
                               ["kcT", kk + "colmask"], ["kcTm"])
                            TT("pool", qdTm[:, 0:nbg, :], hs(fqdT, h).unsqueeze(1).to_broadcast([128, nbg, 128]), cm_[:, b0:b0 + nbg, :], ALU.mult,
                               [kfqd, kk + "colmask"], ["qdTm"])
                            TT("pool", kdm[:, 0:nbg, :], hs(tkd, h).unsqueeze(1).to_broadcast([128, nbg, 128]),
                               rm_[:, b0:b0 + nbg].unsqueeze(2).to_broadcast([128, nbg, 128]), ALU.mult, [ktkd, kk + "rowmask"], ["kdm"])
                            yield
                            for jj in range(nbg):
                                b = b0 + jj
                                if is_s:
                                    skey = f"SBs{b % 2}"
                                    S_h = SBs[b % 2][:, :]
                                    DMA("sp", S_h, sgd[l, b, h, :, :], [], [skey])
                                else:
                                    skey = "SG"
                                    S_h = SG[:, h, :]
                                bx = nbk()
                                MM(ps[bx][:, 0:128], kcTm[:, jj, :], S_h, True, True, ["kcTm", skey], [psr(bx)])
                                TT("dve", hs(uacc, h), hs(uacc, h), ps[bx][:, 0:128], ALU.subtract, ["uacc", psr(bx)], ["uacc"])
                                MM(hs(ps[bo], h), qdTm[:, jj, :], S_h, b == 0, False, ["qdTm", skey], [psr(bo)])
                                MM(ps[bx][:, 128:256], kdm[:, jj, :], hs(uacc, h), True, True, ["kdm", "uacc"], [psr(bx)])
                                STT(S_h, S_h, GT[:, b, h:h + 1], ps[bx][:, 128:256], ALU.mult, ALU.add, [skey, kGT, psr(bx)], [skey])
                                if is_s:
                                    DMA("sp", o_gss[l, b, h, :, :], S_h, [skey], [])
                                yield
                        MM(hs(ps[bo], h), hs(qkTm, h), hs(uacc, h), False, True, ["qkTm", "uacc"], [psr(bo)])
                    if (not is_s) and i == NT - 1:
                        DMA("sp", o_gsp[l].rearrange("h d e -> d h e"), SG[:], ["SG"], [])
                    CP("dve", osb[:], ps[bo][:, :], [psr(bo)], ["osb"])
                    yield
                    TT("dve", stg[1][:, 0:512], osb[:], osb[:], ALU.mult, ["osb"], ["stg1"])
                    REDUCE(gsm[:, 40:44], stg[1][:, 0:512].rearrange("p (h d) -> p h d", h=4), ["stg1"], [gk])
                    ACTF(gsm[:, 40:44], gsm[:, 40:44], AF.Sqrt, [gk, "epsT"], [gk], bias=epsT[:, 0:1], scale=1.0 / GD)
                    RECIP(gsm[:, 40:44], gsm[:, 40:44], [gk], [gk])
                    yield
                    TT("dve", v4(osb), v4(osb), bc(gsm[:, 40:44]), ALU.mult, ["osb", gk], ["osb"])
                    TT("dve", v4(osb), v4(osb), gnb[:].unsqueeze(1).to_broadcast([128, 4, 128]), ALU.mult, ["osb", "gnb"], ["osb"])
                    TT("dve", gob[:], osb[:], tgg[:], ALU.mult, ["osb", ktgg], ["gob"])
                    yield
                    for pr in range(4):
                        TR(psb[:, pr * 128:(pr + 1) * 128], gob[:, pr * 128:(pr + 1) * 128], identb[:], ["gob", "identb"], [psr(7)])
                    CP("dve", GOt[:], psb[:, 0:512].rearrange("p (k t) -> p k t", k=4), [psr(7)], ["GOt"])
                    c0 = S if is_s else i * 128
                    DMA("sp", GOd[:, :, c0:c0 + 128], GOt[:], ["GOt"], ["GOd"])
                    yield

                drainF = genF(0, NT == 0, src_h)
                for _ in drainF:
                    pass
                for i in range(NT + 1):
                    gc_ = genC(i, i == NT)
                    gf_ = genF(i + 1, i + 1 == NT, src_h) if i + 1 <= NT else iter(())
                    alive_c = alive_f = True
                    while alive_c or alive_f:
                        if alive_c:
                            try:
                                next(gc_)
                            except StopIteration:
                                alive_c = False
                        if alive_f:
                            try:
                                next(gf_)
                            except StopIteration:
                                alive_f = False

        def sweep_mem(l, src_h, dst_h):
            with contextlib.ExitStack() as ses:
                sb2 = mk_sb(ses)
                P.barrier()
                WO = sb2("WO", [128, 8, D], BF16); WQ = sb2("WQ", [128, 8, MW], BF16); WM = sb2("WM", [128, 4, D], BF16)
                WKV = sb2("WKV", [128, 8, 2 * MW], BF16)
                load_w(WO, "WO", w_out[l], D); load_w(WQ, "WQ", w_mq[l], MW); load_w(WM, "WM", w_mo[l], D); load_w(WKV, "WKV", w_mkv[l], 2 * MW)
                mixT = sb2("mixT", [128, 8, 128], BF16)
                mkT = sb2("mkT", [128, 8, 128], BF16)
                mvA = sb2("mvA", [128, 2, 4, 129], BF16)
                qT = sb2("qT", [128, 4, 128], BF16)
                PTm = sb2("PTm", [128, 8, 128], BF16)
                PTpad = [sb2(f"PTpad{i}", [128, 8, 128], BF16) for i in range(2)]
                Kf2 = sb2("Kf2", [128, 2, MW]); Vf2 = sb2("Vf2", [128, 2, MW]); Kb2 = sb2("Kb2", [128, 2, MW], BF16)
                mkTb = sb2("mkTb", [128, 8, 128], BF16); mvAb = [sb2(f"mvAb{i}", [128, 2, 4, 129], BF16) for i in range(2)]
                osm = sb2("osm", [128, 512]); osmb = sb2("osmb", [128, 512], BF16); oT = sb2("oT", [128, 4, 128], BF16)
                rden = sb2("rdenm", [128, 4]); sc64 = sb2("sc64m", [128, 64])
                MSET("dve", mvA[:], 1.0, ["mvA"])
                for mt in range(2):
                    xb_ = xt[mt]; xkey = f"xt{mt}"
                    DMA("sp", xb_[:], memp[mt * 128:(mt + 1) * 128, :], [], [xkey])
                    rmsnorm_tile(xb_[:], xkey, 3)
                    for nbk_ in range(2):
                        for kc in range(8):
                            MM(ps[nbk_][:, :], nT[:, kc, :], WKV[:, kc, nbk_ * 512:(nbk_ + 1) * 512], kc == 0, kc == 7, ["nT", "WKV"], [psr(nbk_)])
                    st = stg[mt]; sk = f"stg{mt}"
                    CP("act", st[:, 0:512], ps[0][:, :], [psr(0)], [sk])
                    CP("dve", st[:, 512:1024], ps[1][:, :], [psr(1)], [sk])
                    DMA("sp", o_mkp[l, mt * 128:(mt + 1) * 128, :], st[:, 0:512], [sk], [])
                    DMA("sp", o_mvp[l, mt * 128:(mt + 1) * 128, :], st[:, 512:1024], [sk], [])
                    CP("pool", mvA[:, mt, :, 0:128], st[:, 512:1024].rearrange("p (h d) -> p h d", h=4), [sk], ["mvA"])
                    for h in range(4):
                        for kc in range(8):
                            MM(ps[2][:, h * 128:(h + 1) * 128], WKV[:, kc, h * 128:(h + 1) * 128], nT[:, kc, :], kc == 0, kc == 7, ["nT", "WKV"], [psr(2)])
                    CP("dve", mkT[:, mt * 4:(mt + 1) * 4, :], ps[2][:, :].rearrange("p (h t) -> p h t", h=4), [psr(2)], ["mkT"])
                for u in range(2):
                    MSET("pool", mvAb[u][:], 1.0, [f"mvAb{u}"])
                    MSET("pool", PTpad[u][:], 0.0, [f"PTpad{u}"])

                for i in range(NT + 1):
                    is_s = (i == NT)
                    xa, xkey = load_x(l, i, src_h)
                    c0 = S if is_s else i * 128
                    DMA("sp", mixT[:, 0:4, :], FOd[:, :, c0:c0 + 128], ["FOd"], ["mixT"])
                    DMA("sp", mixT[:, 4:8, :], GOd[:, :, c0:c0 + 128], ["GOd"], ["mixT"])
                    for nb_ in range(2):
                        for kc in range(8):
                            MM(ps[nb_][:, :], mixT[:, kc, :], WO[:, kc, nb_ * 512:(nb_ + 1) * 512], kc == 0, kc == 7, ["mixT", "WO"], [psr(nb_)])
                        TT("dve", xa[:, nb_ * 512:(nb_ + 1) * 512], xa[:, nb_ * 512:(nb_ + 1) * 512], ps[nb_][:, :], ALU.add, [xkey, psr(nb_)], [xkey])
                    rmsnorm_tile(xa, xkey, 1)
                    for h in range(4):
                        for kc in range(8):
                            MM(ps[2][:, h * 128:(h + 1) * 128], WQ[:, kc, h * 128:(h + 1) * 128], nT[:, kc, :], kc == 0, kc == 7, ["nT", "WQ"], [psr(2)])
                    P.emit("act", lambda e: e.mul(out=qT[:], in_=ps[2][:, :].rearrange("p (h t) -> p h t", h=4), mul=MD ** -0.5), reads=[psr(2)], writes=["qT"])
                    A_, B_ = 5, 6
                    if not is_s:
                        for mt in range(2):
                            for h in range(4):
                                MM(ps[3 + mt][:, h * 128:(h + 1) * 128], mkT[:, mt * 4 + h, :], qT[:, h, :], True, True, ["mkT", "qT"], [psr(3 + mt)])
                            ACTF(PTm[:, mt * 4:(mt + 1) * 4, :], ps[3 + mt][:, :].rearrange("p (h t) -> p h t", h=4), AF.Exp, [psr(3 + mt)], ["PTm"])
                        for h in range(4):
                            bk = A_ if h < 2 else B_
                            for mt in range(2):
                                MM(ps[bk][:, (h % 2) * 129:(h % 2 + 1) * 129], PTm[:, mt * 4 + h, :], mvA[:, mt, h, :], mt == 0, mt == 1, ["PTm", "mvA"], [psr(bk)])
                    else:
                        for b in range(16):
                            u = b % 2
                            DMA("sp", Kf2[:], cmk[l, b].rearrange("(t p) c -> p t c", p=128), [], ["Kf2"])
                            DMA("sp", Vf2[:], cmv[l, b].rearrange("(t p) c -> p t c", p=128), [], ["Vf2"])
                            CP("dve", Kb2[:], Kf2[:], ["Kf2"], ["Kb2"])
                            for mt in range(2):
                                for h in range(4):
                                    TR(psb[:, (mt * 4 + h) * 128:(mt * 4 + h + 1) * 128], Kb2[:, mt, h * 128:(h + 1) * 128], identb[:], ["Kb2", "identb"], [psr(7)])
                            CP("act", mkTb[:], psb[:, 0:1024].rearrange("p (k t) -> p k t", k=8), [psr(7)], ["mkTb"])
                            CP("pool", mvAb[u][:, :, :, 0:128], Vf2[:].rearrange("p t (h d) -> p t h d", h=4), ["Vf2"], [f"mvAb{u}"])
                            for mt in range(2):
                                for h in range(4):
                                    j = mt * 4 + h
                                    MM(ps[3][:, j * 8:(j + 1) * 8], mkTb[:, j, :], qT[:, h, 8 * b:8 * b + 8], True, True, ["mkTb", "qT"], [psr(3)])
                            ACTF(PTpad[u][:, :, 8 * b:8 * b + 8], ps[3][:, 0:64].rearrange("p (j q) -> p j q", j=8), AF.Exp, [psr(3)], [f"PTpad{u}"])
                            for h in range(4):
                                bk = A_ if h < 2 else B_
                                for mt in range(2):
                                    MM(ps[bk][:, (h % 2) * 129:(h % 2 + 1) * 129], PTpad[u][:, mt * 4 + h, :], mvAb[u][:, mt, h, :],
                                       b == 0 and mt == 0 and h % 2 == 0, b == 15 and mt == 1, [f"PTpad{u}", f"mvAb{u}"], [psr(bk)])
                            MSET("pool", PTpad[u][:, :, 8 * b:8 * b + 8], 0.0, [f"PTpad{u}"])
                    for half, bk in ((0, A_), (1, B_)):
                        v = ps[bk][:, 0:258].rearrange("p (h e) -> p h e", h=2)
                        CP("dve", rden[:, half * 2:(half + 1) * 2], v[:, :, 128], [psr(bk)], ["rden"])
                    RECIP(rden[:], rden[:], ["rden"], ["rden"])
                    for half, bk in ((0, A_), (1, B_)):
                        v = ps[bk][:, 0:258].rearrange("p (h e) -> p h e", h=2)
                        TT("dve", osm[:, half * 256:(half + 1) * 256].rearrange("p (h d) -> p h d", h=2), v[:, :, 0:128],
                           rden[:, half * 2:(half + 1) * 2].unsqueeze(2).to_broadcast([128, 2, 128]), ALU.mult, [psr(bk), "rden"], ["osm"])
                    CP("dve", osmb[:], osm[:], ["osm"], ["osmb"])
                    for pr in range(4):
                        TR(psb[:, pr * 128:(pr + 1) * 128], osmb[:, pr * 128:(pr + 1) * 128], identb[:], ["osmb", "identb"], [psr(7)])
                    CP("dve", oT[:], psb[:, 0:512].rearrange("p (k t) -> p k t", k=4), [psr(7)], ["oT"])
                    for nb_ in range(2):
                        for kc in range(4):
                            MM(ps[nb_][:, :], oT[:, kc, :], WM[:, kc, nb_ * 512:(nb_ + 1) * 512], kc == 0, kc == 3, ["oT", "WM"], [psr(nb_)])
                        TT("dve", xa[:, nb_ * 512:(nb_ + 1) * 512], xa[:, nb_ * 512:(nb_ + 1) * 512], ps[nb_][:, :], ALU.add, [xkey, psr(nb_)], [xkey])
                    if not is_s:
                        DMA("sp", dst_h[i * 128:(i + 1) * 128, :], xa, [xkey], [])

        def sweep_ffn(l, src_h, dst_h, final):
            with contextlib.ExitStack() as ses:
                sb2 = mk_sb(ses)
                P.barrier()
                WI = sb2("WI", [128, 8, 2 * DFF], BF16); WF = sb2("WF", [128, 22, D], BF16)
                load_w(WI, "WI", w_fi[l], 2 * DFF); load_w(WF, "WF", w_fo[l], D)
                hT = sb2("hT", [128, 22, 128], BF16); sil = sb2("sil", [128, 128])
                if final:
                    DMA("sp", gbc[:, 0, :], g_fin.partition_broadcast(128), [], ["gbc0"])
                for i in range(NT + 1):
                    is_s = (i == NT)
                    xa, xkey = load_x(l, i, src_h)
                    rmsnorm_tile(xa, xkey, 2)
                    for c in range(22):
                        ba, bu = (c % 2) * 2, (c % 2) * 2 + 1
                        for kc in range(8):
                            MM(ps[ba][:, 0:128], WI[:, kc, c * 128:(c + 1) * 128], nT[:, kc, :], kc == 0, kc == 7, ["nT", "WI"], [psr(ba)])
                        for kc in range(8):
                            MM(ps[bu][:, 0:128], WI[:, kc, DFF + c * 128:DFF + (c + 1) * 128], nT[:, kc, :], kc == 0, kc == 7, ["nT", "WI"], [psr(bu)])
                        ACTF(sil[:], ps[ba][:, 0:128], AF.Silu, [psr(ba)], ["sil"])
                        TT("dve", hT[:, c, :], sil[:], ps[bu][:, 0:128], ALU.mult, ["sil", psr(bu)], ["hT"])
                    for nb_ in range(2):
                        bk = 4 + nb_
                        for c in range(22):
                            MM(ps[bk][:, :], hT[:, c, :], WF[:, c, nb_ * 512:(nb_ + 1) * 512], c == 0, c == 21, ["hT", "WF"], [psr(bk)])
                        TT("dve", xa[:, nb_ * 512:(nb_ + 1) * 512], xa[:, nb_ * 512:(nb_ + 1) * 512], ps[bk][:, :], ALU.add, [xkey, psr(bk)], [xkey])
                    if final:
                        ACTF(stg[0][:, 0:D], xa, AF.Square, [xkey], ["stg0", "rs"], accum_out=rs[:, 0:1])
                        ACTF(rs[:, 1:2], rs[:, 0:1], AF.Sqrt, ["rs", "epsT"], ["rs"], bias=epsT[:, 0:1], scale=1.0 / D)
                        RECIP(rs[:, 2:3], rs[:, 1:2], ["rs"], ["rs"])
                        STT(stg[1][:, 0:D], xa, rs[:, 2:3], gbc[:, 0, :], ALU.mult, ALU.mult, [xkey, "rs", "gbc0"], ["stg1"])
                        if is_s:
                            DMA("sp", ys[:, :], stg[1][:, 0:D], ["stg1"], [])
                        else:
                            DMA("sp", yp[i * 128:(i + 1) * 128, :], stg[1][:, 0:D], ["stg1"], [])
                    elif not is_s:
                        DMA("sp", dst_h[i * 128:(i + 1) * 128, :], xa, [xkey], [])

        for l in range(nlayers):
            src_h = xp if l == 0 else h_b
            load_gains(l)
            sweep_fox(l, src_h)
            if DEBUG and l == 0:
                DMA("sp", dbg_fos[:, :, :], FOd[:, :, S:S + 128], ["FOd"], [])
            if stop_after == "fox":
                break
            sweep_gdn(l, src_h)
            if DEBUG and l == 0:
                DMA("sp", dbg_gos[:, :, :], GOd[:, :, S:S + 128], ["GOd"], [])
            if stop_after == "gdn":
                break
            sweep_mem(l, src_h, h_a)
            if DEBUG and l == 0:
                DMA("sp", dbg_xs1[:, :], XS[:], ["XS"], [])
            if stop_after == "mem":
                break
            sweep_ffn(l, h_a, h_b, final=(l == DEPTH - 1))
        P.finish()
    return nc

def _host_consts(cs_p):
    c = {}
    c["k_ident"] = np.eye(128, dtype=np.float32)
    for k, v in _consts(cs_p).items():
        c["kp_" + k] = v
    for k, v in _consts(8).items():
        c["ks_" + k] = v
    key = np.arange(128)[:, None, None]; r = np.arange(4)[None, :, None]; q = np.arange(512)[None, None, :]
    c["k_negq"] = np.where(q >= 128 * r + key, 0.0, NEG).astype(np.float32)
    c["k_iota"] = np.arange(128, dtype=np.float32).reshape(128, 1)
    t = np.arange(128)
    c["k_ub128"] = (t[:, None] <= t[None, :]).astype(np.float32)
    c["k_ls128"] = (t[:, None] > t[None, :]).astype(np.float32)
    shc = np.zeros((128, 3, 128), np.float32); shp = np.zeros((128, 3, 128), np.float32); shs = np.zeros((48, 3, 128), np.float32)
    for s in (1, 2, 3):
        for tt in range(128):
            if tt - s >= 0:
                shc[tt - s, s - 1, tt] = 1.0
            else:
                shp[128 + tt - s, s - 1, tt] = 1.0
        for b in range(16):
            for tl in range(8):
                if tl - s < 0:
                    shs[3 * b + 3 + tl - s, s - 1, 8 * b + tl] = 1.0
    c["k_shc"] = shc; c["k_shp"] = shp; c["k_shs"] = shs
    return c


_IN_ORDER = ("x_prompt", "x_sample", "cache_fox_k", "cache_fox_v", "cache_fox_logf", "state_gdn", "state_gdn_conv",
             "cache_mem_k", "cache_mem_v", "page_table", "mem_prompt")


def kernel(**inp):
    f = lambda a: np.ascontiguousarray(np.asarray(a))
    x_prompt = f(inp["x_prompt"]); x_sample = f(inp["x_sample"])
    B, S, _ = x_prompt.shape
    BS, LS, _ = x_sample.shape
    assert B == 4 and BS == 128 and LS == 8
    page_table = f(inp["page_table"]).astype(np.int32)
    NPG = page_table.shape[1]
    ckf = f(inp["cache_fox_k"]); NPHYS = ckf.shape[1]
    ckv = np.concatenate([ckf.reshape(DEPTH * NPHYS * 128, FW), f(inp["cache_fox_v"]).reshape(DEPTH * NPHYS * 128, FW),
                          f(inp["cache_fox_logf"]).reshape(DEPTH * NPHYS * 128, FH)], axis=1)
    sgd = f(inp["state_gdn"]); scv = f(inp["state_gdn_conv"])
    cmk = f(inp["cache_mem_k"]).reshape(DEPTH, BS, NMEM, MW); cmv = f(inp["cache_mem_v"]).reshape(DEPTH, BS, NMEM, MW)
    memp = f(inp["mem_prompt"])
    cs_p = math.gcd(S, 64)
    nc = build_program(S, NPG, NPHYS, cs_p, nlayers=NLAYERS, stop_after=STOP_AFTER)
    consts = _host_consts(cs_p)
    shared = {
        "ckv": ckv,
        "g_mix": f(inp["g_norm_mix"]), "w_in": f(inp["w_in"]), "b_f": f(inp["b_fox_f"]), "conv_w": f(inp["gdn_conv_w"]),
        "a_log": f(inp["gdn_a_log"]), "dt_b": f(inp["gdn_dt_bias"]), "gnw": f(inp["gdn_norm_w"]), "w_out": f(inp["w_out"]),
        "g_memin": f(inp["g_norm_memin"]), "w_mkv": f(inp["w_mem_kv"]), "g_mem": f(inp["g_norm_mem"]),
        "w_mq": f(inp["w_mem_q"]), "w_mo": f(inp["w_mem_o"]), "g_ffn": f(inp["g_norm_ffn"]), "w_fi": f(inp["w_ffn_in"]),
        "w_fo": f(inp["w_ffn_out"]), "g_fin": f(inp["g_final"]),
    }
    shared.update(consts)
    in_maps = []
    for c in range(8):
        b = c // 2
        m = dict(shared)
        m.update({
            "xp": x_prompt[b], "xs": x_sample[16 * c:16 * c + 16].reshape(128, D),
            "sgd": sgd[:, 16 * c:16 * c + 16], "scv": scv[:, 16 * c:16 * c + 16],
            "cmk": cmk[:, 16 * c:16 * c + 16], "cmv": cmv[:, 16 * c:16 * c + 16],
            "ptab": page_table[16 * c:16 * c + 16], "memp": memp[b],
        })
        in_maps.append({k: np.ascontiguousarray(v) for k, v in m.items()})
    res = run_bass_kernel_spmd(nc, in_maps, core_ids=list(range(8)))
    R = res.results
    ev = [R[2 * b] for b in range(4)]
    cat_b = lambda key: np.stack([r[key] for r in ev], axis=0)
    cat_s = lambda key, ax: np.concatenate([r[key] for r in R], axis=ax)
    yp = cat_b("yp")
    ys = cat_s("ys", 0).reshape(BS, LS, D)
    def pl(key, tail):
        return np.stack([r[key] for r in ev], axis=1).reshape((DEPTH, 4) + tail)
    def sl(key, tail):
        return np.concatenate([r[key].reshape((DEPTH, 16) + tail) for r in R], axis=1)
    outs = (yp, ys,
            pl("o_fkp", (S, FH, FD)), pl("o_fvp", (S, FH, FD)), pl("o_flp", (S, FH)),
            pl("o_gsp", (GH, GD, GD)), pl("o_gcp", (3, GC3)), pl("o_mkp", (NMEM, MH, MD)), pl("o_mvp", (NMEM, MH, MD)),
            sl("o_fks", (LS, FH, FD)), sl("o_fvs", (LS, FH, FD)), sl("o_fls", (LS, FH)),
            sl("o_gss", (GH, GD, GD)), sl("o_gcs", (3, GC3)))
    global _LAST
    _LAST = R
    return tuple(np.ascontiguousarray(o.astype(np.float32)) for o in outs)
```
